# Optimizing a Trainium2 kernel written in Bass

```python
import jax
import jax.numpy as jnp
from jax import lax
import numpy as np

D_MODEL = 2048
BATCH = 8
SEQ = 2048
DEPTH = 1

D_RNN = D_MODEL
LRU_BLOCKS = 16
LRU_BLOCK_DIM = D_RNN // LRU_BLOCKS
CONV_WIDTH = 4
LRU_C = 8.0
N_HEADS = 16
HEAD_DIM = 128
N_KV_GROUPS = 4
HEADS_PER_GROUP = N_HEADS // N_KV_GROUPS
Q_WIDTH = N_HEADS * HEAD_DIM
KV_WIDTH = N_KV_GROUPS * HEAD_DIM
CMP_STRIDE = 16
CMP_BLOCK = 2 * CMP_STRIDE
SLC_BLOCK = 64
N_SELECT = 16
WINDOW = 512
WIN_Q_BLOCK = 128
SLC_Q_BLOCK = 32
ROPE_THETA = 500000.0
ROPE_DIM = HEAD_DIM // 4
D_FF = 5632
NORM_EPS = 1e-6
N_ADA = 9
SPLIT_SIZES = (D_RNN, D_RNN, Q_WIDTH, 6 * KV_WIDTH, 3 * N_HEADS, 2 * D_MODEL)
IN_WIDTH = 2 * D_RNN + Q_WIDTH + 6 * KV_WIDTH + 3 * N_HEADS + 2 * D_MODEL
NEG = -1e30

kernel_name = "hybrid_rglru_nsa_macaron_block"


def rms_norm(x, g):
    x32 = x.astype(jnp.float32)
    y = x32 * lax.rsqrt(jnp.mean(x32 * x32, axis=-1, keepdims=True) + NORM_EPS)
    return (y * g.astype(jnp.float32)).astype(x.dtype)


def modulate(h, shift, scale):
    return h * (1 + scale[:, None, :]) + shift[:, None, :]


def swiglu(u, w_gate, w_up, w_down):
    return (jax.nn.silu(u @ w_gate) * (u @ w_up)) @ w_down


def masked_softmax(s, mask):
    p = jax.nn.softmax(jnp.where(mask, s, NEG), axis=-1)
    return jnp.where(mask, p, 0.0)


def partial_rope(t, cos, sin):
    half = ROPE_DIM // 2
    c = cos[None, :, None, :].astype(t.dtype)
    s = sin[None, :, None, :].astype(t.dtype)
    t1 = t[..., :half]
    t2 = t[..., half:ROPE_DIM]
    return jnp.concatenate([t1 * c - t2 * s, t2 * c + t1 * s, t[..., ROPE_DIM:]], axis=-1)


def rglru_branch(xa, ya, conv_w, conv_b, wr, br, wi, bi, lam):
    B, S, _ = xa.shape
    xc = lax.conv_general_dilated(
        xa, conv_w[:, None, :], window_strides=(1,), padding=[(CONV_WIDTH - 1, 0)],
        dimension_numbers=('NWC', 'WIO', 'NWC'), feature_group_count=D_RNN) + conv_b
    xb = xc.reshape(B, S, LRU_BLOCKS, LRU_BLOCK_DIM)
    r = jax.nn.sigmoid(jnp.einsum('bshi,hij->bshj', xb, wr).reshape(B, S, D_RNN) + br)
    i = jax.nn.sigmoid(jnp.einsum('bshi,hij->bshj', xb, wi).reshape(B, S, D_RNN) + bi)
    log_a = (-LRU_C * r.astype(jnp.float32)) * jax.nn.softplus(-lam.astype(jnp.float32))
    a = jnp.exp(log_a)
    b = jnp.sqrt(-jnp.expm1(2.0 * log_a)) * (i * xc).astype(jnp.float32)

    def combine(left, right):
        a1, b1 = left
        a2, b2 = right
        return a1 * a2, a2 * b1 + b2

    _, h = lax.associative_scan(combine, (a, b), axis=1)
    return h.astype(xa.dtype) * jax.nn.gelu(ya)


def compress(t, pe, w1, w2):
    B, G, S, dh = t.shape
    ch = t.reshape(B, G, S // CMP_STRIDE, CMP_STRIDE, dh)
    blocks = jnp.concatenate([ch[:, :, :-1], ch[:, :, 1:]], axis=3) + pe
    flat = blocks.reshape(B, G, blocks.shape[2], CMP_BLOCK * dh)
    return jax.nn.gelu(flat @ w1) @ w2


def nsa_branch(q, kv, gate_logits, cos, sin, pe_k, w1_k, w2_k, pe_v, w1_v, w2_v):
    B, S, _ = q.shape
    G, HPG, DH = N_KV_GROUPS, HEADS_PER_GROUP, HEAD_DIM
    scale = HEAD_DIM ** -0.5
    pos = jnp.arange(S)

    q = partial_rope(q.reshape(B, S, N_HEADS, DH), cos, sin)
    qg = q.reshape(B, S, G, HPG, DH).transpose(0, 2, 3, 1, 4)
    k_cmp, v_cmp, k_slc, v_slc, k_win, v_win = jnp.split(kv, 6, axis=-1)

    def kv_heads(t, rotate):
        t = t.reshape(B, S, G, DH)
        if rotate:
            t = partial_rope(t, cos, sin)
        return t.transpose(0, 2, 1, 3)

    k_cmp, k_slc, k_win = kv_heads(k_cmp, True), kv_heads(k_slc, True), kv_heads(k_win, True)
    v_cmp, v_slc, v_win = kv_heads(v_cmp, False), kv_heads(v_slc, False), kv_heads(v_win, False)

    kc = compress(k_cmp, pe_k, w1_k, w2_k)
    vc = compress(v_cmp, pe_v, w1_v, w2_v)
    n_cmp = kc.shape[2]
    cmp_start = jnp.arange(n_cmp) * CMP_STRIDE
    cmp_mask = (cmp_start + CMP_BLOCK - 1)[None, :] <= pos[:, None]
    s_cmp = jnp.einsum('bghsd,bgnd->bghsn', qg, kc, preferred_element_type=jnp.float32) * scale
    p_cmp = masked_softmax(s_cmp, cmp_mask)
    o_cmp = jnp.einsum('bghsn,bgnd->bghsd', p_cmp.astype(vc.dtype), vc)

    n_slc = S // SLC_BLOCK
    slc_start = jnp.arange(n_slc) * SLC_BLOCK
    overlap = ((cmp_start[:, None] < slc_start[None, :] + SLC_BLOCK)
               & (cmp_start[:, None] + CMP_BLOCK > slc_start[None, :])).astype(jnp.float32)
    imp = jnp.einsum('bghsn,nj->bgsj', p_cmp, overlap)
    cur = pos // SLC_BLOCK
    blk = jnp.arange(n_slc)
    forced = (blk[None, :] == 0) | (blk[None, :] == cur[:, None]) | (blk[None, :] == cur[:, None] - 1)
    causal_blk = slc_start[None, :] <= pos[:, None]
    imp = jnp.where(forced, jnp.inf, jnp.where(causal_blk, imp, -jnp.inf))
    n_sel = min(N_SELECT, n_slc)
    _, sel_idx = lax.top_k(imp, n_sel)

    kb = k_slc.reshape(B, G, n_slc, SLC_BLOCK, DH)
    vb = v_slc.reshape(B, G, n_slc, SLC_BLOCK, DH)
    nqc = S // SLC_Q_BLOCK
    q_chunks = qg.reshape(B, G, HPG, nqc, SLC_Q_BLOCK, DH).transpose(3, 0, 1, 2, 4, 5)
    idx_chunks = sel_idx.reshape(B, G, nqc, SLC_Q_BLOCK, n_sel).transpose(2, 0, 1, 3, 4)
    pos_chunks = pos.reshape(nqc, SLC_Q_BLOCK)
    gather = jax.vmap(jax.vmap(lambda blocks, ids: blocks[ids]))

    def slc_chunk(args):
        qc, ic, pc = args
        kg = gather(kb, ic)
        vg = gather(vb, ic)
        s = jnp.einsum('bghqd,bgqnld->bghqnl', qc, kg, preferred_element_type=jnp.float32) * scale
        kpos = ic[..., None] * SLC_BLOCK + jnp.arange(SLC_BLOCK)
        mask = (kpos <= pc[None, None, :, None, None]).reshape(B, G, 1, SLC_Q_BLOCK, n_sel * SLC_BLOCK)
        p = masked_softmax(s.reshape(B, G, HPG, SLC_Q_BLOCK, n_sel * SLC_BLOCK), mask)
        return jnp.einsum('bghqnl,bgqnld->bghqd', p.reshape(s.shape).astype(vg.dtype), vg)

    o_slc = lax.map(slc_chunk, (q_chunks, idx_chunks, pos_chunks))
    o_slc = o_slc.transpose(1, 2, 3, 0, 4, 5).reshape(B, G, HPG, S, DH)

    nqb = S // WIN_Q_BLOCK
    span = WINDOW + WIN_Q_BLOCK
    kp = jnp.pad(k_win, ((0, 0), (0, 0), (WINDOW, 0), (0, 0)))
    vp = jnp.pad(v_win, ((0, 0), (0, 0), (WINDOW, 0), (0, 0)))
    qb = qg.reshape(B, G, HPG, nqb, WIN_Q_BLOCK, DH).transpose(3, 0, 1, 2, 4, 5)

    def win_chunk(args):
        qc, b = args
        start = b * WIN_Q_BLOCK
        kw = lax.dynamic_slice_in_dim(kp, start, span, axis=2)
        vw = lax.dynamic_slice_in_dim(vp, start, span, axis=2)
        qpos = start + jnp.arange(WIN_Q_BLOCK)
        kpos = start - WINDOW + jnp.arange(span)
        mask = ((kpos[None, :] <= qpos[:, None]) & (kpos[None, :] > qpos[:, None] - WINDOW)
                & (kpos[None, :] >= 0))
        s = jnp.einsum('bghqd,bgkd->bghqk', qc, kw, preferred_element_type=jnp.float32) * scale
        p = masked_softmax(s, mask)
        return jnp.einsum('bghqk,bgkd->bghqd', p.astype(vw.dtype), vw)

    o_win = lax.map(win_chunk, (qb, jnp.arange(nqb)))
    o_win = o_win.transpose(1, 2, 3, 0, 4, 5).reshape(B, G, HPG, S, DH)

    g = jax.nn.sigmoid(gate_logits).reshape(B, S, G, HPG, 3).transpose(0, 2, 3, 1, 4)
    o = g[..., 0:1] * o_cmp + g[..., 1:2] * o_slc + g[..., 2:3] * o_win
    return o.transpose(0, 3, 1, 2, 4).reshape(B, S, Q_WIDTH)


def setup_inputs(seed: int = 0) -> dict:
    key = jax.random.key(seed)
    ks = jax.random.split(key, 33)
    f32 = jnp.float32

    def nrm(k, shape, scale):
        return jax.random.normal(k, shape, f32) * scale

    def gain(k):
        return 1.0 + 0.05 * jax.random.normal(k, (DEPTH, D_MODEL), f32)

    u = jax.random.uniform(ks[18], (DEPTH, D_RNN), f32, minval=0.9, maxval=0.999)
    p = u ** (1.0 / LRU_C)
    lam = jnp.log(p) - jnp.log1p(-p)
    return {
        "x": nrm(ks[0], (BATCH, SEQ, D_MODEL), 1.0),
        "c": nrm(ks[1], (BATCH, D_MODEL), 1.0),
        "w_ada": nrm(ks[2], (DEPTH, D_MODEL, N_ADA * D_MODEL), D_MODEL ** -0.5),
        "b_ada": nrm(ks[3], (DEPTH, N_ADA * D_MODEL), 0.02),
        "ffn1_pre_g": gain(ks[4]),
        "ffn1_post_g": gain(ks[5]),
        "ffn1_w_gate": nrm(ks[6], (DEPTH, D_MODEL, D_FF), D_MODEL ** -0.5),
        "ffn1_w_up": nrm(ks[7], (DEPTH, D_MODEL, D_FF), D_MODEL ** -0.5),
        "ffn1_w_down": nrm(ks[8], (DEPTH, D_FF, D_MODEL), D_FF ** -0.5),
        "mix_pre_g": gain(ks[9]),
        "mix_post_g": gain(ks[10]),
        "w_in": nrm(ks[11], (DEPTH, D_MODEL, IN_WIDTH), D_MODEL ** -0.5),
        "conv_w": nrm(ks[12], (DEPTH, CONV_WIDTH, D_RNN), CONV_WIDTH ** -0.5),
        "conv_b": nrm(ks[13], (DEPTH, D_RNN), 0.02),
        "lru_wr": nrm(ks[14], (DEPTH, LRU_BLOCKS, LRU_BLOCK_DIM, LRU_BLOCK_DIM), LRU_BLOCK_DIM ** -0.5),
        "lru_br": nrm(ks[15], (DEPTH, D_RNN), 0.02),
        "lru_wi": nrm(ks[16], (DEPTH, LRU_BLOCKS, LRU_BLOCK_DIM, LRU_BLOCK_DIM), LRU_BLOCK_DIM ** -0.5),
        "lru_bi": nrm(ks[17], (DEPTH, D_RNN), 0.02),
        "lru_lambda": lam,
        "cmp_pe_k": nrm(ks[19], (DEPTH, CMP_BLOCK, HEAD_DIM), 0.02),
        "cmp_w1_k": nrm(ks[20], (DEPTH, CMP_BLOCK * HEAD_DIM, HEAD_DIM), (CMP_BLOCK * HEAD_DIM) ** -0.5),
        "cmp_w2_k": nrm(ks[21], (DEPTH, HEAD_DIM, HEAD_DIM), HEAD_DIM ** -0.5),
        "cmp_pe_v": nrm(ks[22], (DEPTH, CMP_BLOCK, HEAD_DIM), 0.02),
        "cmp_w1_v": nrm(ks[23], (DEPTH, CMP_BLOCK * HEAD_DIM, HEAD_DIM), (CMP_BLOCK * HEAD_DIM) ** -0.5),
        "cmp_w2_v": nrm(ks[24], (DEPTH, HEAD_DIM, HEAD_DIM), HEAD_DIM ** -0.5),
        "w_a_out": nrm(ks[25], (DEPTH, D_RNN, D_MODEL), D_RNN ** -0.5),
        "w_b_out": nrm(ks[26], (DEPTH, Q_WIDTH, D_MODEL), Q_WIDTH ** -0.5),
        "w_out": nrm(ks[27], (DEPTH, D_MODEL, D_MODEL), D_MODEL ** -0.5),
        "ffn2_pre_g": gain(ks[28]),
        "ffn2_post_g": gain(ks[29]),
        "ffn2_w_gate": nrm(ks[30], (DEPTH, D_MODEL, D_FF), D_MODEL ** -0.5),
        "ffn2_w_up": nrm(ks[31], (DEPTH, D_MODEL, D_FF), D_MODEL ** -0.5),
        "ffn2_w_down": nrm(ks[32], (DEPTH, D_FF, D_MODEL), D_FF ** -0.5),
    }


def reference(x, c, w_ada, b_ada,
              ffn1_pre_g, ffn1_post_g, ffn1_w_gate, ffn1_w_up, ffn1_w_down,
              mix_pre_g, mix_post_g, w_in, conv_w, conv_b,
              lru_wr, lru_br, lru_wi, lru_bi, lru_lambda,
              cmp_pe_k, cmp_w1_k, cmp_w2_k, cmp_pe_v, cmp_w1_v, cmp_w2_v,
              w_a_out, w_b_out, w_out,
              ffn2_pre_g, ffn2_post_g, ffn2_w_gate, ffn2_w_up, ffn2_w_down):
    B, S, D = x.shape
    pos = jnp.arange(S).astype(jnp.float32)
    inv_freq = ROPE_THETA ** (-jnp.arange(0, ROPE_DIM, 2, dtype=jnp.float32) / ROPE_DIM)
    ang = pos[:, None] * inv_freq[None, :]
    cos, sin = jnp.cos(ang), jnp.sin(ang)
    offsets = np.cumsum(SPLIT_SIZES)[:-1].tolist()
    c_act = jax.nn.silu(c)

    for l in range(DEPTH):
        mod = (c_act @ w_ada[l] + b_ada[l]).reshape(B, N_ADA, D)
        sh1, sc1, g1 = mod[:, 0], mod[:, 1], mod[:, 2]
        sh2, sc2, g2 = mod[:, 3], mod[:, 4], mod[:, 5]
        sh3, sc3, g3 = mod[:, 6], mod[:, 7], mod[:, 8]

        u = modulate(rms_norm(x, ffn1_pre_g[l]), sh1, sc1)
        f = swiglu(u, ffn1_w_gate[l], ffn1_w_up[l], ffn1_w_down[l])
        x = x + 0.5 * g1[:, None, :] * rms_norm(f, ffn1_post_g[l])

        u = modulate(rms_norm(x, mix_pre_g[l]), sh2, sc2)
        xa, ya, q, kv, nsa_g, merge_g = jnp.split(u @ w_in[l], offsets, axis=-1)
        y_a = rglru_branch(xa, ya, conv_w[l], conv_b[l], lru_wr[l], lru_br[l],
                           lru_wi[l], lru_bi[l], lru_lambda[l]) @ w_a_out[l]
        y_b = nsa_branch(q, kv, nsa_g, cos, sin, cmp_pe_k[l], cmp_w1_k[l], cmp_w2_k[l],
                         cmp_pe_v[l], cmp_w1_v[l], cmp_w2_v[l]) @ w_b_out[l]
        gate_a, gate_b = jnp.split(jax.nn.sigmoid(merge_g), 2, axis=-1)
        mixed = (gate_a * y_a + gate_b * y_b) @ w_out[l]
        x = x + g2[:, None, :] * rms_norm(mixed, mix_post_g[l])

        u = modulate(rms_norm(x, ffn2_pre_g[l]), sh3, sc3)
        f = swiglu(u, ffn2_w_gate[l], ffn2_w_up[l], ffn2_w_down[l])
        x = x + 0.5 * g3[:, None, :] * rms_norm(f, ffn2_post_g[l])
    return x
```

```python
import numpy as np
from contextlib import ExitStack
import concourse.bass as bass
import concourse.mybir as mybir
from concourse.bass_utils import run_bass_kernel_spmd

F32 = mybir.dt.float32
BF16 = mybir.dt.bfloat16
AF = mybir.ActivationFunctionType
ALU = mybir.AluOpType

D = 2048
S = 2048
DFF = 5632
NCH = 16
INW = 13360
EPS = 1e-6
NEGB = -30000.0
SCALE = 128 ** -0.5
ENGS = ["pe", "act", "dve", "pool", "sp"]


class Op:
    __slots__ = ("i", "eng", "fn", "deps", "dma", "chan", "sig", "sval", "dval", "n")

    def __init__(self, i, eng, fn, deps, dma, chan):
        self.i, self.eng, self.fn, self.deps, self.dma, self.chan = i, eng, fn, deps, dma, chan
        self.sig = False
        self.sval = 0
        self.dval = 0
        self.n = 1


class Prog:
    def __init__(self, nc):
        self.nc = nc
        self.ops = []
        self.lastw = {}
        self.rd_eng = {}
        self.rd_dma = {}
        self.last_on = {}
        self.pending = {e: set() for e in ENGS}
        self.dma_open = []

    def _add(self, eng, fn, r, w, dma=False, chan=None, n=1):
        i = len(self.ops)
        deps = self.pending[eng]
        self.pending[eng] = set()
        for x in r:
            if x in self.lastw:
                deps.add(self.lastw[x])
        for x in w:
            if x in self.lastw:
                deps.add(self.lastw[x])
            deps.update(self.rd_eng.get(x, {}).values())
            deps.update(self.rd_dma.get(x, ()))
        op = Op(i, eng, fn, deps, dma, chan)
        op.n = n
        self.ops.append(op)
        for x in r:
            if dma:
                self.rd_dma.setdefault(x, set()).add(i)
            else:
                self.rd_eng.setdefault(x, {})[eng] = i
        for x in w:
            self.lastw[x] = i
            self.rd_eng[x] = {}
            self.rd_dma[x] = set()
        self.last_on[eng] = i
        if dma:
            self.dma_open.append(i)
        return op

    def op(self, eng, fn, r=(), w=()):
        return self._add(eng, fn, r, w)

    def dma(self, queue, out, in_, r=(), w=(), chan=None):
        outs = out if isinstance(out, (list, tuple)) else [out]
        ins = in_ if isinstance(in_, (list, tuple)) else [in_]
        outs = [o.ap() if hasattr(o, "_ap") else o for o in outs]
        ins = [o.ap() if hasattr(o, "_ap") else o for o in ins]

        def fn(e, outs=outs, ins=ins):
            return [e.dma_start(out=o, in_=i_) for o, i_ in zip(outs, ins)]
        return self._add(queue, fn, r, w, dma=True, chan=chan, n=len(outs))

    def seq(self, eng, steps):
        for fn, r, w in steps:
            self._add(eng, fn, r, w)

    def barrier(self):
        lasts = dict(self.last_on)
        for e in ENGS:
            for e2, i in lasts.items():
                if e2 != e:
                    self.pending[e].add(i)
            self.pending[e].update(self.dma_open)
        self.dma_open = []

    def emit(self, es):
        nc = self.nc
        ops = self.ops
        for op in ops:
            for d in op.deps:
                P = ops[d]
                if (not P.dma) and (P.eng != op.eng or op.eng != "pe"):
                    P.sig = True
        cnt = {e: 0 for e in ENGS}
        chcnt = {}
        for op in ops:
            if op.dma:
                chcnt[op.chan] = chcnt.get(op.chan, 0) + 16 * op.n
                op.dval = chcnt[op.chan]
            elif op.sig:
                cnt[op.eng] += 1
                op.sval = cnt[op.eng]
        esem = {e: es.enter_context(nc.semaphore("s_" + e)) for e in ENGS}
        csem = {}
        for k, ch in enumerate(chcnt):
            csem[ch] = es.enter_context(nc.semaphore("c%d" % k))
        self.stats = dict(n_ops=len(ops), sig=dict(cnt), nchan=len(chcnt))
        block = es.enter_context(nc.Block())
        bname = {"pe": "tensor", "act": "scalar", "dve": "vector", "pool": "gpsimd", "sp": "sync"}
        for eng in ENGS:
            def body(e, eng=eng):
                waited = {}
                for op in ops:
                    if op.eng != eng:
                        continue
                    need = {}
                    for d in op.deps:
                        P = ops[d]
                        if P.dma:
                            key, val = ("c", P.chan), P.dval
                        elif P.eng != eng or eng != "pe":
                            key, val = ("e", P.eng), P.sval
                        else:
                            continue
                        if need.get(key, 0) < val:
                            need[key] = val
                    for key, val in need.items():
                        if waited.get(key, 0) >= val:
                            continue
                        e.wait_ge(csem[key[1]] if key[0] == "c" else esem[key[1]], val)
                        waited[key] = val
                    if op.dma:
                        for inst in op.fn(e):
                            inst.then_inc(csem[op.chan], 16)
                    else:
                        inst = op.fn(e)
                        if op.sig:
                            inst.then_inc(esem[eng], 1)
                if eng == "sp":
                    for ch, v in chcnt.items():
                        if waited.get(("c", ch), 0) < v:
                            e.wait_ge(csem[ch], v)
                    for e2 in ENGS:
                        if e2 != "sp" and cnt[e2] > 0 and waited.get(("e", e2), 0) < cnt[e2]:
                            e.wait_ge(esem[e2], cnt[e2])
            getattr(block, bname[eng])(body)


class Arena:
    def __init__(self, nc, es, nbytes):
        self.t = es.enter_context(nc.sbuf_tensor("arena", [128, nbytes // 2], BF16))
        self.nbytes = nbytes
        self.base = 0
        self.off = 0

    def alloc(self, shape, dt):
        esz = 4 if dt == F32 else 2
        n = 1
        for s_ in shape[1:]:
            n *= s_
        nb = (n * esz + 63) // 64 * 64
        assert self.off + nb <= self.nbytes, ("arena overflow", self.off, nb, self.nbytes)
        a = self.t[:, self.off // 2:self.off // 2 + n * esz // 2]
        self.off += nb
        if dt == F32:
            a = a.bitcast(F32)
        if len(shape) == 3:
            a = a.rearrange("p (a b) -> p a b", a=shape[1])
        elif len(shape) == 4:
            a = a.rearrange("p (a b c) -> p a b c", a=shape[1], b=shape[2])
        if shape[0] != 128:
            a = a[0:shape[0]]
        return a

    def mark(self):
        self.base = self.off

    def reset(self):
        self.off = self.base


def bc_dram(ap_row, nparts=128):
    return bass.AP(ap_row.tensor, ap_row.offset, [[0, nparts]] + [list(x) for x in ap_row.ap[1:]])


def build(cfg):
    phases = cfg["phases"]
    roles = cfg.get("roles", {})
    nc = bass.Bass("TRN2", target_bir_lowering=False)
    es = ExitStack()
    dr = {}

    class LZ:
        def __init__(self, name, shape, dt, kind):
            self.name, self.shape, self.dt, self.kind, self._ap = name, shape, dt, kind, None

        def ap(self):
            if self._ap is None:
                self._ap = nc.dram_tensor(self.name, self.shape, self.dt, kind=self.kind).ap()
                dr[self.name] = self._ap
            return self._ap

        def __getitem__(self, k):
            return self.ap()[k]

        def rearrange(self, *a, **k):
            return self.ap().rearrange(*a, **k)

    def dram(name, shape, dt, kind="Internal"):
        k = roles.get(name, kind)
        k = {"in": "ExternalInput", "out": "ExternalOutput", "internal": "Internal"}.get(k, k)
        return LZ(name, shape, dt, k)

    IN = "ExternalInput"
    x_in = dram("x", [S, D], F32, IN)
    cT = dram("cT", [128, NCH], F32, IN)
    w_ada = dram("w_ada", [D, 9 * D], F32, IN)
    b_adaT = dram("b_adaT", [128, 144], F32, IN)
    gainsT = dram("gainsT", [128, 6, NCH], F32, IN)
    w1g = dram("ffn1_w_gate", [D, DFF], F32, IN)
    w1u = dram("ffn1_w_up", [D, DFF], F32, IN)
    w1d = dram("ffn1_w_down", [DFF, D], F32, IN)
    w2g = dram("ffn2_w_gate", [D, DFF], F32, IN)
    w2u = dram("ffn2_w_up", [D, DFF], F32, IN)
    w2d = dram("ffn2_w_down", [DFF, D], F32, IN)
    w_in = dram("w_in", [D, INW], F32, IN)
    lruvT = dram("lruvT", [128, 8, NCH], F32, IN)
    lru_w = dram("lru_w", [128, 2, NCH, 128], F32, IN)
    cmp_pe = dram("cmp_pe", [128, 2, 32], F32, IN)
    cmp_w1 = dram("cmp_w1", [128, 2, 32, 128], F32, IN)
    cmp_w2 = dram("cmp_w2", [128, 2, 128], F32, IN)
    w_a = dram("w_a_out", [D, D], F32, IN)
    w_b = dram("w_b_out", [D, D], F32, IN)
    w_o = dram("w_out", [D, D], F32, IN)
    c_rope = dram("c_rope", [32, 2, S], F32, IN)
    c_ident = dram("c_ident", [128, 128], F32, IN)
    c_si2 = dram("c_si2", [8, 256], F32, IN)
    c_pat = dram("c_pat", [8, 128], F32, IN)
    c_tri = dram("c_tri", [128, 2, 128], F32, IN)
    c_E = dram("c_E", [32, S], F32, IN)
    c_ovl = dram("c_ovl", [128, 32], F32, IN)
    c_sel = dram("c_sel", [128, 2, NCH, 32], F32, IN)
    out = dram("out", [S, D], F32, "ExternalOutput")
    vec_s = dram("vec_s", [9, D], F32)
    x1 = dram("x1", [S, D], F32)
    x2 = dram("x2", [S, D], F32)
    xa_s = dram("xa_s", [NCH, 128, S], F32)
    ya_s = dram("ya_s", [NCH, 128, S], F32)
    q_s = dram("q_s", [NCH, 128, S], F32)
    kv_s = dram("kv_s", [6, 4, 128, S], F32)
    vt_s = dram("vt_s", [2, S, 512], BF16)
    ng_s = dram("ng_s", [48, S], F32)
    mg_s = dram("mg_s", [32, 128, S], F32)
    lru_s = dram("lru_s", [NCH, 128, S], BF16)
    nsa_s = dram("nsa_s", [NCH, 128, S], BF16)
    m_s = dram("m_s", [NCH, 128, S], BF16)

    P = Prog(nc)
    ar = Arena(nc, es, 211968)
    pb = [es.enter_context(nc.psum_tensor("pb%d" % i, [128, 512], F32))[:] for i in range(8)]

    ident_f = ar.alloc([128, 128], F32)
    ident_b = ar.alloc([128, 128], BF16)
    ones_b = ar.alloc([128, 128], BF16)
    modT = ar.alloc([128, 144], F32)
    gT = ar.alloc([128, 6, NCH], F32)
    comb = ar.alloc([128, 144], F32)
    small = ar.alloc([128, 64], F32)
    NSLOT = 6
    wslot = [ar.alloc([128, 4096], BF16) for _ in range(NSLOT)]
    ar.mark()
    wctr = [0]
    ring = [0]

    def wload(src_ap, slot=None):
        if slot is None:
            k = ring[0] + wctr[0] % (NSLOT - ring[0])
            wctr[0] += 1
        else:
            k = slot
        res = ("w", k)
        n = 1
        for s_ in src_ap.shape[1:]:
            n *= s_
        assert n <= 4096
        v = wslot[k][:, 0:n]
        if len(src_ap.shape) == 3:
            v = v.rearrange("p (a b) -> p a b", a=src_ap.shape[1])
        P.dma("pool", v, src_ap, w=[res], chan=res)
        return v, res

    def wk(w_dram, c0, ncols):
        return w_dram[:, c0:c0 + ncols].rearrange("(kc p) n -> p kc n", p=128)

    P.dma("sp", ident_f, c_ident, w=["ident_f"], chan="ident_f")
    P.op("dve", lambda e: e.tensor_copy(out=ident_b, in_=ident_f), r=["ident_f"], w=["ident_b"])
    P.op("dve", lambda e: e.memset(ones_b, 1.0), w=["ones_b"])
    P.dma("sp", gT, gainsT, w=["gT"], chan="gT")

    def phase_ada():
        ar.reset()
        cact_f = ar.alloc([128, NCH], F32)
        cact = ar.alloc([128, NCH], BF16)
        badT = ar.alloc([128, 144], F32)
        tr_sb = ar.alloc([128, 2, 128], F32)
        P.dma("sp", cact_f, cT, w=["cact_f"], chan="cact_f")
        P.dma("sp", badT, b_adaT, w=["badT"], chan="badT")
        P.op("act", lambda e: e.activation(out=cact, in_=cact_f, func=AF.Silu), r=["cact_f"], w=["cact"])
        mod_ps = pb[0][:, 0:144]
        for j in range(72):
            wt, res = wload(wk(w_ada, j * 256, 256))
            for cc in range(2):
                col = j * 2 + cc

                def fn(e, wt=wt, cc=cc, col=col):
                    for kc in range(NCH):
                        ins = e.matmul(mod_ps[:, col:col + 1], lhsT=wt[:, kc, cc * 128:(cc + 1) * 128],
                                       rhs=cact[:, kc:kc + 1], start=(kc == 0), stop=(kc == NCH - 1))
                    return ins
                P.op("pe", fn, r=[res, "cact"], w=["mod_ps"])
        P.op("dve", lambda e: e.tensor_tensor(out=modT, in0=mod_ps, in1=badT, op=ALU.add),
             r=["mod_ps", "badT"], w=["modT"])
        for i in range(3):
            sh = modT[:, (3 * i) * 16:(3 * i + 1) * 16]
            sc = modT[:, (3 * i + 1) * 16:(3 * i + 2) * 16]
            ga = modT[:, (3 * i + 2) * 16:(3 * i + 3) * 16]
            A_ = comb[:, (3 * i) * 16:(3 * i + 1) * 16]
            B_ = comb[:, (3 * i + 1) * 16:(3 * i + 2) * 16]
            G_ = comb[:, (3 * i + 2) * 16:(3 * i + 3) * 16]
            wgt = 1.0 if i == 1 else 0.5

            def fn(e, sh=sh, sc=sc, ga=ga, A_=A_, B_=B_, G_=G_, i=i, wgt=wgt):
                e.scalar_tensor_tensor(out=A_, in0=sc, scalar=1.0, in1=gT[:, 2 * i, :], op0=ALU.add, op1=ALU.mult)
                e.tensor_copy(out=B_, in_=sh)
                return e.scalar_tensor_tensor(out=G_, in0=ga, scalar=wgt, in1=gT[:, 2 * i + 1, :],
                                              op0=ALU.mult, op1=ALU.mult)
            P.op("dve", fn, r=["modT", "gT"], w=["comb"])
        tr_ps = pb[1]
        for hh in range(2):
            P.op("pe", lambda e, hh=hh: e.transpose(tr_ps[0:72, hh * 128:(hh + 1) * 128],
                                                   comb[:, hh * 72:(hh + 1) * 72], ident_f),
                 r=["comb", "ident_f"], w=[("pb", 1)])
            P.op("dve", lambda e, hh=hh: e.tensor_copy(out=tr_sb[0:72, hh, :],
                                                       in_=tr_ps[0:72, hh * 128:(hh + 1) * 128]),
                 r=[("pb", 1)], w=[("tr_sb", hh)])
            dst = vec_s.rearrange("v (c f) -> (v c) f", f=128)[hh * 72:(hh + 1) * 72, :]
            P.dma("sp", dst, tr_sb[0:72, hh, :], r=[("tr_sb", hh)], w=["vec_s"], chan=("tr_sb", hh))
        P.barrier()

    def load_vec(dst, rdst, col0, repb, rrep):
        src = comb[:, col0:col0 + NCH]
        v = bass.AP(src.tensor, src.offset, [list(src.ap[0]), list(src.ap[1]), [0, 128]])
        rp = repb.rearrange("p (c f) -> p c f", c=NCH)
        P.op("dve", lambda e: e.tensor_copy(out=rp, in_=v), r=["comb"], w=list(rrep))
        for b4 in range(4):
            def fn(e, b4=b4):
                for c4 in range(4):
                    ins = e.transpose(pb[4 + b4][:, c4 * 128:(c4 + 1) * 128], rp[:, b4 * 4 + c4, :], ident_f)
                return ins
            P.op("pe", fn, r=list(rrep) + ["ident_f"], w=[("pb", 4 + b4)])
            P.op("act", lambda e, b4=b4: e.activation(out=dst[:, b4 * 512:(b4 + 1) * 512], in_=pb[4 + b4], func=AF.Copy),
                 r=[("pb", 4 + b4)], w=[rdst])

    def prologue(x_src, xname, vi, u, t0, T, bufs):
        xt2, utok, vecA, vecB, junk, repb, rrep = bufs
        junk = repb
        load_vec(vecA, "vecA", (3 * vi) * NCH, repb, rrep)
        load_vec(vecB, "vecB", (3 * vi + 1) * NCH, repb, rrep)
        nt = T // 128
        for ti in range(nt):
            xt = xt2[ti % 2]
            rx = ("xt", ti % 2)
            tok = t0 + ti * 128
            P.dma("sp", xt, x_src[tok:tok + 128, :], r=[("xd", xname, tok)], w=[rx], chan=rx)
            ssq = small[:, 0:1]
            rstd = small[:, 1:2]
            P.op("dve", lambda e: e.memset(ssq, 0.0), w=["ssq"])
            P.op("act", lambda e, xt=xt: e.activation(out=junk, in_=xt, func=AF.Square, accum_out=ssq),
                 r=[rx, "ssq"], w=list(rrep) + ["ssq"])
            P.op("dve", lambda e: e.tensor_scalar(out=rstd, in0=ssq, scalar1=1.0 / D, scalar2=EPS,
                                                  op0=ALU.mult, op1=ALU.add), r=["ssq"], w=["rstd"])
            P.op("act", lambda e: e.activation(out=rstd, in_=rstd, func=AF.Sqrt), r=["rstd"], w=["rstd"])
            P.op("dve", lambda e: e.reciprocal(out=rstd, in_=rstd), r=["rstd"], w=["rstd"])
            P.op("dve", lambda e, xt=xt: e.scalar_tensor_tensor(out=xt, in0=xt, scalar=rstd, in1=vecA,
                                                                op0=ALU.mult, op1=ALU.mult),
                 r=[rx, "rstd", "vecA"], w=[rx])
            P.op("pool", lambda e, xt=xt: e.tensor_tensor(out=utok, in0=xt, in1=vecB, op=ALU.add),
                 r=[rx, "vecB"], w=["utok"])
            bk = 2 * (ti % 2)
            for hb in range(2):
                pbank = pb[bk + hb][:].bitcast(BF16)

                def fn(e, pbank=pbank, hb=hb):
                    for c8 in range(8):
                        dc = hb * 8 + c8
                        ins = e.transpose(pbank[:, c8 * 128:(c8 + 1) * 128], utok[:, dc * 128:(dc + 1) * 128], ident_b)
                    return ins
                P.op("pe", fn, r=["utok", "ident_b"], w=[("pb", bk + hb)])
                P.op("act", lambda e, pbank=pbank, hb=hb, ti=ti: e.activation(
                    out=u[:, hb * 8:(hb + 1) * 8, ti * 128:(ti + 1) * 128],
                    in_=pbank.rearrange("p (c t) -> p c t", c=8), func=AF.Copy),
                    r=[("pb", bk + hb)], w=[("u", ti // 4)])

    def epilogue(acc, x_src, xname, vi, x_dst, dname, t0, T, bufs):
        xt2, yt, vecG, rG = bufs
        load_vec(vecG, rG, (3 * vi + 2) * NCH, yt, [("yt", b) for b in range(4)])
        nt = T // 128
        for ti in range(nt):
            xt = xt2[ti % 2]
            rx = ("xt", ti % 2)
            tok = t0 + ti * 128
            P.dma("sp", xt, x_src[tok:tok + 128, :], r=[("xd", xname, tok)], w=[rx], chan=rx)
            bk = 4 * (ti % 2)
            P.op("dve", lambda e: e.memset(small[:, 8:40], 0.0), r=[("ssq4", b) for b in range(4)], w=["ssq4s"])
            ssq = small[:, 0:1]
            rstd = small[:, 1:2]
            for b4 in range(4):
                def fn(e, b4=b4, bk=bk, ti=ti):
                    for c4 in range(4):
                        dc = b4 * 4 + c4
                        ins = e.transpose(pb[bk + b4][:, c4 * 128:(c4 + 1) * 128],
                                          acc[:, dc, ti * 128:(ti + 1) * 128], ident_f)
                    return ins
                P.op("pe", fn, r=[("acc", dc_, ti // 4) for dc_ in range(b4 * 4, b4 * 4 + 4)] + ["ident_f"],
                     w=[("pb", bk + b4)])
                P.op("act", lambda e, b4=b4, bk=bk: e.activation(out=yt[:, b4 * 512:(b4 + 1) * 512], in_=pb[bk + b4],
                                                                func=AF.Square, accum_out=small[:, 8 + 8 * b4:9 + 8 * b4]),
                     r=[("pb", bk + b4), "ssq4s"], w=[("yt", b4), ("ssq4", b4)])
            P.op("dve", lambda e: e.tensor_reduce(out=ssq, in_=small[:, 8:40:8], axis=mybir.AxisListType.X, op=ALU.add),
                 r=[("ssq4", b) for b in range(4)], w=["ssq"])
            P.op("dve", lambda e: e.tensor_scalar(out=rstd, in0=ssq, scalar1=1.0 / D, scalar2=EPS,
                                                  op0=ALU.mult, op1=ALU.add), r=["ssq"], w=["rstd"])
            P.op("act", lambda e: e.activation(out=rstd, in_=rstd, func=AF.Sqrt), r=["rstd"], w=["rstd"])
            P.op("dve", lambda e: e.reciprocal(out=rstd, in_=rstd), r=["rstd"], w=["rstd"])
            for b4 in range(4):
                P.op("dve", lambda e, b4=b4, bk=bk: e.scalar_tensor_tensor(
                    out=yt[:, b4 * 512:(b4 + 1) * 512], in0=pb[bk + b4], scalar=rstd,
                    in1=vecG[:, b4 * 512:(b4 + 1) * 512], op0=ALU.mult, op1=ALU.mult),
                    r=[("pb", bk + b4), "rstd", rG], w=[("yt", b4)])
            P.op("pool", lambda e, xt=xt: e.tensor_tensor(out=xt, in0=xt, in1=yt, op=ALU.add),
                 r=[rx] + [("yt", b) for b in range(4)], w=[rx])
            P.dma("sp", x_dst[tok:tok + 128, :], xt, r=[rx], w=[("xd", dname, tok)], chan=rx)

    def ffn_half(u, acc, hbuf, sbuf, wg, wu, wd):
        NG = DFF // 256
        TT = 2

        def loads_gu(g):
            a = wload(wk(wg, g * 256, 256), slot=(g % 2) * 2)
            b = wload(wk(wu, g * 256, 256), slot=(g % 2) * 2 + 1)
            return a, b

        def loads_d(g):
            return wload(wd[g * 256:(g + 1) * 256, :].rearrange("(fc p) n -> p fc n", p=128), slot=4 + g % 2)
        pgu = {0: loads_gu(0), 1: loads_gu(1)}
        pd = {0: loads_d(0)}
        down_q = []

        def emit_down(g, wdt, wdres, hb, hres):
            items = []
            for dc in range(NCH):
                for tt in range(TT):
                    items.append((g, dc, tt, wdt, wdres, hb, hres))
            return items

        yctr = [0]

        def do_down(item):
            g, dc, tt, wdt, wdres, hb, hres = item
            bank = 4 + (yctr[0] % 4)
            yctr[0] += 1
            yp = pb[bank]

            def fn(e):
                for fc in range(2):
                    ins = e.matmul(yp, lhsT=wdt[:, fc, dc * 128:(dc + 1) * 128], rhs=hb[:, fc, tt * 512:(tt + 1) * 512],
                                   start=(fc == 0), stop=(fc == 1))
                return ins
            P.op("pe", fn, r=[wdres, hres], w=[("pb", bank)])
            a_ = acc[:, dc, tt * 512:(tt + 1) * 512]
            ra = ("acc", dc, tt)
            if g == 0:
                P.op("act", lambda e: e.activation(out=a_, in_=yp, func=AF.Copy), r=[("pb", bank)], w=[ra])
            else:
                P.op("dve", lambda e: e.tensor_tensor(out=a_, in0=yp, in1=a_, op=ALU.add), r=[("pb", bank), ra], w=[ra])

        for g in range(NG):
            (wgt, wgres), (wut, wures) = pgu.pop(g)
            wdt, wdres = pd.pop(g)
            hb = hbuf[g % 2]
            hres = ("h", g % 2)
            k = 0
            for fc in range(2):
                for tt in range(TT):
                    gp = pb[(k % 2) * 2]
                    up = pb[(k % 2) * 2 + 1]
                    rg = ("pb", (k % 2) * 2)
                    ru = ("pb", (k % 2) * 2 + 1)

                    def fng(e, gp=gp, wgt=wgt, fc=fc, tt=tt):
                        for kc in range(NCH):
                            ins = e.matmul(gp, lhsT=wgt[:, kc, fc * 128:(fc + 1) * 128],
                                           rhs=u[:, kc, tt * 512:(tt + 1) * 512], start=(kc == 0), stop=(kc == NCH - 1))
                        return ins

                    def fnu(e, up=up, wut=wut, fc=fc, tt=tt):
                        for kc in range(NCH):
                            ins = e.matmul(up, lhsT=wut[:, kc, fc * 128:(fc + 1) * 128],
                                           rhs=u[:, kc, tt * 512:(tt + 1) * 512], start=(kc == 0), stop=(kc == NCH - 1))
                        return ins
                    P.op("pe", fng, r=[wgres, ("u", tt)], w=[rg])
                    P.op("pe", fnu, r=[wures, ("u", tt)], w=[ru])
                    sb_ = sbuf[k % 2]
                    rs = ("s", k % 2)
                    P.op("act", lambda e, gp=gp, sb_=sb_: e.activation(out=sb_, in_=gp, func=AF.Silu), r=[rg], w=[rs])
                    P.op("dve", lambda e, up=up, sb_=sb_, hb=hb, fc=fc, tt=tt: e.tensor_tensor(
                        out=hb[:, fc, tt * 512:(tt + 1) * 512], in0=up, in1=sb_, op=ALU.mult),
                        r=[ru, rs], w=[hres])
                    k += 1
                    for _ in range(8):
                        if down_q:
                            do_down(down_q.pop(0))
            while down_q:
                do_down(down_q.pop(0))
            down_q = emit_down(g, wdt, wdres, hb, hres)
            if g + 2 < NG:
                pgu[g + 2] = loads_gu(g + 2)
            if g + 1 < NG:
                pd[g + 1] = loads_d(g + 1)
        while down_q:
            do_down(down_q.pop(0))

    def ffn_phase(x_src, xname, vi, x_dst, dname, wg, wu, wd):
        ar.reset()
        u = ar.alloc([128, NCH, 1024], BF16)
        acc = ar.alloc([128, NCH, 1024], F32)
        hbuf = [ar.alloc([128, 2, 1024], BF16) for _ in range(2)]
        sbuf = [ar.alloc([128, 512], F32) for _ in range(2)]
        xt2 = [ar.alloc([128, D], F32) for _ in range(2)]
        yt = ar.alloc([128, D], F32)
        utok = ar.alloc([128, D], BF16)
        vecA = ar.alloc([128, D], F32)
        vecB = ar.alloc([128, D], F32)
        junk = None
        for half in range(2):
            t0 = half * 1024
            prologue(x_src, xname, vi, u, t0, 1024, (xt2, utok, vecA, vecB, junk, yt, [("yt", b) for b in range(4)]))
            ffn_half(u, acc, hbuf, sbuf, wg, wu, wd)
            epilogue(acc, x_src, xname, vi, x_dst, dname, t0, 1024, (xt2, yt, vecA, "vecA"))
        P.barrier()

    def phase_win():
        ar.reset()
        u2 = ar.alloc([128, NCH, S], BF16)
        off_u2 = ar.off
        xt2 = [ar.alloc([128, D], F32) for _ in range(2)]
        utok = ar.alloc([128, D], BF16)
        vecA = ar.alloc([128, D], F32)
        vecB = ar.alloc([128, D], F32)
        junk = None
        repw = ar.alloc([128, D], F32)
        prologue(x1, "x1", 1, u2, 0, S, (xt2, utok, vecA, vecB, junk, repw, ["repw"]))
        P.barrier()
        ar.off = off_u2
        stg = [ar.alloc([128, S], F32) for _ in range(3)]
        vst = [ar.alloc([128, NCH, 256], BF16) for _ in range(2)]
        ctr = {"bank": 0, "stg": 0, "ev": 0, "vst": 0}
        ring[0] = 2
        lru_q = lru_setup(ctr)

        def drain(n):
            for _ in range(n):
                if lru_q["pending"]:
                    lru_q["pending"].pop(0)()

        def fm_tile(c0, dsts, sig=False, ncols=256, names=None):
            wt, res = wload(wk(w_in, c0, ncols))
            nchunk = (ncols + 127) // 128
            for cc in range(nchunk):
                m = min(128, ncols - cc * 128)
                k = ctr["stg"] % 3
                ctr["stg"] += 1
                st = stg[k]
                for tt in range(4):
                    bank = ctr["bank"] % 8
                    ctr["bank"] += 1
                    ps = pb[bank]

                    def fn(e, ps=ps, wt=wt, cc=cc, m=m, tt=tt):
                        for kc in range(NCH):
                            ins = e.matmul(ps[0:m, :], lhsT=wt[:, kc, cc * 128:cc * 128 + m],
                                           rhs=u2[:, kc, tt * 512:(tt + 1) * 512], start=(kc == 0), stop=(kc == NCH - 1))
                        return ins
                    P.op("pe", fn, r=[res, ("u", tt)], w=[("pb", bank)])
                    o_ = st[0:m, tt * 512:(tt + 1) * 512]
                    if sig:
                        P.op("act", lambda e, o_=o_, ps=ps, m=m: e.activation(out=o_, in_=ps[0:m, :], func=AF.Sigmoid),
                             r=[("pb", bank)], w=[("stg", k)])
                    else:
                        ctr["ev"] += 1
                        if ctr["ev"] % 2:
                            P.op("act", lambda e, o_=o_, ps=ps, m=m: e.activation(out=o_, in_=ps[0:m, :], func=AF.Copy),
                                 r=[("pb", bank)], w=[("stg", k)])
                        else:
                            P.op("dve", lambda e, o_=o_, ps=ps, m=m: e.tensor_copy(out=o_, in_=ps[0:m, :]),
                                 r=[("pb", bank)], w=[("stg", k)])
                    drain(2)
                P.dma("sp", dsts[cc], st[0:m, :], r=[("stg", k)], w=[names[cc] if names else ("win_out", c0, cc)],
                      chan=("stg", k))

        def tm_tile(c0, which, hcol):
            wt, res = wload(wk(w_in, c0, 256))
            k = ctr["vst"] % 2
            ctr["vst"] += 1
            vs = vst[k]
            for ti in range(NCH):
                bank = ctr["bank"] % 8
                ctr["bank"] += 1
                ps = pb[bank]

                def fn(e, ps=ps, wt=wt, ti=ti):
                    for kc in range(NCH):
                        ins = e.matmul(ps[:, 0:256], lhsT=u2[:, kc, ti * 128:(ti + 1) * 128], rhs=wt[:, kc, :],
                                       start=(kc == 0), stop=(kc == NCH - 1))
                    return ins
                P.op("pe", fn, r=[res, ("u", ti // 4)], w=[("pb", bank)])
                ctr["ev"] += 1
                if ctr["ev"] % 2:
                    P.op("act", lambda e, ps=ps, vs=vs, ti=ti: e.activation(out=vs[:, ti, :], in_=ps[:, 0:256], func=AF.Copy),
                         r=[("pb", bank)], w=[("vst", k)])
                else:
                    P.op("dve", lambda e, ps=ps, vs=vs, ti=ti: e.tensor_copy(out=vs[:, ti, :], in_=ps[:, 0:256]),
                         r=[("pb", bank)], w=[("vst", k)])
                if ti % 2:
                    drain(2)
            dst = vt_s[which].rearrange("(ti p) c -> p ti c", p=128)[:, :, hcol * 256:(hcol + 1) * 256]
            P.dma("sp", dst, vs, r=[("vst", k)], w=[("vt_out", which, hcol)], chan=("vst", k))

        for j in range(8):
            fm_tile(j * 256, [xa_s[2 * j], xa_s[2 * j + 1]], names=[("xa_s", 2 * j), ("xa_s", 2 * j + 1)])
            fm_tile(2048 + j * 256, [ya_s[2 * j], ya_s[2 * j + 1]], names=[("ya_s", 2 * j), ("ya_s", 2 * j + 1)])
            lru_q["pending"] += lru_chunk_ops(lru_q, 2 * j)
            lru_q["pending"] += lru_chunk_ops(lru_q, 2 * j + 1)
        for j in range(8):
            fm_tile(4096 + j * 256, [q_s[2 * j], q_s[2 * j + 1]])
        for i in range(6):
            for hcol in range(2):
                c0 = 6144 + i * 512 + hcol * 256
                if i == 3:
                    tm_tile(c0, 0, hcol)
                elif i == 5:
                    tm_tile(c0, 1, hcol)
                else:
                    fm_tile(c0, [kv_s[i, 2 * hcol], kv_s[i, 2 * hcol + 1]])
        fm_tile(9216, [ng_s], ncols=48)
        for j in range(16):
            fm_tile(9264 + j * 256, [mg_s[2 * j], mg_s[2 * j + 1]], sig=True)
        drain(10 ** 6)
        ring[0] = 0
        P.barrier()

    def lru_setup(ctr):
        lv = ar.alloc([128, 8, NCH], F32)
        cvec = ar.alloc([128, NCH], F32)
        c2vec = ar.alloc([128, NCH], F32)
        onef = ar.alloc([128, 1], F32)
        Lq = dict(lv=lv, cvec=cvec, c2vec=c2vec, onef=onef, ctr=ctr, pending=[])
        for nm in ("xa", "ya", "xc", "rr", "ii"):
            Lq[nm] = ar.alloc([128, S], F32)
        Lq["xcb"] = ar.alloc([128, S], BF16)
        P.dma("sp", lv, lruvT, w=["lv"], chan="lv")
        P.op("dve", lambda e: e.memset(onef, 1.0), w=["onef"])
        Lq["wr"] = wload(lru_w[:, 0], slot=0)
        Lq["wi"] = wload(lru_w[:, 1], slot=1)
        P.op("act", lambda e: e.activation(out=cvec, in_=lv[:, 7, :], func=AF.Exp, scale=-1.0), r=["lv"], w=["cvec"])
        P.op("dve", lambda e: e.tensor_scalar(out=cvec, in0=cvec, scalar1=1.0, scalar2=None, op0=ALU.add),
             r=["cvec"], w=["cvec"])
        P.op("act", lambda e: e.activation(out=cvec, in_=cvec, func=AF.Ln), r=["cvec"], w=["cvec"])
        P.op("dve", lambda e: e.tensor_scalar(out=c2vec, in0=cvec, scalar1=-16.0, scalar2=None, op0=ALU.mult),
             r=["cvec"], w=["c2vec"])
        P.op("dve", lambda e: e.tensor_scalar(out=cvec, in0=cvec, scalar1=-8.0, scalar2=None, op0=ALU.mult),
             r=["cvec"], w=["cvec"])
        return Lq

    def lru_chunk_ops(Lq, c):
        lv, cvec, c2vec, onef, ctr = Lq["lv"], Lq["cvec"], Lq["c2vec"], Lq["onef"], Lq["ctr"]
        xa, ya, xc, rr, ii, xcb = Lq["xa"], Lq["ya"], Lq["xc"], Lq["rr"], Lq["ii"], Lq["xcb"]
        (wr_t, wr_res), (wi_t, wi_res) = Lq["wr"], Lq["wi"]
        L = []
        L.append(lambda: P.dma("sp", xa, xa_s[c], r=[("xa_s", c)], w=["l_xa"], chan="l_xa"))
        L.append(lambda: P.dma("sp", ya, ya_s[c], r=[("ya_s", c)], w=["l_ya"], chan="l_ya"))
        L.append(lambda: P.op("dve", lambda e: e.tensor_scalar(out=xc, in0=xa, scalar1=lv[:, 3, c:c + 1],
                                                              scalar2=lv[:, 4, c:c + 1], op0=ALU.mult, op1=ALU.add),
                              r=["l_xa", "lv"], w=["l_xc"]))
        for sft in (1, 2, 3):
            L.append(lambda sft=sft: P.op("dve", lambda e: e.scalar_tensor_tensor(
                out=xc[:, sft:S], in0=xa[:, 0:S - sft], scalar=lv[:, 3 - sft, c:c + 1], in1=xc[:, sft:S],
                op0=ALU.mult, op1=ALU.add), r=["l_xa", "lv", "l_xc"], w=["l_xc"]))
        L.append(lambda: P.op("act", lambda e: e.activation(out=xcb, in_=xc, func=AF.Copy), r=["l_xc"], w=["l_xcb"]))
        for gi, (wt, wres, dst, bcol, rdst) in enumerate(((wr_t, wr_res, rr, 5, "l_rr"), (wi_t, wi_res, ii, 6, "l_ii"))):
            for tt in range(4):
                def mm(wt=wt, wres=wres, dst=dst, bcol=bcol, rdst=rdst, tt=tt):
                    bank = ctr["bank"] % 8
                    ctr["bank"] += 1
                    ps = pb[bank]
                    P.op("pe", lambda e: e.matmul(ps, lhsT=wt[:, c, :], rhs=xcb[:, tt * 512:(tt + 1) * 512],
                                                  start=True, stop=True), r=[wres, "l_xcb"], w=[("pb", bank)])
                    P.op("act", lambda e: e.activation(out=dst[:, tt * 512:(tt + 1) * 512], in_=ps, func=AF.Sigmoid,
                                                       bias=lv[:, bcol, c:c + 1]), r=[("pb", bank), "lv"], w=[rdst])
                L.append(mm)
        L.append(lambda: P.op("act", lambda e: e.activation(out=xa, in_=rr, func=AF.Exp, scale=cvec[:, c:c + 1]),
                              r=["l_rr", "cvec", "l_xc"], w=["l_xa"]))
        L.append(lambda: P.op("act", lambda e: e.activation(out=rr, in_=rr, func=AF.Exp, scale=c2vec[:, c:c + 1]),
                              r=["l_rr", "c2vec"], w=["l_rr"]))
        L.append(lambda: P.op("act", lambda e: e.activation(out=rr, in_=rr, func=AF.Sqrt, scale=-1.0, bias=onef[:, 0:1]),
                              r=["l_rr", "onef"], w=["l_rr"]))
        L.append(lambda: P.op("act", lambda e: e.activation(out=ya, in_=ya, func=AF.Gelu_apprx_tanh), r=["l_ya"], w=["l_ya"]))
        L.append(lambda: P.op("dve", lambda e: e.tensor_tensor(out=ii, in0=ii, in1=xc, op=ALU.mult),
                              r=["l_ii", "l_xc"], w=["l_ii"]))
        L.append(lambda: P.op("dve", lambda e: e.tensor_tensor(out=rr, in0=rr, in1=ii, op=ALU.mult),
                              r=["l_rr", "l_ii"], w=["l_rr"]))
        L.append(lambda: P.op("dve", lambda e: e.tensor_tensor_scan(out=xc, data0=xa, data1=rr, initial=0.0,
                                                                   op0=ALU.mult, op1=ALU.add),
                              r=["l_xa", "l_rr", "l_ii"], w=["l_xc"]))
        L.append(lambda: P.op("dve", lambda e: e.tensor_tensor(out=xcb, in0=xc, in1=ya, op=ALU.mult),
                              r=["l_xc", "l_ya"], w=["l_xcb"]))
        L.append(lambda: P.dma("sp", lru_s[c], xcb, r=["l_xcb"], w=[("lru_out",)], chan="l_xcb"))
        return L

    def phase_nsa():
        ar.reset()
        rope = ar.alloc([32, 2, S], F32)
        si2 = ar.alloc([8, 256], BF16)
        pat_f = ar.alloc([8, 1, 128], F32)
        pat4 = ar.alloc([8, 1, 4, 128], BF16)
        tri_f = ar.alloc([128, 2, 128], F32)
        tri4 = ar.alloc([128, 2, 4, 128], BF16)
        Eb = ar.alloc([32, S], BF16)
        ovl = ar.alloc([128, 32], F32)
        selc = ar.alloc([128, 2, NCH, 32], F32)
        w2b = ar.alloc([128, 2, 128], BF16)
        peb = ar.alloc([128, 2, 32], BF16)
        bias_kv = ar.alloc([128, 2], F32)
        P.dma("sp", rope, c_rope, w=["rope"], chan="rope")
        P.dma("pool", si2, c_si2, w=["si2"], chan="si2")
        P.dma("sp", pat_f, c_pat.rearrange("k (o q) -> k o q", o=1), w=["pat_f"], chan="pat_f")
        P.dma("sp", tri_f, c_tri, w=["tri_f"], chan="tri_f")
        P.dma("pool", Eb, c_E, w=["Eb"], chan="Eb")
        P.dma("sp", ovl, c_ovl, w=["ovl"], chan="ovl")
        P.dma("sp", selc, c_sel, w=["selc"], chan="selc")
        P.dma("pool", w2b, cmp_w2, w=["w2b"], chan="w2b")
        P.dma("pool", peb, cmp_pe, w=["peb"], chan="peb")
        w1k, w1k_res = wload(cmp_w1[:, 0], slot=0)
        w1v, w1v_res = wload(cmp_w1[:, 1], slot=1)
        w1 = (w1k, w1v)
        w1res = (w1k_res, w1v_res)

        def bcast4(e, dst, src, n_outer):
            s_ = src
            v = bass.AP(s_.tensor, s_.offset, [list(s_.ap[0]), list(s_.ap[1]), [0, 4], list(s_.ap[-1])])
            return e.tensor_copy(out=dst, in_=v)
        P.op("dve", lambda e: bcast4(e, pat4, pat_f, 1), r=["pat_f"], w=["pat4"])
        P.op("dve", lambda e: bcast4(e, tri4, tri_f, 2), r=["tri_f"], w=["tri4"])
        for kv in range(2):
            def fn(e, kv=kv):
                for l in range(32):
                    ins = e.matmul(pb[6][:, kv:kv + 1], lhsT=w1[kv][:, l, :], rhs=peb[:, kv, l:l + 1],
                                   start=(l == 0), stop=(l == 31))
                return ins
            P.op("pe", fn, r=[w1res[kv], "peb"], w=[("pb", 6)])
        P.op("dve", lambda e: e.tensor_copy(out=bias_kv, in_=pb[6][:, 0:2]), r=[("pb", 6)], w=["bias_kv"])

        tf2 = [ar.alloc([128, S], F32) for _ in range(2)]
        kb = [ar.alloc([128, S], BF16) for _ in range(4)]
        tp = ar.alloc([32, S], F32)
        qb = ar.alloc([128, 4, S], BF16)
        vtm = [ar.alloc([128, NCH, 128], BF16) for _ in range(2)]
        glq2 = [ar.alloc([128, 12, 256], F32) for _ in range(2)]
        nsa_g = ar.alloc([128, 4, S], BF16)
        NPT = 6
        ptr = [ar.alloc([128, 4, 128], BF16) for _ in range(NPT)]
        hcmp = ar.alloc([128, 2, 128], BF16)
        kcT = ar.alloc([128, 128], BF16)
        vc = ar.alloc([128, 128], BF16)
        pn = ar.alloc([128, 4, 128], F32)
        rd = ar.alloc([128, 4, 128], F32)
        wg_ = ar.alloc([128, 4, 128], F32)
        oacc2 = [ar.alloc([128, 4, 128], F32) for _ in range(2)]
        tmpb = ar.alloc([128, 4, 128], F32)
        val = ar.alloc([128, 32], F32)
        val2 = ar.alloc([128, 32], F32)
        m8 = ar.alloc([128, 16], F32)
        negsel = ar.alloc([128, 32], BF16)
        nselT4 = ar.alloc([32, 4, 128], BF16)
        print("nsa arena used", ar.off, "of", ar.nbytes)

        def v4(ap512):
            return ap512.rearrange("p (h q) -> p h q", h=4)

        def rope_inplace(tf, rtf, src):
            P.dma("sp", [tp[0:16], tp[16:32]], [src[16:32, :], src[0:16, :]], r=[("win_out",)], w=["tp"], chan="tp")

            P.op("pool", lambda e: e.tensor_tensor(out=tp, in0=tp, in1=rope[:, 1, :], op=ALU.mult),
                 r=["tp", "rope"], w=["tp"])
            P.op("pool", lambda e, tf=tf: e.tensor_tensor(out=tf[0:32, :], in0=tf[0:32, :], in1=rope[:, 0, :], op=ALU.mult),
                 r=[rtf, "rope"], w=[rtf])
            P.op("pool", lambda e, tf=tf: e.tensor_tensor(out=tf[0:32, :], in0=tf[0:32, :], in1=tp, op=ALU.add),
                 r=[rtf, "tp"], w=[rtf])

        brc = [0]
        tfc = [0]
        sctr = [0]
        pctr = [0]

        for g in range(4):
            srcs = [kv_s[0, g], kv_s[1, g], kv_s[2, g], kv_s[4, g]]
            for i4 in range(4):
                tfb = tf2[tfc[0] % 2]
                rtf = ("tf", tfc[0] % 2)
                tfc[0] += 1
                P.dma("sp", tfb, srcs[i4], r=[("win_out",)], w=[rtf], chan=rtf)
                if i4 != 1:
                    rope_inplace(tfb, rtf, srcs[i4])
                P.op("act", lambda e, i4=i4, tfb=tfb: e.activation(out=kb[i4], in_=tfb, func=AF.Copy),
                     r=[rtf], w=[("kb", i4)])
            for w_ in range(2):
                src = vt_s[w_].rearrange("(ti p) c -> p ti c", p=128)[:, :, g * 128:(g + 1) * 128]
                P.dma("sp", vtm[w_], src, r=[("vt_out",)], w=[("vtm", w_)], chan=("vtm", w_))
            for kv in range(2):
                src_b = kb[kv]
                hps = pb[6]

                def fn(e, kv=kv, src_b=src_b, hps=hps):
                    for l in range(32):
                        rhs = bass.AP(src_b.tensor, src_b.offset + l, [list(src_b.ap[0]), [16, 127]])
                        ins = e.matmul(hps[:, 0:127], lhsT=w1[kv][:, l, :], rhs=rhs,
                                       start=(l == 0), stop=(l == 31))
                    return ins
                P.op("pe", fn, r=[w1res[kv], ("kb", kv)], w=[("pb", 6)])
                P.op("act", lambda e, kv=kv, hps=hps: e.activation(out=hcmp[:, kv, 0:127], in_=hps[:, 0:127],
                                                                  func=AF.Gelu_apprx_tanh, bias=bias_kv[:, kv:kv + 1]),
                     r=[("pb", 6), "bias_kv"], w=[("hcmp", kv)])
                ops_ = pb[7]
                if kv == 0:
                    P.op("pe", lambda e, ops_=ops_: e.matmul(ops_[:, 0:127], lhsT=w2b[:, 0, :], rhs=hcmp[:, 0, 0:127],
                                                             start=True, stop=True),
                         r=["w2b", ("hcmp", 0)], w=[("pb", 7)])
                    P.op("dve", lambda e, ops_=ops_: e.tensor_copy(out=kcT[:, 0:127], in_=ops_[:, 0:127]),
                         r=[("pb", 7)], w=["kcT"])
                else:
                    P.op("pe", lambda e, ops_=ops_: e.matmul(ops_[0:127, 0:128], lhsT=hcmp[:, 1, 0:127], rhs=w2b[:, 1, :],
                                                             start=True, stop=True),
                         r=["w2b", ("hcmp", 1)], w=[("pb", 7)])
                    P.op("dve", lambda e, ops_=ops_: e.tensor_copy(out=vc[0:127, :], in_=ops_[0:127, 0:128]),
                         r=[("pb", 7)], w=["vc"])
            for h in range(4):
                qf = tf2[tfc[0] % 2]
                rq_ = ("tf", tfc[0] % 2)
                tfc[0] += 1
                P.dma("sp", qf, q_s[g * 4 + h], r=[("win_out",)], w=[rq_], chan=rq_)
                rope_inplace(qf, rq_, q_s[g * 4 + h])
                P.op("act", lambda e, qf=qf, h=h: e.activation(out=qb[:, h, :], in_=qf, func=AF.Copy),
                     r=[rq_], w=[("qb", h)])
            kslc_b, kwin_b = kb[2], kb[3]

            SB = [0, 1, 6, 7]
            DEPTH = 3
            fifo = []

            def push(item):
                fifo.append(item)
                while len(fifo) > DEPTH:
                    fifo.pop(0)()

            def drain_all():
                while fifo:
                    fifo.pop(0)()

            def attend(tiles, rq, post):
                pair = brc[0] % 2
                brc[0] += 1
                Db, Ob = 2 + 2 * pair, 3 + 2 * pair
                Dp, Op_ = pb[Db], pb[Ob]
                n = len(tiles)
                for ti in range(n):
                    nk, lk, masks, v_ap, rl = tiles[ti]
                    sb_ = SB[sctr[0] % 4]
                    sctr[0] += 1
                    Sp = pb[sb_]

                    def fn(e, Sp=Sp, nk=nk, lk=lk, masks=masks):
                        ins = e.matmul(v4(Sp)[0:nk], lhsT=lk, rhs=rq, start=True, stop=(len(masks) == 0))
                        for mi, (ml, mr, _) in enumerate(masks):
                            ins = e.matmul(v4(Sp)[0:nk], lhsT=ml, rhs=mr, start=False, stop=(mi == len(masks) - 1))
                        return ins
                    rl2 = list(rl)
                    for (_, _, mrl) in masks:
                        rl2 += mrl
                    P.op("pe", fn, r=rl2 + [("qb", hh_) for hh_ in range(4)] + ["ident_b"], w=[("pb", sb_)])
                    pk = pctr[0] % NPT
                    pctr[0] += 1
                    pt = ptr[pk]
                    P.op("act", lambda e, Sp=Sp, pt=pt, nk=nk: e.activation(out=pt[0:nk], in_=v4(Sp)[0:nk], func=AF.Exp,
                                                                            scale=SCALE),
                         r=[("pb", sb_)], w=[("pt", pk)])

                    def pv(ti=ti, pt=pt, pk=pk, nk=nk, v_ap=v_ap, rl=rl):
                        def fnp(e):
                            e.matmul(v4(Dp), lhsT=ones_b[0:nk, :], rhs=pt[0:nk], start=(ti == 0), stop=(ti == n - 1))
                            return e.matmul(v4(Op_), lhsT=v_ap, rhs=pt[0:nk], start=(ti == 0), stop=(ti == n - 1))
                        P.op("pe", fnp, r=[("pt", pk), "ones_b"] + list(rl), w=[("pb", Db), ("pb", Ob)])
                        if ti == n - 1:
                            post(Db, Ob, (pt, ("pt", pk)))
                    push(pv)

            def combine(Db, Ob, br, qt, first):
                glq = glq2[(qt // 2) % 2]
                rglq = ("glq", (qt // 2) % 2)
                oacc = oacc2[qt % 2]
                roa = ("oacc", qt % 2)
                gsl = glq[:, br:12:3, (qt % 2) * 128:(qt % 2) * 128 + 128]
                if not first:
                    P.op("dve", lambda e: e.reciprocal(out=rd, in_=v4(pb[Db])), r=[("pb", Db)], w=["rd"])
                P.op("pool", lambda e: e.tensor_tensor(out=wg_, in0=rd, in1=gsl, op=ALU.mult), r=["rd", rglq], w=["wg"])
                if first:
                    P.op("dve", lambda e: e.tensor_tensor(out=oacc, in0=v4(pb[Ob]), in1=wg_, op=ALU.mult),
                         r=[("pb", Ob), "wg"], w=[roa])
                else:
                    P.op("dve", lambda e: e.tensor_tensor(out=tmpb, in0=v4(pb[Ob]), in1=wg_, op=ALU.mult),
                         r=[("pb", Ob), "wg"], w=["tmpb"])
                    P.op("pool", lambda e: e.tensor_tensor(out=oacc, in0=oacc, in1=tmpb, op=ALU.add),
                         r=[roa, "tmpb"], w=[roa])

            def emit_imp(qt, ncnt):
                def fni(e):
                    for h in range(4):
                        ins = e.matmul(pb[6][:, 0:32], lhsT=pn[0:ncnt, h, :], rhs=ovl[0:ncnt, :],
                                       start=(h == 0), stop=(h == 3))
                    return ins
                P.op("pe", fni, r=["pn", "ovl"], w=[("pb", 6)])
                P.seq("dve", [
                    (lambda e: e.tensor_tensor(out=val, in0=pb[6][:, 0:32], in1=selc[:, 0, qt, :], op=ALU.mult),
                     [("pb", 6), "selc"], ["val"]),
                    (lambda e: e.tensor_tensor(out=val, in0=val, in1=selc[:, 1, qt, :], op=ALU.add),
                     ["val", "selc"], ["val"]),
                    (lambda e: e.max(out=m8[:, 0:8], in_=val), ["val"], ["m8a"]),
                    (lambda e: e.tensor_scalar(out=val2, in0=val, scalar1=m8[:, 7:8], scalar2=-1e9,
                                               op0=ALU.is_ge, op1=ALU.mult), ["val", "m8a"], ["val2"]),
                    (lambda e: e.tensor_tensor(out=val2, in0=val2, in1=val, op=ALU.add), ["val2", "val"], ["val2"]),
                    (lambda e: e.max(out=m8[:, 8:16], in_=val2), ["val2"], ["m8b"]),
                    (lambda e: e.tensor_scalar(out=negsel, in0=val, scalar1=m8[:, 15:16], scalar2=NEGB,
                                               op0=ALU.is_lt, op1=ALU.mult), ["val", "m8b"], ["negsel"]),
                ])

            def post_cmp(qt, ncnt):
                def post(Db, Ob, lastpt):
                    pt_c, rpt = lastpt
                    P.op("dve", lambda e: e.tensor_scalar(out=rd, in0=v4(pb[Db]), scalar1=1e-30, scalar2=None, op0=ALU.max),
                         r=[("pb", Db)], w=["rd"])
                    P.op("dve", lambda e: e.reciprocal(out=rd, in_=rd), r=["rd"], w=["rd"])
                    if qt >= 8:
                        P.op("dve", lambda e: e.tensor_tensor(out=pn[0:ncnt], in0=pt_c[0:ncnt], in1=rd[0:ncnt], op=ALU.mult),
                             r=[rpt, "rd"], w=["pn"])
                    combine(Db, Ob, 0, qt, True)
                    if qt >= 8:
                        push(lambda: emit_imp(qt, ncnt))
                return post

            def post_mid(br, qt):
                def post(Db, Ob, lastpt):
                    combine(Db, Ob, br, qt, False)
                return post

            def post_last(br, qt):
                def post(Db, Ob, lastpt):
                    combine(Db, Ob, br, qt, False)
                    P.op("act", lambda e: e.activation(out=nsa_g[:, :, qt * 128:(qt + 1) * 128], in_=oacc2[qt % 2], func=AF.Copy),
                         r=[("oacc", qt % 2)], w=["nsa_g"])
                return post

            for qt in range(NCH):
                if qt % 2 == 0:
                    qq = qt // 2
                    glq = glq2[qq % 2]
                    rglq = ("glq", qq % 2)
                    src = bass.AP(ng_s.ap().tensor, ng_s.ap().offset + g * 12 * S + qq * 256, [[0, 128], [S, 12], [1, 256]])
                    P.dma("sp", glq, src, r=[("win_out",)], w=[rglq], chan=rglq)
                    P.op("act", lambda e, glq=glq: e.activation(out=glq, in_=glq, func=AF.Sigmoid), r=[rglq], w=[rglq])
                rq = qb[:, :, qt * 128:(qt + 1) * 128]
                ncnt = min(127, 8 * qt + 7)
                c_lo = 129 - 8 * qt
                tiles = [(ncnt, kcT[:, 0:ncnt], [(si2[0:8, c_lo:c_lo + ncnt], pat4[:, 0], ["si2", "pat4"])],
                          vc[0:ncnt, :], ["kcT", "vc"])]
                attend(tiles, rq, post_cmp(qt, ncnt))
                tiles = []
                for kt in range(max(0, qt - 4), qt + 1):
                    masks = []
                    if kt == qt:
                        masks.append((ident_b, tri4[:, 0], ["tri4"]))
                    if kt == qt - 4:
                        masks.append((ident_b, tri4[:, 1], ["tri4"]))
                    tiles.append((128, kwin_b[:, kt * 128:(kt + 1) * 128], masks, vtm[1][:, kt, :], [("kb", 3), ("vtm", 1)]))
                attend(tiles, rq, post_mid(2, qt))
                if qt >= 8:
                    drain_all()
                    tps = pb[7][:].bitcast(BF16)
                    P.op("pe", lambda e, tps=tps: e.transpose(tps[0:32, 0:128], negsel, ident_b),
                         r=["negsel", "ident_b"], w=[("pb", 7)])
                    P.op("dve", lambda e, tps=tps: bcast4(e, nselT4.rearrange("p (o h) q -> p o h q", o=1),
                                                          tps[0:32, 0:128].rearrange("p (o q) -> p o q", o=1), 1),
                         r=[("pb", 7)], w=["nselT4"])
                tiles = []
                for kt in range(qt + 1):
                    masks = []
                    if qt >= 8:
                        masks.append((Eb[0:32, kt * 128:(kt + 1) * 128], nselT4, ["Eb", "nselT4"]))
                    if kt == qt:
                        masks.append((ident_b, tri4[:, 0], ["tri4"]))
                    tiles.append((128, kslc_b[:, kt * 128:(kt + 1) * 128], masks, vtm[0][:, kt, :], [("kb", 2), ("vtm", 0)]))
                attend(tiles, rq, post_last(1, qt))
            drain_all()
            for h in range(4):
                P.dma("sp", nsa_s[g * 4 + h], nsa_g[:, h, :], r=["nsa_g"], w=[("nsa_out",)], chan="nsa_g")
        P.barrier()

    def phase_merge():
        for half in range(2):
            t0 = half * 1024
            ar.reset()
            lru_h = ar.alloc([128, NCH, 1024], BF16)
            nsa_h = ar.alloc([128, NCH, 1024], BF16)
            gab = [[ar.alloc([128, 1024], F32) for _ in range(2)] for _ in range(2)]
            mst = [ar.alloc([128, 1024], BF16) for _ in range(2)]
            t1 = ar.alloc([128, 512], F32)
            t2 = ar.alloc([128, 512], F32)
            P.dma("sp", lru_h, lru_s.rearrange("c p s -> p c s")[:, :, t0:t0 + 1024], r=[("lru_out",)], w=["lru_h"],
                  chan="lru_h")
            P.dma("sp", nsa_h, nsa_s.rearrange("c p s -> p c s")[:, :, t0:t0 + 1024], r=[("nsa_out",)], w=["nsa_h"],
                  chan="nsa_h")
            bctr = 0
            for j in range(8):
                wa_t, wa_res = wload(wk(w_a, j * 256, 256))
                wb_t, wb_res = wload(wk(w_b, j * 256, 256))
                for cc in range(2):
                    dc = j * 2 + cc
                    k = dc % 2
                    P.dma("sp", gab[0][k], mg_s[dc][:, t0:t0 + 1024], r=[("win_out",)], w=[("ga", k)], chan=("ga", k))
                    P.dma("sp", gab[1][k], mg_s[16 + dc][:, t0:t0 + 1024], r=[("win_out",)], w=[("gb", k)], chan=("gb", k))
                    for tt in range(2):
                        ba, bb = (bctr % 4) * 2, (bctr % 4) * 2 + 1
                        bctr += 1
                        for (bank, wt, wres, act_, ares) in ((ba, wa_t, wa_res, lru_h, "lru_h"), (bb, wb_t, wb_res, nsa_h, "nsa_h")):
                            def fn(e, bank=bank, wt=wt, act_=act_, cc=cc, tt=tt):
                                for kc in range(NCH):
                                    ins = e.matmul(pb[bank], lhsT=wt[:, kc, cc * 128:(cc + 1) * 128],
                                                   rhs=act_[:, kc, tt * 512:(tt + 1) * 512], start=(kc == 0), stop=(kc == NCH - 1))
                                return ins
                            P.op("pe", fn, r=[wres, ares], w=[("pb", bank)])

                        P.op("dve", lambda e, ba=ba, k=k, tt=tt: e.tensor_tensor(
                            out=t1, in0=pb[ba], in1=gab[0][k][:, tt * 512:(tt + 1) * 512], op=ALU.mult),
                            r=[("pb", ba), ("ga", k)], w=["t1"])
                        P.op("dve", lambda e, bb=bb, k=k, tt=tt: e.tensor_tensor(
                            out=t2, in0=pb[bb], in1=gab[1][k][:, tt * 512:(tt + 1) * 512], op=ALU.mult),
                            r=[("pb", bb), ("gb", k)], w=["t2"])
                        P.op("dve", lambda e, k=k, tt=tt: e.tensor_tensor(
                            out=mst[k][:, tt * 512:(tt + 1) * 512], in0=t1, in1=t2, op=ALU.add),
                            r=["t1", "t2"], w=[("mst", k)])
                    P.dma("sp", m_s[dc][:, t0:t0 + 1024], mst[k], r=[("mst", k)], w=[("m_out", half)], chan=("mst", k))
            P.barrier()
            ar.reset()
            u = ar.alloc([128, NCH, 1024], BF16)
            acc = ar.alloc([128, NCH, 1024], F32)
            xt2 = [ar.alloc([128, D], F32) for _ in range(2)]
            yt = ar.alloc([128, D], F32)
            vecG = ar.alloc([128, D], F32)
            P.dma("sp", u, m_s.rearrange("c p s -> p c s")[:, :, t0:t0 + 1024], r=[("m_out", half)],
                  w=[("u", 0), ("u", 1)], chan="u_m")
            ev = 0
            for j in range(8):
                wo_t, wo_res = wload(wk(w_o, j * 256, 256))
                for cc in range(2):
                    dc = j * 2 + cc
                    for tt in range(2):
                        bank = bctr % 8
                        bctr += 1

                        def fn(e, bank=bank, wo_t=wo_t, cc=cc, tt=tt):
                            for kc in range(NCH):
                                ins = e.matmul(pb[bank], lhsT=wo_t[:, kc, cc * 128:(cc + 1) * 128],
                                               rhs=u[:, kc, tt * 512:(tt + 1) * 512], start=(kc == 0), stop=(kc == NCH - 1))
                            return ins
                        P.op("pe", fn, r=[wo_res, ("u", tt)], w=[("pb", bank)])
                        a_ = acc[:, dc, tt * 512:(tt + 1) * 512]
                        ev += 1
                        if ev % 2:
                            P.op("act", lambda e, a_=a_, bank=bank: e.activation(out=a_, in_=pb[bank], func=AF.Copy),
                                 r=[("pb", bank)], w=[("acc", dc, tt)])
                        else:
                            P.op("dve", lambda e, a_=a_, bank=bank: e.tensor_copy(out=a_, in_=pb[bank]),
                                 r=[("pb", bank)], w=[("acc", dc, tt)])
            epilogue(acc, x1, "x1", 1, x2, "x2", t0, 1024, (xt2, yt, vecG, "vecG"))
            P.barrier()

    if "ada" in phases:
        phase_ada()
    if "ffn1" in phases:
        ffn_phase(x_in, "x", 0, x1, "x1", w1g, w1u, w1d)
    if "win" in phases:
        phase_win()
    if "nsa" in phases:
        phase_nsa()
    if "merge" in phases:
        phase_merge()
    if "ffn2" in phases:
        ffn_phase(x2, "x2", 2, out, "out", w2g, w2u, w2d)
    P.barrier()
    P.emit(es)
    es.close()
    P.dr = dr
    return nc, P


def _consts():
    c = {}
    pos = np.arange(S, dtype=np.float32)
    inv_freq = (np.float32(500000.0) ** (-np.arange(0, 32, 2, dtype=np.float32) / np.float32(32))).astype(np.float32)
    ang = (pos[:, None] * inv_freq[None, :]).astype(np.float32)
    cos, sin = np.cos(ang).astype(np.float32), np.sin(ang).astype(np.float32)
    rope = np.zeros((32, 2, S), np.float32)
    rope[0:16, 0] = cos.T
    rope[16:32, 0] = cos.T
    rope[0:16, 1] = -sin.T
    rope[16:32, 1] = sin.T
    c["c_rope"] = rope
    c["c_ident"] = np.eye(128, dtype=np.float32)
    k8 = np.arange(8)[:, None]
    c["c_si2"] = (np.arange(256)[None, :] == k8 + 128).astype(np.float32)
    c["c_pat"] = np.where(16 * k8 + 15 <= np.arange(128)[None, :], 0.0, NEGB).astype(np.float32)
    k = np.arange(128)[:, None]
    ql = np.arange(128)[None, :]
    tri = np.zeros((128, 2, 128), np.float32)
    tri[:, 0, :] = np.where(k <= ql, 0.0, NEGB)
    tri[:, 1, :] = np.where(k > ql, 0.0, NEGB)
    c["c_tri"] = tri
    j = np.arange(32)[:, None]
    kk = np.arange(S)[None, :]
    c["c_E"] = (kk // 64 == j).astype(np.float32)
    nn = np.arange(128)[:, None]
    jj = np.arange(32)[None, :]
    ovl = ((16 * nn < 64 * jj + 64) & (16 * nn + 32 > 64 * jj) & (nn < 127)).astype(np.float32)
    c["c_ovl"] = ovl
    tok = np.arange(S)
    cur = tok // 64
    blk = np.arange(32)
    forced = (blk[None, :] == 0) | (blk[None, :] == cur[:, None]) | (blk[None, :] == cur[:, None] - 1)
    causal = (blk[None, :] * 64) <= tok[:, None]
    c01 = (causal & ~forced).astype(np.float32)
    adj = np.where(forced, 10.0 + blk[None, :], np.where(causal, 0.0, -1.0 - 0.01 * blk[None, :])).astype(np.float32)
    sel = np.zeros((128, 2, NCH, 32), np.float32)
    sel[:, 0] = c01.reshape(NCH, 128, 32).transpose(1, 0, 2)
    sel[:, 1] = adj.reshape(NCH, 128, 32).transpose(1, 0, 2)
    c["c_sel"] = sel
    return c


def _fm(v):
    return np.ascontiguousarray(np.asarray(v, np.float32).reshape(NCH, 128).T)


def prep_shared(inp):
    sh = {}
    sh["w_ada"] = np.ascontiguousarray(inp["w_ada"][0])
    sh["b_adaT"] = np.ascontiguousarray(inp["b_ada"][0].reshape(144, 128).T)
    sh["gainsT"] = np.ascontiguousarray(np.stack([_fm(inp[k][0]) for k in (
        "ffn1_pre_g", "ffn1_post_g", "mix_pre_g", "mix_post_g", "ffn2_pre_g", "ffn2_post_g")], axis=1))
    for k in ("ffn1_w_gate", "ffn1_w_up", "ffn1_w_down", "ffn2_w_gate", "ffn2_w_up", "ffn2_w_down",
              "w_in", "w_a_out", "w_b_out", "w_out"):
        sh[k] = np.ascontiguousarray(inp[k][0])
    cw = inp["conv_w"][0]
    sh["lruvT"] = np.ascontiguousarray(np.stack(
        [_fm(cw[0]), _fm(cw[1]), _fm(cw[2]), _fm(cw[3]), _fm(inp["conv_b"][0]), _fm(inp["lru_br"][0]),
         _fm(inp["lru_bi"][0]), _fm(inp["lru_lambda"][0])], axis=1))
    sh["lru_w"] = np.ascontiguousarray(np.stack([inp["lru_wr"][0], inp["lru_wi"][0]], axis=0).transpose(2, 0, 1, 3))
    sh["cmp_pe"] = np.ascontiguousarray(np.stack([inp["cmp_pe_k"][0].T, inp["cmp_pe_v"][0].T], axis=1))
    sh["cmp_w1"] = np.ascontiguousarray(np.stack(
        [inp["cmp_w1_k"][0].reshape(32, 128, 128).transpose(1, 0, 2),
         inp["cmp_w1_v"][0].reshape(32, 128, 128).transpose(1, 0, 2)], axis=1))
    sh["cmp_w2"] = np.ascontiguousarray(np.stack([inp["cmp_w2_k"][0], inp["cmp_w2_v"][0]], axis=1))
    sh.update(_consts())
    return sh


ALL_PHASES = ["ada", "ffn1", "win", "nsa", "merge", "ffn2"]


def kernel(**inp):
    nc, P = build({"phases": ALL_PHASES, "roles": {"x1": "out", "x2": "out"}})
    sh = prep_shared(inp)
    sh = {k: v for k, v in sh.items() if k in P.dr}
    in_maps = []
    for b in range(8):
        m = dict(sh)
        m["x"] = np.ascontiguousarray(inp["x"][b], dtype=np.float32)
        m["cT"] = _fm(inp["c"][b])
        in_maps.append(m)
    res = run_bass_kernel_spmd(nc, in_maps, core_ids=list(range(8)))
    return np.stack([np.asarray(r["out"], np.float32) for r in res.results], axis=0)
```

```python
import numpy as np
from contextlib import ExitStack
import concourse.bass as bass
import concourse.mybir as mybir
from concourse.bass_utils import run_bass_kernel_spmd

F32 = mybir.dt.float32
BF16 = mybir.dt.bfloat16
AF = mybir.ActivationFunctionType
ALU = mybir.AluOpType

D = 2048
S = 2048
DFF = 5632
NCH = 16
INW = 13360
EPS = 1e-6
NEGB = -30000.0
SCALE = 128 ** -0.5
ENGS = ["pe", "act", "dve", "pool", "sp"]


class Op:
    __slots__ = ("i", "eng", "fn", "deps", "dma", "chan", "sig", "sval", "dval", "n")

    def __init__(self, i, eng, fn, deps, dma, chan):
        self.i, self.eng, self.fn, self.deps, self.dma, self.chan = i, eng, fn, deps, dma, chan
        self.sig = False
        self.sval = 0
        self.dval = 0
        self.n = 1


class Prog:
    def __init__(self, nc):
        self.nc = nc
        self.ops = []
        self.lastw = {}
        self.rd_eng = {}
        self.rd_dma = {}
        self.last_on = {}
        self.pending = {e: set() for e in ENGS}
        self.dma_open = []

    def _add(self, eng, fn, r, w, dma=False, chan=None, n=1):
        i = len(self.ops)
        deps = self.pending[eng]
        self.pending[eng] = set()
        for x in r:
            if x in self.lastw:
                deps.add(self.lastw[x])
        for x in w:
            if x in self.lastw:
                deps.add(self.lastw[x])
            deps.update(self.rd_eng.get(x, {}).values())
            deps.update(self.rd_dma.get(x, ()))
        op = Op(i, eng, fn, deps, dma, chan)
        op.n = n
        self.ops.append(op)
        for x in r:
            if dma:
                self.rd_dma.setdefault(x, set()).add(i)
            else:
                self.rd_eng.setdefault(x, {})[eng] = i
        for x in w:
            self.lastw[x] = i
            self.rd_eng[x] = {}
            self.rd_dma[x] = set()
        self.last_on[eng] = i
        if dma:
            self.dma_open.append(i)
        return op

    def op(self, eng, fn, r=(), w=()):
        return self._add(eng, fn, r, w)

    def dma(self, queue, out, in_, r=(), w=(), chan=None):
        outs = out if isinstance(out, (list, tuple)) else [out]
        ins = in_ if isinstance(in_, (list, tuple)) else [in_]
        outs = [o.ap() if hasattr(o, "_ap") else o for o in outs]
        ins = [o.ap() if hasattr(o, "_ap") else o for o in ins]

        def fn(e, outs=outs, ins=ins):
            return [e.dma_start(out=o, in_=i_) for o, i_ in zip(outs, ins)]
        return self._add(queue, fn, r, w, dma=True, chan=chan, n=len(outs))

    def seq(self, eng, steps):
        for fn, r, w in steps:
            self._add(eng, fn, r, w)

    def barrier(self):
        lasts = dict(self.last_on)
        for e in ENGS:
            for e2, i in lasts.items():
                if e2 != e:
                    self.pending[e].add(i)
            self.pending[e].update(self.dma_open)
        self.dma_open = []

    def emit(self, es):
        nc = self.nc
        ops = self.ops
        for op in ops:
            for d in op.deps:
                P = ops[d]
                if (not P.dma) and (P.eng != op.eng or op.eng != "pe"):
                    P.sig = True
        cnt = {e: 0 for e in ENGS}
        chcnt = {}
        for op in ops:
            if op.dma:
                chcnt[op.chan] = chcnt.get(op.chan, 0) + 16 * op.n
                op.dval = chcnt[op.chan]
            elif op.sig:
                cnt[op.eng] += 1
                op.sval = cnt[op.eng]
        esem = {e: es.enter_context(nc.semaphore("s_" + e)) for e in ENGS}
        csem = {}
        for k, ch in enumerate(chcnt):
            csem[ch] = es.enter_context(nc.semaphore("c%d" % k))
        self.stats = dict(n_ops=len(ops), sig=dict(cnt), nchan=len(chcnt))
        block = es.enter_context(nc.Block())
        bname = {"pe": "tensor", "act": "scalar", "dve": "vector", "pool": "gpsimd", "sp": "sync"}
        for eng in ENGS:
            def body(e, eng=eng):
                waited = {}
                for op in ops:
                    if op.eng != eng:
                        continue
                    need = {}
                    for d in op.deps:
                        P = ops[d]
                        if P.dma:
                            key, val = ("c", P.chan), P.dval
                        elif P.eng != eng or eng != "pe":
                            key, val = ("e", P.eng), P.sval
                        else:
                            continue
                        if need.get(key, 0) < val:
                            need[key] = val
                    for key, val in need.items():
                        if waited.get(key, 0) >= val:
                            continue
                        e.wait_ge(csem[key[1]] if key[0] == "c" else esem[key[1]], val)
                        waited[key] = val
                    if op.dma:
                        for inst in op.fn(e):
                            inst.then_inc(csem[op.chan], 16)
                    else:
                        inst = op.fn(e)
                        if op.sig:
                            inst.then_inc(esem[eng], 1)
                if eng == "sp":
                    for ch, v in chcnt.items():
                        if waited.get(("c", ch), 0) < v:
                            e.wait_ge(csem[ch], v)
                    for e2 in ENGS:
                        if e2 != "sp" and cnt[e2] > 0 and waited.get(("e", e2), 0) < cnt[e2]:
                            e.wait_ge(esem[e2], cnt[e2])
            getattr(block, bname[eng])(body)


class Arena:
    def __init__(self, nc, es, nbytes):
        self.t = es.enter_context(nc.sbuf_tensor("arena", [128, nbytes // 2], BF16))
        self.nbytes = nbytes
        self.base = 0
        self.off = 0

    def alloc(self, shape, dt):
        esz = 4 if dt == F32 else 2
        n = 1
        for s_ in shape[1:]:
            n *= s_
        nb = (n * esz + 63) // 64 * 64
        assert self.off + nb <= self.nbytes, ("arena overflow", self.off, nb, self.nbytes)
        a = self.t[:, self.off // 2:self.off // 2 + n * esz // 2]
        self.off += nb
        if dt == F32:
            a = a.bitcast(F32)
        if len(shape) == 3:
            a = a.rearrange("p (a b) -> p a b", a=shape[1])
        elif len(shape) == 4:
            a = a.rearrange("p (a b c) -> p a b c", a=shape[1], b=shape[2])
        if shape[0] != 128:
            a = a[0:shape[0]]
        return a

    def mark(self):
        self.base = self.off

    def reset(self):
        self.off = self.base


def bc_dram(ap_row, nparts=128):
    return bass.AP(ap_row.tensor, ap_row.offset, [[0, nparts]] + [list(x) for x in ap_row.ap[1:]])


def build(cfg):
    phases = cfg["phases"]
    roles = cfg.get("roles", {})
    nc = bass.Bass("TRN2", target_bir_lowering=False)
    es = ExitStack()
    dr = {}

    class LZ:
        def __init__(self, name, shape, dt, kind):
            self.name, self.shape, self.dt, self.kind, self._ap = name, shape, dt, kind, None

        def ap(self):
            if self._ap is None:
                self._ap = nc.dram_tensor(self.name, self.shape, self.dt, kind=self.kind).ap()
                dr[self.name] = self._ap
            return self._ap

        def __getitem__(self, k):
            return self.ap()[k]

        def rearrange(self, *a, **k):
            return self.ap().rearrange(*a, **k)

    def dram(name, shape, dt, kind="Internal"):
        k = roles.get(name, kind)
        k = {"in": "ExternalInput", "out": "ExternalOutput", "internal": "Internal"}.get(k, k)
        return LZ(name, shape, dt, k)

    IN = "ExternalInput"
    x_in = dram("x", [S, D], F32, IN)
    cT = dram("cT", [128, NCH], F32, IN)
    w_ada = dram("w_ada", [D, 9 * D], F32, IN)
    b_adaT = dram("b_adaT", [128, 144], F32, IN)
    gainsT = dram("gainsT", [128, 6, NCH], F32, IN)
    w1g = dram("ffn1_w_gate", [D, DFF], F32, IN)
    w1u = dram("ffn1_w_up", [D, DFF], F32, IN)
    w1d = dram("ffn1_w_down", [DFF, D], F32, IN)
    w2g = dram("ffn2_w_gate", [D, DFF], F32, IN)
    w2u = dram("ffn2_w_up", [D, DFF], F32, IN)
    w2d = dram("ffn2_w_down", [DFF, D], F32, IN)
    w_in = dram("w_in", [D, INW], F32, IN)
    lruvT = dram("lruvT", [128, 8, NCH], F32, IN)
    lru_w = dram("lru_w", [128, 2, NCH, 128], F32, IN)
    cmp_pe = dram("cmp_pe", [128, 2, 32], F32, IN)
    cmp_w1 = dram("cmp_w1", [128, 2, 32, 128], F32, IN)
    cmp_w2 = dram("cmp_w2", [128, 2, 128], F32, IN)
    w_a = dram("w_a_out", [D, D], F32, IN)
    w_b = dram("w_b_out", [D, D], F32, IN)
    w_o = dram("w_out", [D, D], F32, IN)
    c_rope = dram("c_rope", [32, 2, S], F32, IN)
    c_ident = dram("c_ident", [128, 128], F32, IN)
    c_si2 = dram("c_si2", [8, 256], F32, IN)
    c_pat = dram("c_pat", [8, 128], F32, IN)
    c_tri = dram("c_tri", [128, 2, 128], F32, IN)
    c_E = dram("c_E", [32, S], F32, IN)
    c_ovl = dram("c_ovl", [128, 32], F32, IN)
    c_sel = dram("c_sel", [128, 2, NCH, 32], F32, IN)
    out = dram("out", [S, D], F32, "ExternalOutput")
    vec_s = dram("vec_s", [9, D], F32)
    x1 = dram("x1", [S, D], F32)
    x2 = dram("x2", [S, D], F32)
    xa_s = dram("xa_s", [NCH, 128, S], F32)
    ya_s = dram("ya_s", [NCH, 128, S], F32)
    q_s = dram("q_s", [NCH, 128, S], F32)
    kv_s = dram("kv_s", [6, 4, 128, S], F32)
    vt_s = dram("vt_s", [2, S, 512], BF16)
    ng_s = dram("ng_s", [48, S], F32)
    mg_s = dram("mg_s", [32, 128, S], F32)
    lru_s = dram("lru_s", [NCH, 128, S], BF16)
    nsa_s = dram("nsa_s", [NCH, 128, S], BF16)
    m_s = dram("m_s", [NCH, 128, S], BF16)

    P = Prog(nc)
    ar = Arena(nc, es, 211968)
    pb = [es.enter_context(nc.psum_tensor("pb%d" % i, [128, 512], F32))[:] for i in range(8)]

    ident_f = ar.alloc([128, 128], F32)
    ident_b = ar.alloc([128, 128], BF16)
    ones_b = ar.alloc([128, 128], BF16)
    modT = ar.alloc([128, 144], F32)
    gT = ar.alloc([128, 6, NCH], F32)
    comb = ar.alloc([128, 144], F32)
    small = ar.alloc([128, 64], F32)
    NSLOT = 6
    wslot = [ar.alloc([128, 4096], BF16) for _ in range(NSLOT)]
    ar.mark()
    wctr = [0]
    ring = [0]

    def wload(src_ap, slot=None):
        if slot is None:
            k = ring[0] + wctr[0] % (NSLOT - ring[0])
            wctr[0] += 1
        else:
            k = slot
        res = ("w", k)
        n = 1
        for s_ in src_ap.shape[1:]:
            n *= s_
        assert n <= 4096
        v = wslot[k][:, 0:n]
        if len(src_ap.shape) == 3:
            v = v.rearrange("p (a b) -> p a b", a=src_ap.shape[1])
        P.dma("pool", v, src_ap, w=[res], chan=res)
        return v, res

    def wk(w_dram, c0, ncols):
        return w_dram[:, c0:c0 + ncols].rearrange("(kc p) n -> p kc n", p=128)

    P.dma("sp", ident_f, c_ident, w=["ident_f"], chan="ident_f")
    P.op("dve", lambda e: e.tensor_copy(out=ident_b, in_=ident_f), r=["ident_f"], w=["ident_b"])
    P.op("dve", lambda e: e.memset(ones_b, 1.0), w=["ones_b"])
    P.dma("sp", gT, gainsT, w=["gT"], chan="gT")

    def phase_ada():
        ar.reset()
        cact_f = ar.alloc([128, NCH], F32)
        cact = ar.alloc([128, NCH], BF16)
        badT = ar.alloc([128, 144], F32)
        tr_sb = ar.alloc([128, 2, 128], F32)
        P.dma("sp", cact_f, cT, w=["cact_f"], chan="cact_f")
        P.dma("sp", badT, b_adaT, w=["badT"], chan="badT")
        P.op("act", lambda e: e.activation(out=cact, in_=cact_f, func=AF.Silu), r=["cact_f"], w=["cact"])
        mod_ps = pb[0][:, 0:144]
        for j in range(72):
            wt, res = wload(wk(w_ada, j * 256, 256))
            for cc in range(2):
                col = j * 2 + cc

                def fn(e, wt=wt, cc=cc, col=col):
                    for kc in range(NCH):
                        ins = e.matmul(mod_ps[:, col:col + 1], lhsT=wt[:, kc, cc * 128:(cc + 1) * 128],
                                       rhs=cact[:, kc:kc + 1], start=(kc == 0), stop=(kc == NCH - 1))
                    return ins
                P.op("pe", fn, r=[res, "cact"], w=["mod_ps"])
        P.op("dve", lambda e: e.tensor_tensor(out=modT, in0=mod_ps, in1=badT, op=ALU.add),
             r=["mod_ps", "badT"], w=["modT"])
        for i in range(3):
            sh = modT[:, (3 * i) * 16:(3 * i + 1) * 16]
            sc = modT[:, (3 * i + 1) * 16:(3 * i + 2) * 16]
            ga = modT[:, (3 * i + 2) * 16:(3 * i + 3) * 16]
            A_ = comb[:, (3 * i) * 16:(3 * i + 1) * 16]
            B_ = comb[:, (3 * i + 1) * 16:(3 * i + 2) * 16]
            G_ = comb[:, (3 * i + 2) * 16:(3 * i + 3) * 16]
            wgt = 1.0 if i == 1 else 0.5

            def fn(e, sh=sh, sc=sc, ga=ga, A_=A_, B_=B_, G_=G_, i=i, wgt=wgt):
                e.scalar_tensor_tensor(out=A_, in0=sc, scalar=1.0, in1=gT[:, 2 * i, :], op0=ALU.add, op1=ALU.mult)
                e.tensor_copy(out=B_, in_=sh)
                return e.scalar_tensor_tensor(out=G_, in0=ga, scalar=wgt, in1=gT[:, 2 * i + 1, :],
                                              op0=ALU.mult, op1=ALU.mult)
            P.op("dve", fn, r=["modT", "gT"], w=["comb"])
        tr_ps = pb[1]
        for hh in range(2):
            P.op("pe", lambda e, hh=hh: e.transpose(tr_ps[0:72, hh * 128:(hh + 1) * 128],
                                                   comb[:, hh * 72:(hh + 1) * 72], ident_f),
                 r=["comb", "ident_f"], w=[("pb", 1)])
            P.op("dve", lambda e, hh=hh: e.tensor_copy(out=tr_sb[0:72, hh, :],
                                                       in_=tr_ps[0:72, hh * 128:(hh + 1) * 128]),
                 r=[("pb", 1)], w=[("tr_sb", hh)])
            dst = vec_s.rearrange("v (c f) -> (v c) f", f=128)[hh * 72:(hh + 1) * 72, :]
            P.dma("sp", dst, tr_sb[0:72, hh, :], r=[("tr_sb", hh)], w=["vec_s"], chan=("tr_sb", hh))
        P.barrier()

    def load_vec(dst, rdst, col0, repb, rrep):
        src = comb[:, col0:col0 + NCH]
        v = bass.AP(src.tensor, src.offset, [list(src.ap[0]), list(src.ap[1]), [0, 128]])
        rp = repb.rearrange("p (c f) -> p c f", c=NCH)
        P.op("dve", lambda e: e.tensor_copy(out=rp, in_=v), r=["comb"], w=list(rrep))
        for b4 in range(4):
            def fn(e, b4=b4):
                for c4 in range(4):
                    ins = e.transpose(pb[4 + b4][:, c4 * 128:(c4 + 1) * 128], rp[:, b4 * 4 + c4, :], ident_f)
                return ins
            P.op("pe", fn, r=list(rrep) + ["ident_f"], w=[("pb", 4 + b4)])
            P.op("act", lambda e, b4=b4: e.activation(out=dst[:, b4 * 512:(b4 + 1) * 512], in_=pb[4 + b4], func=AF.Copy),
                 r=[("pb", 4 + b4)], w=[rdst])

    def prologue(x_src, xname, vi, u, t0, T, bufs):
        xt2, utok, vecA, vecB, junk, repb, rrep = bufs
        junk = repb
        load_vec(vecA, "vecA", (3 * vi) * NCH, repb, rrep)
        load_vec(vecB, "vecB", (3 * vi + 1) * NCH, repb, rrep)
        nt = T // 128
        for ti in range(nt):
            xt = xt2[ti % 2]
            rx = ("xt", ti % 2)
            tok = t0 + ti * 128
            P.dma("sp", xt, x_src[tok:tok + 128, :], r=[("xd", xname, tok)], w=[rx], chan=rx)
            ssq = small[:, 0:1]
            rstd = small[:, 1:2]
            P.op("dve", lambda e: e.memset(ssq, 0.0), w=["ssq"])
            P.op("act", lambda e, xt=xt: e.activation(out=junk, in_=xt, func=AF.Square, accum_out=ssq),
                 r=[rx, "ssq"], w=list(rrep) + ["ssq"])
            P.op("dve", lambda e: e.tensor_scalar(out=rstd, in0=ssq, scalar1=1.0 / D, scalar2=EPS,
                                                  op0=ALU.mult, op1=ALU.add), r=["ssq"], w=["rstd"])
            P.op("act", lambda e: e.activation(out=rstd, in_=rstd, func=AF.Sqrt), r=["rstd"], w=["rstd"])
            P.op("dve", lambda e: e.reciprocal(out=rstd, in_=rstd), r=["rstd"], w=["rstd"])
            P.op("dve", lambda e, xt=xt: e.scalar_tensor_tensor(out=xt, in0=xt, scalar=rstd, in1=vecA,
                                                                op0=ALU.mult, op1=ALU.mult),
                 r=[rx, "rstd", "vecA"], w=[rx])
            P.op("pool", lambda e, xt=xt: e.tensor_tensor(out=utok, in0=xt, in1=vecB, op=ALU.add),
                 r=[rx, "vecB"], w=["utok"])
            bk = 2 * (ti % 2)
            for hb in range(2):
                pbank = pb[bk + hb][:].bitcast(BF16)

                def fn(e, pbank=pbank, hb=hb):
                    for c8 in range(8):
                        dc = hb * 8 + c8
                        ins = e.transpose(pbank[:, c8 * 128:(c8 + 1) * 128], utok[:, dc * 128:(dc + 1) * 128], ident_b)
                    return ins
                P.op("pe", fn, r=["utok", "ident_b"], w=[("pb", bk + hb)])
                P.op("act", lambda e, pbank=pbank, hb=hb, ti=ti: e.activation(
                    out=u[:, hb * 8:(hb + 1) * 8, ti * 128:(ti + 1) * 128],
                    in_=pbank.rearrange("p (c t) -> p c t", c=8), func=AF.Copy),
                    r=[("pb", bk + hb)], w=[("u", ti // 4)])

    def epilogue(acc, x_src, xname, vi, x_dst, dname, t0, T, bufs):
        xt2, yt, vecG, rG = bufs
        load_vec(vecG, rG, (3 * vi + 2) * NCH, yt, [("yt", b) for b in range(4)])
        nt = T // 128
        for ti in range(nt):
            xt = xt2[ti % 2]
            rx = ("xt", ti % 2)
            tok = t0 + ti * 128
            P.dma("sp", xt, x_src[tok:tok + 128, :], r=[("xd", xname, tok)], w=[rx], chan=rx)
            bk = 4 * (ti % 2)
            P.op("dve", lambda e: e.memset(small[:, 8:40], 0.0), r=[("ssq4", b) for b in range(4)], w=["ssq4s"])
            ssq = small[:, 0:1]
            rstd = small[:, 1:2]
            for b4 in range(4):
                def fn(e, b4=b4, bk=bk, ti=ti):
                    for c4 in range(4):
                        dc = b4 * 4 + c4
                        ins = e.transpose(pb[bk + b4][:, c4 * 128:(c4 + 1) * 128],
                                          acc[:, dc, ti * 128:(ti + 1) * 128], ident_f)
                    return ins
                P.op("pe", fn, r=[("acc", dc_, ti // 4) for dc_ in range(b4 * 4, b4 * 4 + 4)] + ["ident_f"],
                     w=[("pb", bk + b4)])
                P.op("act", lambda e, b4=b4, bk=bk: e.activation(out=yt[:, b4 * 512:(b4 + 1) * 512], in_=pb[bk + b4],
                                                                func=AF.Square, accum_out=small[:, 8 + 8 * b4:9 + 8 * b4]),
                     r=[("pb", bk + b4), "ssq4s"], w=[("yt", b4), ("ssq4", b4)])
            P.op("dve", lambda e: e.tensor_reduce(out=ssq, in_=small[:, 8:40:8], axis=mybir.AxisListType.X, op=ALU.add),
                 r=[("ssq4", b) for b in range(4)], w=["ssq"])
            P.op("dve", lambda e: e.tensor_scalar(out=rstd, in0=ssq, scalar1=1.0 / D, scalar2=EPS,
                                                  op0=ALU.mult, op1=ALU.add), r=["ssq"], w=["rstd"])
            P.op("act", lambda e: e.activation(out=rstd, in_=rstd, func=AF.Sqrt), r=["rstd"], w=["rstd"])
            P.op("dve", lambda e: e.reciprocal(out=rstd, in_=rstd), r=["rstd"], w=["rstd"])
            for b4 in range(4):
                P.op("dve", lambda e, b4=b4, bk=bk: e.scalar_tensor_tensor(
                    out=yt[:, b4 * 512:(b4 + 1) * 512], in0=pb[bk + b4], scalar=rstd,
                    in1=vecG[:, b4 * 512:(b4 + 1) * 512], op0=ALU.mult, op1=ALU.mult),
                    r=[("pb", bk + b4), "rstd", rG], w=[("yt", b4)])
            P.op("pool", lambda e, xt=xt: e.tensor_tensor(out=xt, in0=xt, in1=yt, op=ALU.add),
                 r=[rx] + [("yt", b) for b in range(4)], w=[rx])
            P.dma("sp", x_dst[tok:tok + 128, :], xt, r=[rx], w=[("xd", dname, tok)], chan=rx)

    def ffn_half(u, acc, hbuf, sbuf, wg, wu, wd):
        NG = DFF // 256
        TT = 2

        def loads_gu(g):
            a = wload(wk(wg, g * 256, 256), slot=(g % 2) * 2)
            b = wload(wk(wu, g * 256, 256), slot=(g % 2) * 2 + 1)
            return a, b

        def loads_d(g):
            return wload(wd[g * 256:(g + 1) * 256, :].rearrange("(fc p) n -> p fc n", p=128), slot=4 + g % 2)
        pgu = {0: loads_gu(0), 1: loads_gu(1)}
        pd = {0: loads_d(0)}
        down_q = []

        def emit_down(g, wdt, wdres, hb, hres):
            items = []
            for dc in range(NCH):
                for tt in range(TT):
                    items.append((g, dc, tt, wdt, wdres, hb, hres))
            return items

        yctr = [0]

        def do_down(item):
            g, dc, tt, wdt, wdres, hb, hres = item
            bank = 4 + (yctr[0] % 4)
            yctr[0] += 1
            yp = pb[bank]

            def fn(e):
                for fc in range(2):
                    ins = e.matmul(yp, lhsT=wdt[:, fc, dc * 128:(dc + 1) * 128], rhs=hb[:, fc, tt * 512:(tt + 1) * 512],
                                   start=(fc == 0), stop=(fc == 1))
                return ins
            P.op("pe", fn, r=[wdres, hres], w=[("pb", bank)])
            a_ = acc[:, dc, tt * 512:(tt + 1) * 512]
            ra = ("acc", dc, tt)
            if g == 0:
                P.op("act", lambda e: e.activation(out=a_, in_=yp, func=AF.Copy), r=[("pb", bank)], w=[ra])
            else:
                P.op("dve", lambda e: e.tensor_tensor(out=a_, in0=yp, in1=a_, op=ALU.add), r=[("pb", bank), ra], w=[ra])

        for g in range(NG):
            (wgt, wgres), (wut, wures) = pgu.pop(g)
            wdt, wdres = pd.pop(g)
            hb = hbuf[g % 2]
            hres = ("h", g % 2)
            k = 0
            for fc in range(2):
                for tt in range(TT):
                    gp = pb[(k % 2) * 2]
                    up = pb[(k % 2) * 2 + 1]
                    rg = ("pb", (k % 2) * 2)
                    ru = ("pb", (k % 2) * 2 + 1)

                    def fng(e, gp=gp, wgt=wgt, fc=fc, tt=tt):
                        for kc in range(NCH):
                            ins = e.matmul(gp, lhsT=wgt[:, kc, fc * 128:(fc + 1) * 128],
                                           rhs=u[:, kc, tt * 512:(tt + 1) * 512], start=(kc == 0), stop=(kc == NCH - 1))
                        return ins

                    def fnu(e, up=up, wut=wut, fc=fc, tt=tt):
                        for kc in range(NCH):
                            ins = e.matmul(up, lhsT=wut[:, kc, fc * 128:(fc + 1) * 128],
                                           rhs=u[:, kc, tt * 512:(tt + 1) * 512], start=(kc == 0), stop=(kc == NCH - 1))
                        return ins
                    P.op("pe", fng, r=[wgres, ("u", tt)], w=[rg])
                    P.op("pe", fnu, r=[wures, ("u", tt)], w=[ru])
                    sb_ = sbuf[k % 2]
                    rs = ("s", k % 2)
                    P.op("act", lambda e, gp=gp, sb_=sb_: e.activation(out=sb_, in_=gp, func=AF.Silu), r=[rg], w=[rs])
                    P.op("dve", lambda e, up=up, sb_=sb_, hb=hb, fc=fc, tt=tt: e.tensor_tensor(
                        out=hb[:, fc, tt * 512:(tt + 1) * 512], in0=up, in1=sb_, op=ALU.mult),
                        r=[ru, rs], w=[hres])
                    k += 1
                    for _ in range(8):
                        if down_q:
                            do_down(down_q.pop(0))
            while down_q:
                do_down(down_q.pop(0))
            down_q = emit_down(g, wdt, wdres, hb, hres)
            if g + 2 < NG:
                pgu[g + 2] = loads_gu(g + 2)
            if g + 1 < NG:
                pd[g + 1] = loads_d(g + 1)
        while down_q:
            do_down(down_q.pop(0))

    def ffn_phase(x_src, xname, vi, x_dst, dname, wg, wu, wd):
        ar.reset()
        u = ar.alloc([128, NCH, 1024], BF16)
        acc = ar.alloc([128, NCH, 1024], F32)
        hbuf = [ar.alloc([128, 2, 1024], BF16) for _ in range(2)]
        sbuf = [ar.alloc([128, 512], F32) for _ in range(2)]
        xt2 = [ar.alloc([128, D], F32) for _ in range(2)]
        yt = ar.alloc([128, D], F32)
        utok = ar.alloc([128, D], BF16)
        vecA = ar.alloc([128, D], F32)
        vecB = ar.alloc([128, D], F32)
        junk = None
        for half in range(2):
            t0 = half * 1024
            prologue(x_src, xname, vi, u, t0, 1024, (xt2, utok, vecA, vecB, junk, yt, [("yt", b) for b in range(4)]))
            ffn_half(u, acc, hbuf, sbuf, wg, wu, wd)
            epilogue(acc, x_src, xname, vi, x_dst, dname, t0, 1024, (xt2, yt, vecA, "vecA"))
        P.barrier()

    def phase_win():
        ar.reset()
        u2 = ar.alloc([128, NCH, S], BF16)
        off_u2 = ar.off
        xt2 = [ar.alloc([128, D], F32) for _ in range(2)]
        utok = ar.alloc([128, D], BF16)
        vecA = ar.alloc([128, D], F32)
        vecB = ar.alloc([128, D], F32)
        junk = None
        repw = ar.alloc([128, D], F32)
        prologue(x1, "x1", 1, u2, 0, S, (xt2, utok, vecA, vecB, junk, repw, ["repw"]))
        P.barrier()
        ar.off = off_u2
        stg = [ar.alloc([128, S], F32) for _ in range(3)]
        vst = [ar.alloc([128, NCH, 256], BF16) for _ in range(2)]
        ctr = {"bank": 0, "stg": 0, "ev": 0, "vst": 0}
        ring[0] = 2
        lru_q = lru_setup(ctr)

        def drain(n):
            for _ in range(n):
                if lru_q["pending"]:
                    lru_q["pending"].pop(0)()

        def fm_tile(c0, dsts, sig=False, ncols=256, names=None):
            wt, res = wload(wk(w_in, c0, ncols))
            nchunk = (ncols + 127) // 128
            for cc in range(nchunk):
                m = min(128, ncols - cc * 128)
                k = ctr["stg"] % 3
                ctr["stg"] += 1
                st = stg[k]
                for tt in range(4):
                    bank = ctr["bank"] % 8
                    ctr["bank"] += 1
                    ps = pb[bank]

                    def fn(e, ps=ps, wt=wt, cc=cc, m=m, tt=tt):
                        for kc in range(NCH):
                            ins = e.matmul(ps[0:m, :], lhsT=wt[:, kc, cc * 128:cc * 128 + m],
                                           rhs=u2[:, kc, tt * 512:(tt + 1) * 512], start=(kc == 0), stop=(kc == NCH - 1))
                        return ins
                    P.op("pe", fn, r=[res, ("u", tt)], w=[("pb", bank)])
                    o_ = st[0:m, tt * 512:(tt + 1) * 512]
                    if sig:
                        P.op("act", lambda e, o_=o_, ps=ps, m=m: e.activation(out=o_, in_=ps[0:m, :], func=AF.Sigmoid),
                             r=[("pb", bank)], w=[("stg", k)])
                    else:
                        ctr["ev"] += 1
                        if ctr["ev"] % 2:
                            P.op("act", lambda e, o_=o_, ps=ps, m=m: e.activation(out=o_, in_=ps[0:m, :], func=AF.Copy),
                                 r=[("pb", bank)], w=[("stg", k)])
                        else:
                            P.op("dve", lambda e, o_=o_, ps=ps, m=m: e.tensor_copy(out=o_, in_=ps[0:m, :]),
                                 r=[("pb", bank)], w=[("stg", k)])
                    drain(2)
                P.dma("sp", dsts[cc], st[0:m, :], r=[("stg", k)], w=[names[cc] if names else ("win_out", c0, cc)],
                      chan=("stg", k))

        def tm_tile(c0, which, hcol):
            wt, res = wload(wk(w_in, c0, 256))
            k = ctr["vst"] % 2
            ctr["vst"] += 1
            vs = vst[k]
            for ti in range(NCH):
                bank = ctr["bank"] % 8
                ctr["bank"] += 1
                ps = pb[bank]

                def fn(e, ps=ps, wt=wt, ti=ti):
                    for kc in range(NCH):
                        ins = e.matmul(ps[:, 0:256], lhsT=u2[:, kc, ti * 128:(ti + 1) * 128], rhs=wt[:, kc, :],
                                       start=(kc == 0), stop=(kc == NCH - 1))
                    return ins
                P.op("pe", fn, r=[res, ("u", ti // 4)], w=[("pb", bank)])
                ctr["ev"] += 1
                if ctr["ev"] % 2:
                    P.op("act", lambda e, ps=ps, vs=vs, ti=ti: e.activation(out=vs[:, ti, :], in_=ps[:, 0:256], func=AF.Copy),
                         r=[("pb", bank)], w=[("vst", k)])
                else:
                    P.op("dve", lambda e, ps=ps, vs=vs, ti=ti: e.tensor_copy(out=vs[:, ti, :], in_=ps[:, 0:256]),
                         r=[("pb", bank)], w=[("vst", k)])
                if ti % 2:
                    drain(2)
            dst = vt_s[which].rearrange("(ti p) c -> p ti c", p=128)[:, :, hcol * 256:(hcol + 1) * 256]
            P.dma("sp", dst, vs, r=[("vst", k)], w=[("vt_out", which, hcol)], chan=("vst", k))

        for j in range(8):
            fm_tile(j * 256, [xa_s[2 * j], xa_s[2 * j + 1]], names=[("xa_s", 2 * j), ("xa_s", 2 * j + 1)])
            fm_tile(2048 + j * 256, [ya_s[2 * j], ya_s[2 * j + 1]], names=[("ya_s", 2 * j), ("ya_s", 2 * j + 1)])
            lru_q["pending"] += lru_chunk_ops(lru_q, 2 * j)
            lru_q["pending"] += lru_chunk_ops(lru_q, 2 * j + 1)
        for j in range(8):
            fm_tile(4096 + j * 256, [q_s[2 * j], q_s[2 * j + 1]])
        for i in range(6):
            for hcol in range(2):
                c0 = 6144 + i * 512 + hcol * 256
                if i == 3:
                    tm_tile(c0, 0, hcol)
                elif i == 5:
                    tm_tile(c0, 1, hcol)
                else:
                    fm_tile(c0, [kv_s[i, 2 * hcol], kv_s[i, 2 * hcol + 1]])
        fm_tile(9216, [ng_s], ncols=48)
        for j in range(16):
            fm_tile(9264 + j * 256, [mg_s[2 * j], mg_s[2 * j + 1]], sig=True)
        drain(10 ** 6)
        ring[0] = 0
        P.barrier()

    def lru_setup(ctr):
        lv = ar.alloc([128, 8, NCH], F32)
        cvec = ar.alloc([128, NCH], F32)
        c2vec = ar.alloc([128, NCH], F32)
        onef = ar.alloc([128, 1], F32)
        Lq = dict(lv=lv, cvec=cvec, c2vec=c2vec, onef=onef, ctr=ctr, pending=[])
        for nm in ("xa", "ya", "xc", "rr", "ii"):
            Lq[nm] = ar.alloc([128, S], F32)
        Lq["xcb"] = ar.alloc([128, S], BF16)
        P.dma("sp", lv, lruvT, w=["lv"], chan="lv")
        P.op("dve", lambda e: e.memset(onef, 1.0), w=["onef"])
        Lq["wr"] = wload(lru_w[:, 0], slot=0)
        Lq["wi"] = wload(lru_w[:, 1], slot=1)
        P.op("act", lambda e: e.activation(out=cvec, in_=lv[:, 7, :], func=AF.Exp, scale=-1.0), r=["lv"], w=["cvec"])
        P.op("dve", lambda e: e.tensor_scalar(out=cvec, in0=cvec, scalar1=1.0, scalar2=None, op0=ALU.add),
             r=["cvec"], w=["cvec"])
        P.op("act", lambda e: e.activation(out=cvec, in_=cvec, func=AF.Ln), r=["cvec"], w=["cvec"])
        P.op("dve", lambda e: e.tensor_scalar(out=c2vec, in0=cvec, scalar1=-16.0, scalar2=None, op0=ALU.mult),
             r=["cvec"], w=["c2vec"])
        P.op("dve", lambda e: e.tensor_scalar(out=cvec, in0=cvec, scalar1=-8.0, scalar2=None, op0=ALU.mult),
             r=["cvec"], w=["cvec"])
        return Lq

    def lru_chunk_ops(Lq, c):
        lv, cvec, c2vec, onef, ctr = Lq["lv"], Lq["cvec"], Lq["c2vec"], Lq["onef"], Lq["ctr"]
        xa, ya, xc, rr, ii, xcb = Lq["xa"], Lq["ya"], Lq["xc"], Lq["rr"], Lq["ii"], Lq["xcb"]
        (wr_t, wr_res), (wi_t, wi_res) = Lq["wr"], Lq["wi"]
        L = []
        L.append(lambda: P.dma("sp", xa, xa_s[c], r=[("xa_s", c)], w=["l_xa"], chan="l_xa"))
        L.append(lambda: P.dma("sp", ya, ya_s[c], r=[("ya_s", c)], w=["l_ya"], chan="l_ya"))
        L.append(lambda: P.op("dve", lambda e: e.tensor_scalar(out=xc, in0=xa, scalar1=lv[:, 3, c:c + 1],
                                                              scalar2=lv[:, 4, c:c + 1], op0=ALU.mult, op1=ALU.add),
                              r=["l_xa", "lv"], w=["l_xc"]))
        for sft in (1, 2, 3):
            L.append(lambda sft=sft: P.op("dve", lambda e: e.scalar_tensor_tensor(
                out=xc[:, sft:S], in0=xa[:, 0:S - sft], scalar=lv[:, 3 - sft, c:c + 1], in1=xc[:, sft:S],
                op0=ALU.mult, op1=ALU.add), r=["l_xa", "lv", "l_xc"], w=["l_xc"]))
        L.append(lambda: P.op("act", lambda e: e.activation(out=xcb, in_=xc, func=AF.Copy), r=["l_xc"], w=["l_xcb"]))
        for gi, (wt, wres, dst, bcol, rdst) in enumerate(((wr_t, wr_res, rr, 5, "l_rr"), (wi_t, wi_res, ii, 6, "l_ii"))):
            for tt in range(4):
                def mm(wt=wt, wres=wres, dst=dst, bcol=bcol, rdst=rdst, tt=tt):
                    bank = ctr["bank"] % 8
                    ctr["bank"] += 1
                    ps = pb[bank]
                    P.op("pe", lambda e: e.matmul(ps, lhsT=wt[:, c, :], rhs=xcb[:, tt * 512:(tt + 1) * 512],
                                                  start=True, stop=True), r=[wres, "l_xcb"], w=[("pb", bank)])
                    P.op("act", lambda e: e.activation(out=dst[:, tt * 512:(tt + 1) * 512], in_=ps, func=AF.Sigmoid,
                                                       bias=lv[:, bcol, c:c + 1]), r=[("pb", bank), "lv"], w=[rdst])
                L.append(mm)
        L.append(lambda: P.op("act", lambda e: e.activation(out=xa, in_=rr, func=AF.Exp, scale=cvec[:, c:c + 1]),
                              r=["l_rr", "cvec", "l_xc"], w=["l_xa"]))
        L.append(lambda: P.op("act", lambda e: e.activation(out=rr, in_=rr, func=AF.Exp, scale=c2vec[:, c:c + 1]),
                              r=["l_rr", "c2vec"], w=["l_rr"]))
        L.append(lambda: P.op("act", lambda e: e.activation(out=rr, in_=rr, func=AF.Sqrt, scale=-1.0, bias=onef[:, 0:1]),
                              r=["l_rr", "onef"], w=["l_rr"]))
        L.append(lambda: P.op("act", lambda e: e.activation(out=ya, in_=ya, func=AF.Gelu_apprx_tanh), r=["l_ya"], w=["l_ya"]))
        L.append(lambda: P.op("dve", lambda e: e.tensor_tensor(out=ii, in0=ii, in1=xc, op=ALU.mult),
                              r=["l_ii", "l_xc"], w=["l_ii"]))
        L.append(lambda: P.op("dve", lambda e: e.tensor_tensor(out=rr, in0=rr, in1=ii, op=ALU.mult),
                              r=["l_rr", "l_ii"], w=["l_rr"]))
        L.append(lambda: P.op("dve", lambda e: e.tensor_tensor_scan(out=xc, data0=xa, data1=rr, initial=0.0,
                                                                   op0=ALU.mult, op1=ALU.add),
                              r=["l_xa", "l_rr", "l_ii"], w=["l_xc"]))
        L.append(lambda: P.op("dve", lambda e: e.tensor_tensor(out=xcb, in0=xc, in1=ya, op=ALU.mult),
                              r=["l_xc", "l_ya"], w=["l_xcb"]))
        L.append(lambda: P.dma("sp", lru_s[c], xcb, r=["l_xcb"], w=[("lru_out",)], chan="l_xcb"))
        return L

    def phase_nsa():
        ar.reset()
        rope = ar.alloc([32, 2, S], F32)
        si2 = ar.alloc([8, 256], BF16)
        pat_f = ar.alloc([8, 1, 128], F32)
        pat4 = ar.alloc([8, 1, 4, 128], BF16)
        tri_f = ar.alloc([128, 2, 128], F32)
        tri4 = ar.alloc([128, 2, 4, 128], BF16)
        Eb = ar.alloc([32, S], BF16)
        ovl = ar.alloc([128, 32], F32)
        selc = ar.alloc([128, 2, NCH, 32], F32)
        w2b = ar.alloc([128, 2, 128], BF16)
        peb = ar.alloc([128, 2, 32], BF16)
        bias_kv = ar.alloc([128, 2], F32)
        P.dma("sp", rope, c_rope, w=["rope"], chan="rope")
        P.dma("pool", si2, c_si2, w=["si2"], chan="si2")
        P.dma("sp", pat_f, c_pat.rearrange("k (o q) -> k o q", o=1), w=["pat_f"], chan="pat_f")
        P.dma("sp", tri_f, c_tri, w=["tri_f"], chan="tri_f")
        P.dma("pool", Eb, c_E, w=["Eb"], chan="Eb")
        P.dma("sp", ovl, c_ovl, w=["ovl"], chan="ovl")
        P.dma("sp", selc, c_sel, w=["selc"], chan="selc")
        P.dma("pool", w2b, cmp_w2, w=["w2b"], chan="w2b")
        P.dma("pool", peb, cmp_pe, w=["peb"], chan="peb")
        w1k, w1k_res = wload(cmp_w1[:, 0], slot=0)
        w1v, w1v_res = wload(cmp_w1[:, 1], slot=1)
        w1 = (w1k, w1v)
        w1res = (w1k_res, w1v_res)

        def bcast4(e, dst, src, n_outer):
            s_ = src
            v = bass.AP(s_.tensor, s_.offset, [list(s_.ap[0]), list(s_.ap[1]), [0, 4], list(s_.ap[-1])])
            return e.tensor_copy(out=dst, in_=v)
        P.op("dve", lambda e: bcast4(e, pat4, pat_f, 1), r=["pat_f"], w=["pat4"])
        P.op("dve", lambda e: bcast4(e, tri4, tri_f, 2), r=["tri_f"], w=["tri4"])
        for kv in range(2):
            def fn(e, kv=kv):
                for l in range(32):
                    ins = e.matmul(pb[6][:, kv:kv + 1], lhsT=w1[kv][:, l, :], rhs=peb[:, kv, l:l + 1],
                                   start=(l == 0), stop=(l == 31))
                return ins
            P.op("pe", fn, r=[w1res[kv], "peb"], w=[("pb", 6)])
        P.op("dve", lambda e: e.tensor_copy(out=bias_kv, in_=pb[6][:, 0:2]), r=[("pb", 6)], w=["bias_kv"])

        tf2 = [ar.alloc([128, S], F32) for _ in range(2)]
        kb = [ar.alloc([128, S], BF16) for _ in range(4)]
        tp = ar.alloc([32, S], F32)
        qb = ar.alloc([128, 4, S], BF16)
        vtm = [ar.alloc([128, NCH, 128], BF16) for _ in range(2)]
        glq2 = [ar.alloc([128, 12, 256], F32) for _ in range(2)]
        nsa_g = ar.alloc([128, 4, S], BF16)
        NPT = 6
        ptr = [ar.alloc([128, 4, 128], BF16) for _ in range(NPT)]
        hcmp = ar.alloc([128, 2, 128], BF16)
        kcT = ar.alloc([128, 128], BF16)
        vc = ar.alloc([128, 128], BF16)
        pn = ar.alloc([128, 4, 128], F32)
        rd = ar.alloc([128, 4, 128], F32)
        wg_ = ar.alloc([128, 4, 128], F32)
        oacc2 = [ar.alloc([128, 4, 128], F32) for _ in range(2)]
        tmpb = ar.alloc([128, 4, 128], F32)
        val = ar.alloc([128, 32], F32)
        val2 = ar.alloc([128, 32], F32)
        m8 = ar.alloc([128, 16], F32)
        negsel = ar.alloc([128, 32], BF16)
        nselT4 = ar.alloc([32, 4, 128], BF16)
        print("nsa arena used", ar.off, "of", ar.nbytes)

        def v4(ap512):
            return ap512.rearrange("p (h q) -> p h q", h=4)

        def rope_inplace(tf, rtf, src):
            P.dma("sp", [tp[0:16], tp[16:32]], [src[16:32, :], src[0:16, :]], r=[("win_out",)], w=["tp"], chan="tp")

            P.op("pool", lambda e: e.tensor_tensor(out=tp, in0=tp, in1=rope[:, 1, :], op=ALU.mult),
                 r=["tp", "rope"], w=["tp"])
            P.op("pool", lambda e, tf=tf: e.tensor_tensor(out=tf[0:32, :], in0=tf[0:32, :], in1=rope[:, 0, :], op=ALU.mult),
                 r=[rtf, "rope"], w=[rtf])
            P.op("pool", lambda e, tf=tf: e.tensor_tensor(out=tf[0:32, :], in0=tf[0:32, :], in1=tp, op=ALU.add),
                 r=[rtf, "tp"], w=[rtf])

        brc = [0]
        tfc = [0]
        sctr = [0]
        pctr = [0]

        for g in range(4):
            srcs = [kv_s[0, g], kv_s[1, g], kv_s[2, g], kv_s[4, g]]
            for i4 in range(4):
                tfb = tf2[tfc[0] % 2]
                rtf = ("tf", tfc[0] % 2)
                tfc[0] += 1
                P.dma("sp", tfb, srcs[i4], r=[("win_out",)], w=[rtf], chan=rtf)
                if i4 != 1:
                    rope_inplace(tfb, rtf, srcs[i4])
                P.op("act", lambda e, i4=i4, tfb=tfb: e.activation(out=kb[i4], in_=tfb, func=AF.Copy),
                     r=[rtf], w=[("kb", i4)])
            for w_ in range(2):
                src = vt_s[w_].rearrange("(ti p) c -> p ti c", p=128)[:, :, g * 128:(g + 1) * 128]
                P.dma("sp", vtm[w_], src, r=[("vt_out",)], w=[("vtm", w_)], chan=("vtm", w_))
            for kv in range(2):
                src_b = kb[kv]
                hps = pb[6]

                def fn(e, kv=kv, src_b=src_b, hps=hps):
                    for l in range(32):
                        rhs = bass.AP(src_b.tensor, src_b.offset + l, [list(src_b.ap[0]), [16, 127]])
                        ins = e.matmul(hps[:, 0:127], lhsT=w1[kv][:, l, :], rhs=rhs,
                                       start=(l == 0), stop=(l == 31))
                    return ins
                P.op("pe", fn, r=[w1res[kv], ("kb", kv)], w=[("pb", 6)])
                P.op("act", lambda e, kv=kv, hps=hps: e.activation(out=hcmp[:, kv, 0:127], in_=hps[:, 0:127],
                                                                  func=AF.Gelu_apprx_tanh, bias=bias_kv[:, kv:kv + 1]),
                     r=[("pb", 6), "bias_kv"], w=[("hcmp", kv)])
                ops_ = pb[7]
                if kv == 0:
                    P.op("pe", lambda e, ops_=ops_: e.matmul(ops_[:, 0:127], lhsT=w2b[:, 0, :], rhs=hcmp[:, 0, 0:127],
                                                             start=True, stop=True),
                         r=["w2b", ("hcmp", 0)], w=[("pb", 7)])
                    P.op("dve", lambda e, ops_=ops_: e.tensor_copy(out=kcT[:, 0:127], in_=ops_[:, 0:127]),
                         r=[("pb", 7)], w=["kcT"])
                else:
                    P.op("pe", lambda e, ops_=ops_: e.matmul(ops_[0:127, 0:128], lhsT=hcmp[:, 1, 0:127], rhs=w2b[:, 1, :],
                                                             start=True, stop=True),
                         r=["w2b", ("hcmp", 1)], w=[("pb", 7)])
                    P.op("dve", lambda e, ops_=ops_: e.tensor_copy(out=vc[0:127, :], in_=ops_[0:127, 0:128]),
                         r=[("pb", 7)], w=["vc"])
            for h in range(4):
                qf = tf2[tfc[0] % 2]
                rq_ = ("tf", tfc[0] % 2)
                tfc[0] += 1
                P.dma("sp", qf, q_s[g * 4 + h], r=[("win_out",)], w=[rq_], chan=rq_)
                rope_inplace(qf, rq_, q_s[g * 4 + h])
                P.op("act", lambda e, qf=qf, h=h: e.activation(out=qb[:, h, :], in_=qf, func=AF.Copy),
                     r=[rq_], w=[("qb", h)])
            kslc_b, kwin_b = kb[2], kb[3]

            SB = [0, 1, 6, 7]
            DEPTH = 3
            fifo = []

            def push(item):
                fifo.append(item)
                while len(fifo) > DEPTH:
                    fifo.pop(0)()

            def drain_all():
                while fifo:
                    fifo.pop(0)()

            def attend(tiles, rq, post):
                pair = brc[0] % 2
                brc[0] += 1
                Db, Ob = 2 + 2 * pair, 3 + 2 * pair
                Dp, Op_ = pb[Db], pb[Ob]
                n = len(tiles)
                for ti in range(n):
                    nk, lk, masks, v_ap, rl = tiles[ti]
                    sb_ = SB[sctr[0] % 4]
                    sctr[0] += 1
                    Sp = pb[sb_]

                    def fn(e, Sp=Sp, nk=nk, lk=lk, masks=masks):
                        ins = e.matmul(v4(Sp)[0:nk], lhsT=lk, rhs=rq, start=True, stop=(len(masks) == 0))
                        for mi, (ml, mr, _) in enumerate(masks):
                            ins = e.matmul(v4(Sp)[0:nk], lhsT=ml, rhs=mr, start=False, stop=(mi == len(masks) - 1))
                        return ins
                    rl2 = list(rl)
                    for (_, _, mrl) in masks:
                        rl2 += mrl
                    P.op("pe", fn, r=rl2 + [("qb", hh_) for hh_ in range(4)] + ["ident_b"], w=[("pb", sb_)])
                    pk = pctr[0] % NPT
                    pctr[0] += 1
                    pt = ptr[pk]
                    P.op("act", lambda e, Sp=Sp, pt=pt, nk=nk: e.activation(out=pt[0:nk], in_=v4(Sp)[0:nk], func=AF.Exp,
                                                                            scale=SCALE),
                         r=[("pb", sb_)], w=[("pt", pk)])

                    def pv(ti=ti, pt=pt, pk=pk, nk=nk, v_ap=v_ap, rl=rl):
                        def fnp(e):
                            e.matmul(v4(Dp), lhsT=ones_b[0:nk, :], rhs=pt[0:nk], start=(ti == 0), stop=(ti == n - 1))
                            return e.matmul(v4(Op_), lhsT=v_ap, rhs=pt[0:nk], start=(ti == 0), stop=(ti == n - 1))
                        P.op("pe", fnp, r=[("pt", pk), "ones_b"] + list(rl), w=[("pb", Db), ("pb", Ob)])
                        if ti == n - 1:
                            post(Db, Ob, (pt, ("pt", pk)))
                    push(pv)

            def combine(Db, Ob, br, qt, first):
                glq = glq2[(qt // 2) % 2]
                rglq = ("glq", (qt // 2) % 2)
                oacc = oacc2[qt % 2]
                roa = ("oacc", qt % 2)
                gsl = glq[:, br:12:3, (qt % 2) * 128:(qt % 2) * 128 + 128]
                if not first:
                    P.op("dve", lambda e: e.reciprocal(out=rd, in_=v4(pb[Db])), r=[("pb", Db)], w=["rd"])
                    P.op("dve", lambda e: e.tensor_tensor(out=wg_, in0=rd, in1=gsl, op=ALU.mult), r=["rd", rglq], w=["wg"])
                else:
                    P.op("dve", lambda e: e.tensor_tensor(out=wg_, in0=rd, in1=gsl, op=ALU.mult), r=["rd", rglq], w=["wg"])
                if first:
                    P.op("dve", lambda e: e.tensor_tensor(out=oacc, in0=v4(pb[Ob]), in1=wg_, op=ALU.mult),
                         r=[("pb", Ob), "wg"], w=[roa])
                else:
                    P.op("dve", lambda e: e.tensor_tensor(out=tmpb, in0=v4(pb[Ob]), in1=wg_, op=ALU.mult),
                         r=[("pb", Ob), "wg"], w=["tmpb"])
                    P.op("pool", lambda e: e.tensor_tensor(out=oacc, in0=oacc, in1=tmpb, op=ALU.add),
                         r=[roa, "tmpb"], w=[roa])

            def emit_imp(qt, ncnt):
                def fni(e):
                    for h in range(4):
                        ins = e.matmul(pb[6][:, 0:32], lhsT=pn[0:ncnt, h, :], rhs=ovl[0:ncnt, :],
                                       start=(h == 0), stop=(h == 3))
                    return ins
                P.op("pe", fni, r=["pn", "ovl"], w=[("pb", 6)])
                P.seq("dve", [
                    (lambda e: e.tensor_tensor(out=val, in0=pb[6][:, 0:32], in1=selc[:, 0, qt, :], op=ALU.mult),
                     [("pb", 6), "selc"], ["val"]),
                    (lambda e: e.tensor_tensor(out=val, in0=val, in1=selc[:, 1, qt, :], op=ALU.add),
                     ["val", "selc"], ["val"]),
                    (lambda e: e.max(out=m8[:, 0:8], in_=val), ["val"], ["m8a"]),
                    (lambda e: e.tensor_scalar(out=val2, in0=val, scalar1=m8[:, 7:8], scalar2=-1e9,
                                               op0=ALU.is_ge, op1=ALU.mult), ["val", "m8a"], ["val2"]),
                    (lambda e: e.tensor_tensor(out=val2, in0=val2, in1=val, op=ALU.add), ["val2", "val"], ["val2"]),
                    (lambda e: e.max(out=m8[:, 8:16], in_=val2), ["val2"], ["m8b"]),
                    (lambda e: e.tensor_scalar(out=negsel, in0=val, scalar1=m8[:, 15:16], scalar2=NEGB,
                                               op0=ALU.is_lt, op1=ALU.mult), ["val", "m8b"], ["negsel"]),
                ])

            def post_cmp(qt, ncnt):
                def post(Db, Ob, lastpt):
                    pt_c, rpt = lastpt
                    P.op("dve", lambda e: e.tensor_scalar(out=rd, in0=v4(pb[Db]), scalar1=1e-30, scalar2=None, op0=ALU.max),
                         r=[("pb", Db)], w=["rd"])
                    P.op("dve", lambda e: e.reciprocal(out=rd, in_=rd), r=["rd"], w=["rd"])
                    if qt >= 8:
                        P.op("dve", lambda e: e.tensor_tensor(out=pn[0:ncnt], in0=pt_c[0:ncnt], in1=rd[0:ncnt], op=ALU.mult),
                             r=[rpt, "rd"], w=["pn"])
                    combine(Db, Ob, 0, qt, True)
                    if qt >= 8:
                        push(lambda: emit_imp(qt, ncnt))
                return post

            def post_mid(br, qt):
                def post(Db, Ob, lastpt):
                    combine(Db, Ob, br, qt, False)
                return post

            def post_last(br, qt):
                def post(Db, Ob, lastpt):
                    combine(Db, Ob, br, qt, False)
                    P.op("act", lambda e: e.activation(out=nsa_g[:, :, qt * 128:(qt + 1) * 128], in_=oacc2[qt % 2], func=AF.Copy),
                         r=[("oacc", qt % 2)], w=["nsa_g"])
                return post

            for qt in range(NCH):
                if qt % 2 == 0:
                    qq = qt // 2
                    glq = glq2[qq % 2]
                    rglq = ("glq", qq % 2)
                    src = bass.AP(ng_s.ap().tensor, ng_s.ap().offset + g * 12 * S + qq * 256, [[0, 128], [S, 12], [1, 256]])
                    P.dma("sp", glq, src, r=[("win_out",)], w=[rglq], chan=rglq)
                    P.op("act", lambda e, glq=glq: e.activation(out=glq, in_=glq, func=AF.Sigmoid), r=[rglq], w=[rglq])
                rq = qb[:, :, qt * 128:(qt + 1) * 128]
                ncnt = min(127, 8 * qt + 7)
                c_lo = 129 - 8 * qt
                tiles = [(ncnt, kcT[:, 0:ncnt], [(si2[0:8, c_lo:c_lo + ncnt], pat4[:, 0], ["si2", "pat4"])],
                          vc[0:ncnt, :], ["kcT", "vc"])]
                attend(tiles, rq, post_cmp(qt, ncnt))
                tiles = []
                for kt in range(max(0, qt - 4), qt + 1):
                    masks = []
                    if kt == qt:
                        masks.append((ident_b, tri4[:, 0], ["tri4"]))
                    if kt == qt - 4:
                        masks.append((ident_b, tri4[:, 1], ["tri4"]))
                    tiles.append((128, kwin_b[:, kt * 128:(kt + 1) * 128], masks, vtm[1][:, kt, :], [("kb", 3), ("vtm", 1)]))
                attend(tiles, rq, post_mid(2, qt))
                if qt >= 8:
                    drain_all()
                    tps = pb[7][:].bitcast(BF16)
                    P.op("pe", lambda e, tps=tps: e.transpose(tps[0:32, 0:128], negsel, ident_b),
                         r=["negsel", "ident_b"], w=[("pb", 7)])
                    P.op("dve", lambda e, tps=tps: bcast4(e, nselT4.rearrange("p (o h) q -> p o h q", o=1),
                                                          tps[0:32, 0:128].rearrange("p (o q) -> p o q", o=1), 1),
                         r=[("pb", 7)], w=["nselT4"])
                tiles = []
                for kt in range(qt + 1):
                    masks = []
                    if qt >= 8:
                        masks.append((Eb[0:32, kt * 128:(kt + 1) * 128], nselT4, ["Eb", "nselT4"]))
                    if kt == qt:
                        masks.append((ident_b, tri4[:, 0], ["tri4"]))
                    tiles.append((128, kslc_b[:, kt * 128:(kt + 1) * 128], masks, vtm[0][:, kt, :], [("kb", 2), ("vtm", 0)]))
                attend(tiles, rq, post_last(1, qt))
            drain_all()
            for h in range(4):
                P.dma("sp", nsa_s[g * 4 + h], nsa_g[:, h, :], r=["nsa_g"], w=[("nsa_out",)], chan="nsa_g")
        P.barrier()

    def phase_merge():
        for half in range(2):
            t0 = half * 1024
            ar.reset()
            lru_h = ar.alloc([128, NCH, 1024], BF16)
            nsa_h = ar.alloc([128, NCH, 1024], BF16)
            gab = [[ar.alloc([128, 1024], F32) for _ in range(2)] for _ in range(2)]
            mst = [ar.alloc([128, 1024], BF16) for _ in range(2)]
            t1 = ar.alloc([128, 512], F32)
            t2 = ar.alloc([128, 512], F32)
            P.dma("sp", lru_h, lru_s.rearrange("c p s -> p c s")[:, :, t0:t0 + 1024], r=[("lru_out",)], w=["lru_h"],
                  chan="lru_h")
            P.dma("sp", nsa_h, nsa_s.rearrange("c p s -> p c s")[:, :, t0:t0 + 1024], r=[("nsa_out",)], w=["nsa_h"],
                  chan="nsa_h")
            bctr = 0
            for j in range(8):
                wa_t, wa_res = wload(wk(w_a, j * 256, 256))
                wb_t, wb_res = wload(wk(w_b, j * 256, 256))
                for cc in range(2):
                    dc = j * 2 + cc
                    k = dc % 2
                    P.dma("sp", gab[0][k], mg_s[dc][:, t0:t0 + 1024], r=[("win_out",)], w=[("ga", k)], chan=("ga", k))
                    P.dma("sp", gab[1][k], mg_s[16 + dc][:, t0:t0 + 1024], r=[("win_out",)], w=[("gb", k)], chan=("gb", k))
                    for tt in range(2):
                        ba, bb = (bctr % 4) * 2, (bctr % 4) * 2 + 1
                        bctr += 1
                        for (bank, wt, wres, act_, ares) in ((ba, wa_t, wa_res, lru_h, "lru_h"), (bb, wb_t, wb_res, nsa_h, "nsa_h")):
                            def fn(e, bank=bank, wt=wt, act_=act_, cc=cc, tt=tt):
                                for kc in range(NCH):
                                    ins = e.matmul(pb[bank], lhsT=wt[:, kc, cc * 128:(cc + 1) * 128],
                                                   rhs=act_[:, kc, tt * 512:(tt + 1) * 512], start=(kc == 0), stop=(kc == NCH - 1))
                                return ins
                            P.op("pe", fn, r=[wres, ares], w=[("pb", bank)])

                        P.op("dve", lambda e, ba=ba, k=k, tt=tt: e.tensor_tensor(
                            out=t1, in0=pb[ba], in1=gab[0][k][:, tt * 512:(tt + 1) * 512], op=ALU.mult),
                            r=[("pb", ba), ("ga", k)], w=["t1"])
                        P.op("dve", lambda e, bb=bb, k=k, tt=tt: e.tensor_tensor(
                            out=t2, in0=pb[bb], in1=gab[1][k][:, tt * 512:(tt + 1) * 512], op=ALU.mult),
                            r=[("pb", bb), ("gb", k)], w=["t2"])
                        P.op("dve", lambda e, k=k, tt=tt: e.tensor_tensor(
                            out=mst[k][:, tt * 512:(tt + 1) * 512], in0=t1, in1=t2, op=ALU.add),
                            r=["t1", "t2"], w=[("mst", k)])
                    P.dma("sp", m_s[dc][:, t0:t0 + 1024], mst[k], r=[("mst", k)], w=[("m_out", half)], chan=("mst", k))
            P.barrier()
            ar.reset()
            u = ar.alloc([128, NCH, 1024], BF16)
            acc = ar.alloc([128, NCH, 1024], F32)
            xt2 = [ar.alloc([128, D], F32) for _ in range(2)]
            yt = ar.alloc([128, D], F32)
            vecG = ar.alloc([128, D], F32)
            P.dma("sp", u, m_s.rearrange("c p s -> p c s")[:, :, t0:t0 + 1024], r=[("m_out", half)],
                  w=[("u", 0), ("u", 1)], chan="u_m")
            ev = 0
            for j in range(8):
                wo_t, wo_res = wload(wk(w_o, j * 256, 256))
                for cc in range(2):
                    dc = j * 2 + cc
                    for tt in range(2):
                        bank = bctr % 8
                        bctr += 1

                        def fn(e, bank=bank, wo_t=wo_t, cc=cc, tt=tt):
                            for kc in range(NCH):
                                ins = e.matmul(pb[bank], lhsT=wo_t[:, kc, cc * 128:(cc + 1) * 128],
                                               rhs=u[:, kc, tt * 512:(tt + 1) * 512], start=(kc == 0), stop=(kc == NCH - 1))
                            return ins
                        P.op("pe", fn, r=[wo_res, ("u", tt)], w=[("pb", bank)])
                        a_ = acc[:, dc, tt * 512:(tt + 1) * 512]
                        ev += 1
                        if ev % 2:
                            P.op("act", lambda e, a_=a_, bank=bank: e.activation(out=a_, in_=pb[bank], func=AF.Copy),
                                 r=[("pb", bank)], w=[("acc", dc, tt)])
                        else:
                            P.op("dve", lambda e, a_=a_, bank=bank: e.tensor_copy(out=a_, in_=pb[bank]),
                                 r=[("pb", bank)], w=[("acc", dc, tt)])
            epilogue(acc, x1, "x1", 1, x2, "x2", t0, 1024, (xt2, yt, vecG, "vecG"))
            P.barrier()

    if "ada" in phases:
        phase_ada()
    if "ffn1" in phases:
        ffn_phase(x_in, "x", 0, x1, "x1", w1g, w1u, w1d)
    if "win" in phases:
        phase_win()
    if "nsa" in phases:
        phase_nsa()
    if "merge" in phases:
        phase_merge()
    if "ffn2" in phases:
        ffn_phase(x2, "x2", 2, out, "out", w2g, w2u, w2d)
    P.barrier()
    P.emit(es)
    es.close()
    P.dr = dr
    return nc, P


def _consts():
    c = {}
    pos = np.arange(S, dtype=np.float32)
    inv_freq = (np.float32(500000.0) ** (-np.arange(0, 32, 2, dtype=np.float32) / np.float32(32))).astype(np.float32)
    ang = (pos[:, None] * inv_freq[None, :]).astype(np.float32)
    cos, sin = np.cos(ang).astype(np.float32), np.sin(ang).astype(np.float32)
    rope = np.zeros((32, 2, S), np.float32)
    rope[0:16, 0] = cos.T
    rope[16:32, 0] = cos.T
    rope[0:16, 1] = -sin.T
    rope[16:32, 1] = sin.T
    c["c_rope"] = rope
    c["c_ident"] = np.eye(128, dtype=np.float32)
    k8 = np.arange(8)[:, None]
    c["c_si2"] = (np.arange(256)[None, :] == k8 + 128).astype(np.float32)
    c["c_pat"] = np.where(16 * k8 + 15 <= np.arange(128)[None, :], 0.0, NEGB).astype(np.float32)
    k = np.arange(128)[:, None]
    ql = np.arange(128)[None, :]
    tri = np.zeros((128, 2, 128), np.float32)
    tri[:, 0, :] = np.where(k <= ql, 0.0, NEGB)
    tri[:, 1, :] = np.where(k > ql, 0.0, NEGB)
    c["c_tri"] = tri
    j = np.arange(32)[:, None]
    kk = np.arange(S)[None, :]
    c["c_E"] = (kk // 64 == j).astype(np.float32)
    nn = np.arange(128)[:, None]
    jj = np.arange(32)[None, :]
    ovl = ((16 * nn < 64 * jj + 64) & (16 * nn + 32 > 64 * jj) & (nn < 127)).astype(np.float32)
    c["c_ovl"] = ovl
    tok = np.arange(S)
    cur = tok // 64
    blk = np.arange(32)
    forced = (blk[None, :] == 0) | (blk[None, :] == cur[:, None]) | (blk[None, :] == cur[:, None] - 1)
    causal = (blk[None, :] * 64) <= tok[:, None]
    c01 = (causal & ~forced).astype(np.float32)
    adj = np.where(forced, 10.0 + blk[None, :], np.where(causal, 0.0, -1.0 - 0.01 * blk[None, :])).astype(np.float32)
    sel = np.zeros((128, 2, NCH, 32), np.float32)
    sel[:, 0] = c01.reshape(NCH, 128, 32).transpose(1, 0, 2)
    sel[:, 1] = adj.reshape(NCH, 128, 32).transpose(1, 0, 2)
    c["c_sel"] = sel
    return c


def _fm(v):
    return np.ascontiguousarray(np.asarray(v, np.float32).reshape(NCH, 128).T)


def prep_shared(inp):
    sh = {}
    sh["w_ada"] = np.ascontiguousarray(inp["w_ada"][0])
    sh["b_adaT"] = np.ascontiguousarray(inp["b_ada"][0].reshape(144, 128).T)
    sh["gainsT"] = np.ascontiguousarray(np.stack([_fm(inp[k][0]) for k in (
        "ffn1_pre_g", "ffn1_post_g", "mix_pre_g", "mix_post_g", "ffn2_pre_g", "ffn2_post_g")], axis=1))
    for k in ("ffn1_w_gate", "ffn1_w_up", "ffn1_w_down", "ffn2_w_gate", "ffn2_w_up", "ffn2_w_down",
              "w_in", "w_a_out", "w_b_out", "w_out"):
        sh[k] = np.ascontiguousarray(inp[k][0])
    cw = inp["conv_w"][0]
    sh["lruvT"] = np.ascontiguousarray(np.stack(
        [_fm(cw[0]), _fm(cw[1]), _fm(cw[2]), _fm(cw[3]), _fm(inp["conv_b"][0]), _fm(inp["lru_br"][0]),
         _fm(inp["lru_bi"][0]), _fm(inp["lru_lambda"][0])], axis=1))
    sh["lru_w"] = np.ascontiguousarray(np.stack([inp["lru_wr"][0], inp["lru_wi"][0]], axis=0).transpose(2, 0, 1, 3))
    sh["cmp_pe"] = np.ascontiguousarray(np.stack([inp["cmp_pe_k"][0].T, inp["cmp_pe_v"][0].T], axis=1))
    sh["cmp_w1"] = np.ascontiguousarray(np.stack(
        [inp["cmp_w1_k"][0].reshape(32, 128, 128).transpose(1, 0, 2),
         inp["cmp_w1_v"][0].reshape(32, 128, 128).transpose(1, 0, 2)], axis=1))
    sh["cmp_w2"] = np.ascontiguousarray(np.stack([inp["cmp_w2_k"][0], inp["cmp_w2_v"][0]], axis=1))
    sh.update(_consts())
    return sh


ALL_PHASES = ["ada", "ffn1", "win", "nsa", "merge", "ffn2"]


def kernel(**inp):
    nc, P = build({"phases": ALL_PHASES, "roles": {"x1": "out", "x2": "out"}})
    sh = prep_shared(inp)
    sh = {k: v for k, v in sh.items() if k in P.dr}
    in_maps = []
    for b in range(8):
        m = dict(sh)
        m["x"] = np.ascontiguousarray(inp["x"][b], dtype=np.float32)
        m["cT"] = _fm(inp["c"][b])
        in_maps.append(m)
    res = run_bass_kernel_spmd(nc, in_maps, core_ids=list(range(8)))
    return np.stack([np.asarray(r["out"], np.float32) for r in res.results], axis=0)
```

```python
import numpy as np
from contextlib import ExitStack
import concourse.bass as bass
import concourse.mybir as mybir
from concourse.bass_utils import run_bass_kernel_spmd

F32 = mybir.dt.float32
BF16 = mybir.dt.bfloat16
AF = mybir.ActivationFunctionType
ALU = mybir.AluOpType

D = 2048
S = 2048
DFF = 5632
NCH = 16
INW = 13360
EPS = 1e-6
NEGB = -30000.0
SCALE = 128 ** -0.5
ENGS = ["pe", "act", "dve", "pool", "sp"]


class Op:
    __slots__ = ("i", "eng", "fn", "deps", "dma", "chan", "sig", "sval", "dval", "n")

    def __init__(self, i, eng, fn, deps, dma, chan):
        self.i, self.eng, self.fn, self.deps, self.dma, self.chan = i, eng, fn, deps, dma, chan
        self.sig = False
        self.sval = 0
        self.dval = 0
        self.n = 1


class Prog:
    def __init__(self, nc):
        self.nc = nc
        self.ops = []
        self.lastw = {}
        self.rd_eng = {}
        self.rd_dma = {}
        self.last_on = {}
        self.pending = {e: set() for e in ENGS}
        self.dma_open = []

    def _add(self, eng, fn, r, w, dma=False, chan=None, n=1):
        i = len(self.ops)
        deps = self.pending[eng]
        self.pending[eng] = set()
        for x in r:
            if x in self.lastw:
                deps.add(self.lastw[x])
        for x in w:
            if x in self.lastw:
                deps.add(self.lastw[x])
            deps.update(self.rd_eng.get(x, {}).values())
            deps.update(self.rd_dma.get(x, ()))
        op = Op(i, eng, fn, deps, dma, chan)
        op.n = n
        self.ops.append(op)
        for x in r:
            if dma:
                self.rd_dma.setdefault(x, set()).add(i)
            else:
                self.rd_eng.setdefault(x, {})[eng] = i
        for x in w:
            self.lastw[x] = i
            self.rd_eng[x] = {}
            self.rd_dma[x] = set()
        self.last_on[eng] = i
        if dma:
            self.dma_open.append(i)
        return op

    def op(self, eng, fn, r=(), w=()):
        return self._add(eng, fn, r, w)

    def dma(self, queue, out, in_, r=(), w=(), chan=None):
        outs = out if isinstance(out, (list, tuple)) else [out]
        ins = in_ if isinstance(in_, (list, tuple)) else [in_]
        outs = [o.ap() if hasattr(o, "_ap") else o for o in outs]
        ins = [o.ap() if hasattr(o, "_ap") else o for o in ins]

        def fn(e, outs=outs, ins=ins):
            return [e.dma_start(out=o, in_=i_) for o, i_ in zip(outs, ins)]
        return self._add(queue, fn, r, w, dma=True, chan=chan, n=len(outs))

    def seq(self, eng, steps):
        for fn, r, w in steps:
            self._add(eng, fn, r, w)

    def barrier(self):
        lasts = dict(self.last_on)
        for e in ENGS:
            for e2, i in lasts.items():
                if e2 != e:
                    self.pending[e].add(i)
            self.pending[e].update(self.dma_open)
        self.dma_open = []

    def emit(self, es):
        nc = self.nc
        ops = self.ops
        for op in ops:
            for d in op.deps:
                P = ops[d]
                if (not P.dma) and (P.eng != op.eng or op.eng != "pe"):
                    P.sig = True
        cnt = {e: 0 for e in ENGS}
        chcnt = {}
        for op in ops:
            if op.dma:
                chcnt[op.chan] = chcnt.get(op.chan, 0) + 16 * op.n
                op.dval = chcnt[op.chan]
            elif op.sig:
                cnt[op.eng] += 1
                op.sval = cnt[op.eng]
        esem = {e: es.enter_context(nc.semaphore("s_" + e)) for e in ENGS}
        csem = {}
        for k, ch in enumerate(chcnt):
            csem[ch] = es.enter_context(nc.semaphore("c%d" % k))
        self.stats = dict(n_ops=len(ops), sig=dict(cnt), nchan=len(chcnt))
        block = es.enter_context(nc.Block())
        bname = {"pe": "tensor", "act": "scalar", "dve": "vector", "pool": "gpsimd", "sp": "sync"}
        for eng in ENGS:
            def body(e, eng=eng):
                waited = {}
                for op in ops:
                    if op.eng != eng:
                        continue
                    need = {}
                    for d in op.deps:
                        P = ops[d]
                        if P.dma:
                            key, val = ("c", P.chan), P.dval
                        elif P.eng != eng or eng != "pe":
                            key, val = ("e", P.eng), P.sval
                        else:
                            continue
                        if need.get(key, 0) < val:
                            need[key] = val
                    for key, val in need.items():
                        if waited.get(key, 0) >= val:
                            continue
                        e.wait_ge(csem[key[1]] if key[0] == "c" else esem[key[1]], val)
                        waited[key] = val
                    if op.dma:
                        for inst in op.fn(e):
                            inst.then_inc(csem[op.chan], 16)
                    else:
                        inst = op.fn(e)
                        if op.sig:
                            inst.then_inc(esem[eng], 1)
                if eng == "sp":
                    for ch, v in chcnt.items():
                        if waited.get(("c", ch), 0) < v:
                            e.wait_ge(csem[ch], v)
                    for e2 in ENGS:
                        if e2 != "sp" and cnt[e2] > 0 and waited.get(("e", e2), 0) < cnt[e2]:
                            e.wait_ge(esem[e2], cnt[e2])
            getattr(block, bname[eng])(body)


class Arena:
    def __init__(self, nc, es, nbytes):
        self.t = es.enter_context(nc.sbuf_tensor("arena", [128, nbytes // 2], BF16))
        self.nbytes = nbytes
        self.base = 0
        self.off = 0

    def alloc(self, shape, dt):
        esz = 4 if dt == F32 else 2
        n = 1
        for s_ in shape[1:]:
            n *= s_
        nb = (n * esz + 63) // 64 * 64
        assert self.off + nb <= self.nbytes, ("arena overflow", self.off, nb, self.nbytes)
        a = self.t[:, self.off // 2:self.off // 2 + n * esz // 2]
        self.off += nb
        if dt == F32:
            a = a.bitcast(F32)
        if len(shape) == 3:
            a = a.rearrange("p (a b) -> p a b", a=shape[1])
        elif len(shape) == 4:
            a = a.rearrange("p (a b c) -> p a b c", a=shape[1], b=shape[2])
        if shape[0] != 128:
            a = a[0:shape[0]]
        return a

    def mark(self):
        self.base = self.off

    def reset(self):
        self.off = self.base


def bc_dram(ap_row, nparts=128):
    return bass.AP(ap_row.tensor, ap_row.offset, [[0, nparts]] + [list(x) for x in ap_row.ap[1:]])


def build(cfg):
    phases = cfg["phases"]
    roles = cfg.get("roles", {})
    nc = bass.Bass("TRN2", target_bir_lowering=False)
    es = ExitStack()
    dr = {}

    class LZ:
        def __init__(self, name, shape, dt, kind):
            self.name, self.shape, self.dt, self.kind, self._ap = name, shape, dt, kind, None

        def ap(self):
            if self._ap is None:
                self._ap = nc.dram_tensor(self.name, self.shape, self.dt, kind=self.kind).ap()
                dr[self.name] = self._ap
            return self._ap

        def __getitem__(self, k):
            return self.ap()[k]

        def rearrange(self, *a, **k):
            return self.ap().rearrange(*a, **k)

    def dram(name, shape, dt, kind="Internal"):
        k = roles.get(name, kind)
        k = {"in": "ExternalInput", "out": "ExternalOutput", "internal": "Internal"}.get(k, k)
        return LZ(name, shape, dt, k)

    IN = "ExternalInput"
    x_in = dram("x", [S, D], F32, IN)
    cT = dram("cT", [128, NCH], F32, IN)
    w_ada = dram("w_ada", [D, 9 * D], F32, IN)
    b_adaT = dram("b_adaT", [128, 144], F32, IN)
    gainsT = dram("gainsT", [128, 6, NCH], F32, IN)
    w1g = dram("ffn1_w_gate", [D, DFF], F32, IN)
    w1u = dram("ffn1_w_up", [D, DFF], F32, IN)
    w1d = dram("ffn1_w_down", [DFF, D], F32, IN)
    w2g = dram("ffn2_w_gate", [D, DFF], F32, IN)
    w2u = dram("ffn2_w_up", [D, DFF], F32, IN)
    w2d = dram("ffn2_w_down", [DFF, D], F32, IN)
    w_in = dram("w_in", [D, INW], F32, IN)
    lruvT = dram("lruvT", [128, 8, NCH], F32, IN)
    lru_w = dram("lru_w", [128, 2, NCH, 128], F32, IN)
    cmp_pe = dram("cmp_pe", [128, 2, 32], F32, IN)
    cmp_w1 = dram("cmp_w1", [128, 2, 32, 128], F32, IN)
    cmp_w2 = dram("cmp_w2", [128, 2, 128], F32, IN)
    w_a = dram("w_a_out", [D, D], F32, IN)
    w_b = dram("w_b_out", [D, D], F32, IN)
    w_o = dram("w_out", [D, D], F32, IN)
    c_rope = dram("c_rope", [32, 2, S], F32, IN)
    c_ident = dram("c_ident", [128, 128], F32, IN)
    c_si2 = dram("c_si2", [8, 256], F32, IN)
    c_pat = dram("c_pat", [8, 128], F32, IN)
    c_tri = dram("c_tri", [128, 2, 128], F32, IN)
    c_E = dram("c_E", [32, S], F32, IN)
    c_ovl = dram("c_ovl", [128, 32], F32, IN)
    c_sel = dram("c_sel", [128, 2, NCH, 32], F32, IN)
    out = dram("out", [S, D], F32, "ExternalOutput")
    vec_s = dram("vec_s", [9, D], F32)
    x1 = dram("x1", [S, D], F32)
    x2 = dram("x2", [S, D], F32)
    xa_s = dram("xa_s", [NCH, 128, S], F32)
    ya_s = dram("ya_s", [NCH, 128, S], F32)
    q_s = dram("q_s", [NCH, 128, S], F32)
    kv_s = dram("kv_s", [6, 4, 128, S], F32)
    vt_s = dram("vt_s", [2, S, 512], BF16)
    ng_s = dram("ng_s", [48, S], F32)
    mg_s = dram("mg_s", [32, 128, S], F32)
    lru_s = dram("lru_s", [NCH, 128, S], BF16)
    nsa_s = dram("nsa_s", [NCH, 128, S], BF16)
    m_s = dram("m_s", [NCH, 128, S], BF16)

    P = Prog(nc)
    ar = Arena(nc, es, 211968)
    pb = [es.enter_context(nc.psum_tensor("pb%d" % i, [128, 512], F32))[:] for i in range(8)]

    ident_f = ar.alloc([128, 128], F32)
    ident_b = ar.alloc([128, 128], BF16)
    ones_b = ar.alloc([128, 128], BF16)
    modT = ar.alloc([128, 144], F32)
    gT = ar.alloc([128, 6, NCH], F32)
    comb = ar.alloc([128, 144], F32)
    small = ar.alloc([128, 64], F32)
    NSLOT = 6
    wslot = [ar.alloc([128, 4096], BF16) for _ in range(NSLOT)]
    ar.mark()
    wctr = [0]
    ring = [0]

    def wload(src_ap, slot=None):
        if slot is None:
            k = ring[0] + wctr[0] % (NSLOT - ring[0])
            wctr[0] += 1
        else:
            k = slot
        res = ("w", k)
        n = 1
        for s_ in src_ap.shape[1:]:
            n *= s_
        assert n <= 4096
        v = wslot[k][:, 0:n]
        if len(src_ap.shape) == 3:
            v = v.rearrange("p (a b) -> p a b", a=src_ap.shape[1])
        P.dma("pool", v, src_ap, w=[res], chan=res)
        return v, res

    def wk(w_dram, c0, ncols):
        return w_dram[:, c0:c0 + ncols].rearrange("(kc p) n -> p kc n", p=128)

    P.dma("sp", ident_f, c_ident, w=["ident_f"], chan="ident_f")
    P.op("dve", lambda e: e.tensor_copy(out=ident_b, in_=ident_f), r=["ident_f"], w=["ident_b"])
    P.op("dve", lambda e: e.memset(ones_b, 1.0), w=["ones_b"])
    P.dma("sp", gT, gainsT, w=["gT"], chan="gT")

    def phase_ada():
        ar.reset()
        cact_f = ar.alloc([128, NCH], F32)
        cact = ar.alloc([128, NCH], BF16)
        badT = ar.alloc([128, 144], F32)
        tr_sb = ar.alloc([128, 2, 128], F32)
        P.dma("sp", cact_f, cT, w=["cact_f"], chan="cact_f")
        P.dma("sp", badT, b_adaT, w=["badT"], chan="badT")
        P.op("act", lambda e: e.activation(out=cact, in_=cact_f, func=AF.Silu), r=["cact_f"], w=["cact"])
        mod_ps = pb[0][:, 0:144]
        for j in range(72):
            wt, res = wload(wk(w_ada, j * 256, 256))
            for cc in range(2):
                col = j * 2 + cc

                def fn(e, wt=wt, cc=cc, col=col):
                    for kc in range(NCH):
                        ins = e.matmul(mod_ps[:, col:col + 1], lhsT=wt[:, kc, cc * 128:(cc + 1) * 128],
                                       rhs=cact[:, kc:kc + 1], start=(kc == 0), stop=(kc == NCH - 1))
                    return ins
                P.op("pe", fn, r=[res, "cact"], w=["mod_ps"])
        P.op("dve", lambda e: e.tensor_tensor(out=modT, in0=mod_ps, in1=badT, op=ALU.add),
             r=["mod_ps", "badT"], w=["modT"])
        for i in range(3):
            sh = modT[:, (3 * i) * 16:(3 * i + 1) * 16]
            sc = modT[:, (3 * i + 1) * 16:(3 * i + 2) * 16]
            ga = modT[:, (3 * i + 2) * 16:(3 * i + 3) * 16]
            A_ = comb[:, (3 * i) * 16:(3 * i + 1) * 16]
            B_ = comb[:, (3 * i + 1) * 16:(3 * i + 2) * 16]
            G_ = comb[:, (3 * i + 2) * 16:(3 * i + 3) * 16]
            wgt = 1.0 if i == 1 else 0.5

            def fn(e, sh=sh, sc=sc, ga=ga, A_=A_, B_=B_, G_=G_, i=i, wgt=wgt):
                e.scalar_tensor_tensor(out=A_, in0=sc, scalar=1.0, in1=gT[:, 2 * i, :], op0=ALU.add, op1=ALU.mult)
                e.tensor_copy(out=B_, in_=sh)
                return e.scalar_tensor_tensor(out=G_, in0=ga, scalar=wgt, in1=gT[:, 2 * i + 1, :],
                                              op0=ALU.mult, op1=ALU.mult)
            P.op("dve", fn, r=["modT", "gT"], w=["comb"])
        tr_ps = pb[1]
        for hh in range(2):
            P.op("pe", lambda e, hh=hh: e.transpose(tr_ps[0:72, hh * 128:(hh + 1) * 128],
                                                   comb[:, hh * 72:(hh + 1) * 72], ident_f),
                 r=["comb", "ident_f"], w=[("pb", 1)])
            P.op("dve", lambda e, hh=hh: e.tensor_copy(out=tr_sb[0:72, hh, :],
                                                       in_=tr_ps[0:72, hh * 128:(hh + 1) * 128]),
                 r=[("pb", 1)], w=[("tr_sb", hh)])
            dst = vec_s.rearrange("v (c f) -> (v c) f", f=128)[hh * 72:(hh + 1) * 72, :]
            P.dma("sp", dst, tr_sb[0:72, hh, :], r=[("tr_sb", hh)], w=["vec_s"], chan=("tr_sb", hh))
        P.barrier()

    def load_vec(dst, rdst, col0, repb, rrep):
        src = comb[:, col0:col0 + NCH]
        v = bass.AP(src.tensor, src.offset, [list(src.ap[0]), list(src.ap[1]), [0, 128]])
        rp = repb.rearrange("p (c f) -> p c f", c=NCH)
        P.op("dve", lambda e: e.tensor_copy(out=rp, in_=v), r=["comb"], w=list(rrep))
        for b4 in range(4):
            def fn(e, b4=b4):
                for c4 in range(4):
                    ins = e.transpose(pb[4 + b4][:, c4 * 128:(c4 + 1) * 128], rp[:, b4 * 4 + c4, :], ident_f)
                return ins
            P.op("pe", fn, r=list(rrep) + ["ident_f"], w=[("pb", 4 + b4)])
            P.op("act", lambda e, b4=b4: e.activation(out=dst[:, b4 * 512:(b4 + 1) * 512], in_=pb[4 + b4], func=AF.Copy),
                 r=[("pb", 4 + b4)], w=[rdst])

    def prologue(x_src, xname, vi, u, t0, T, bufs):
        xt2, utok, vecA, vecB, junk, repb, rrep = bufs
        junk = repb
        load_vec(vecA, "vecA", (3 * vi) * NCH, repb, rrep)
        load_vec(vecB, "vecB", (3 * vi + 1) * NCH, repb, rrep)
        nt = T // 128
        for ti in range(nt):
            xt = xt2[ti % 2]
            rx = ("xt", ti % 2)
            tok = t0 + ti * 128
            P.dma("sp", xt, x_src[tok:tok + 128, :], r=[("xd", xname, tok)], w=[rx], chan=rx)
            ssq = small[:, 0:1]
            rstd = small[:, 1:2]
            P.op("dve", lambda e: e.memset(ssq, 0.0), w=["ssq"])
            P.op("act", lambda e, xt=xt: e.activation(out=junk, in_=xt, func=AF.Square, accum_out=ssq),
                 r=[rx, "ssq"], w=list(rrep) + ["ssq"])
            P.op("dve", lambda e: e.tensor_scalar(out=rstd, in0=ssq, scalar1=1.0 / D, scalar2=EPS,
                                                  op0=ALU.mult, op1=ALU.add), r=["ssq"], w=["rstd"])
            P.op("act", lambda e: e.activation(out=rstd, in_=rstd, func=AF.Sqrt), r=["rstd"], w=["rstd"])
            P.op("dve", lambda e: e.reciprocal(out=rstd, in_=rstd), r=["rstd"], w=["rstd"])
            P.op("dve", lambda e, xt=xt: e.scalar_tensor_tensor(out=xt, in0=xt, scalar=rstd, in1=vecA,
                                                                op0=ALU.mult, op1=ALU.mult),
                 r=[rx, "rstd", "vecA"], w=[rx])
            P.op("dve", lambda e, xt=xt: e.tensor_tensor(out=utok, in0=xt, in1=vecB, op=ALU.add),
                 r=[rx, "vecB"], w=["utok"])
            bk = 2 * (ti % 2)
            for hb in range(2):
                pbank = pb[bk + hb][:].bitcast(BF16)

                def fn(e, pbank=pbank, hb=hb):
                    for c8 in range(8):
                        dc = hb * 8 + c8
                        ins = e.transpose(pbank[:, c8 * 128:(c8 + 1) * 128], utok[:, dc * 128:(dc + 1) * 128], ident_b)
                    return ins
                P.op("pe", fn, r=["utok", "ident_b"], w=[("pb", bk + hb)])
                P.op("act", lambda e, pbank=pbank, hb=hb, ti=ti: e.activation(
                    out=u[:, hb * 8:(hb + 1) * 8, ti * 128:(ti + 1) * 128],
                    in_=pbank.rearrange("p (c t) -> p c t", c=8), func=AF.Copy),
                    r=[("pb", bk + hb)], w=[("u", ti // 4)])

    def epilogue(acc, x_src, xname, vi, x_dst, dname, t0, T, bufs):
        xt2, yt, vecG, rG = bufs
        load_vec(vecG, rG, (3 * vi + 2) * NCH, yt, [("yt", b) for b in range(4)])
        nt = T // 128
        for ti in range(nt):
            xt = xt2[ti % 2]
            rx = ("xt", ti % 2)
            tok = t0 + ti * 128
            P.dma("sp", xt, x_src[tok:tok + 128, :], r=[("xd", xname, tok)], w=[rx], chan=rx)
            bk = 4 * (ti % 2)
            P.op("dve", lambda e: e.memset(small[:, 8:40], 0.0), r=[("ssq4", b) for b in range(4)], w=["ssq4s"])
            ssq = small[:, 0:1]
            rstd = small[:, 1:2]
            for b4 in range(4):
                def fn(e, b4=b4, bk=bk, ti=ti):
                    for c4 in range(4):
                        dc = b4 * 4 + c4
                        ins = e.transpose(pb[bk + b4][:, c4 * 128:(c4 + 1) * 128],
                                          acc[:, dc, ti * 128:(ti + 1) * 128], ident_f)
                    return ins
                P.op("pe", fn, r=[("acc", dc_, ti // 4) for dc_ in range(b4 * 4, b4 * 4 + 4)] + ["ident_f"],
                     w=[("pb", bk + b4)])
                P.op("act", lambda e, b4=b4, bk=bk: e.activation(out=yt[:, b4 * 512:(b4 + 1) * 512], in_=pb[bk + b4],
                                                                func=AF.Square, accum_out=small[:, 8 + 8 * b4:9 + 8 * b4]),
                     r=[("pb", bk + b4), "ssq4s"], w=[("yt", b4), ("ssq4", b4)])
            P.op("dve", lambda e: e.tensor_reduce(out=ssq, in_=small[:, 8:40:8], axis=mybir.AxisListType.X, op=ALU.add),
                 r=[("ssq4", b) for b in range(4)], w=["ssq"])
            P.op("dve", lambda e: e.tensor_scalar(out=rstd, in0=ssq, scalar1=1.0 / D, scalar2=EPS,
                                                  op0=ALU.mult, op1=ALU.add), r=["ssq"], w=["rstd"])
            P.op("act", lambda e: e.activation(out=rstd, in_=rstd, func=AF.Sqrt), r=["rstd"], w=["rstd"])
            P.op("dve", lambda e: e.reciprocal(out=rstd, in_=rstd), r=["rstd"], w=["rstd"])
            for b4 in range(4):
                P.op("dve", lambda e, b4=b4, bk=bk: e.scalar_tensor_tensor(
                    out=yt[:, b4 * 512:(b4 + 1) * 512], in0=pb[bk + b4], scalar=rstd,
                    in1=vecG[:, b4 * 512:(b4 + 1) * 512], op0=ALU.mult, op1=ALU.mult),
                    r=[("pb", bk + b4), "rstd", rG], w=[("yt", b4)])
            P.op("dve", lambda e, xt=xt: e.tensor_tensor(out=xt, in0=xt, in1=yt, op=ALU.add),
                 r=[rx] + [("yt", b) for b in range(4)], w=[rx])
            P.dma("sp", x_dst[tok:tok + 128, :], xt, r=[rx], w=[("xd", dname, tok)], chan=rx)

    def ffn_half(u, acc, hbuf, sbuf, wg, wu, wd):
        NG = DFF // 256
        TT = 2

        def loads_gu(g):
            a = wload(wk(wg, g * 256, 256), slot=(g % 2) * 2)
            b = wload(wk(wu, g * 256, 256), slot=(g % 2) * 2 + 1)
            return a, b

        def loads_d(g):
            return wload(wd[g * 256:(g + 1) * 256, :].rearrange("(fc p) n -> p fc n", p=128), slot=4 + g % 2)
        pgu = {0: loads_gu(0), 1: loads_gu(1)}
        pd = {0: loads_d(0)}
        down_q = []

        def emit_down(g, wdt, wdres, hb, hres):
            items = []
            for dc in range(NCH):
                for tt in range(TT):
                    items.append((g, dc, tt, wdt, wdres, hb, hres))
            return items

        yctr = [0]

        def do_down(item):
            g, dc, tt, wdt, wdres, hb, hres = item
            bank = 4 + (yctr[0] % 4)
            yctr[0] += 1
            yp = pb[bank]

            def fn(e):
                for fc in range(2):
                    ins = e.matmul(yp, lhsT=wdt[:, fc, dc * 128:(dc + 1) * 128], rhs=hb[:, fc, tt * 512:(tt + 1) * 512],
                                   start=(fc == 0), stop=(fc == 1))
                return ins
            P.op("pe", fn, r=[wdres, hres], w=[("pb", bank)])
            a_ = acc[:, dc, tt * 512:(tt + 1) * 512]
            ra = ("acc", dc, tt)
            if g == 0:
                P.op("act", lambda e: e.activation(out=a_, in_=yp, func=AF.Copy), r=[("pb", bank)], w=[ra])
            else:
                P.op("dve", lambda e: e.tensor_tensor(out=a_, in0=yp, in1=a_, op=ALU.add), r=[("pb", bank), ra], w=[ra])

        for g in range(NG):
            (wgt, wgres), (wut, wures) = pgu.pop(g)
            wdt, wdres = pd.pop(g)
            hb = hbuf[g % 2]
            hres = ("h", g % 2)
            k = 0
            for fc in range(2):
                for tt in range(TT):
                    gp = pb[(k % 2) * 2]
                    up = pb[(k % 2) * 2 + 1]
                    rg = ("pb", (k % 2) * 2)
                    ru = ("pb", (k % 2) * 2 + 1)

                    def fng(e, gp=gp, wgt=wgt, fc=fc, tt=tt):
                        for kc in range(NCH):
                            ins = e.matmul(gp, lhsT=wgt[:, kc, fc * 128:(fc + 1) * 128],
                                           rhs=u[:, kc, tt * 512:(tt + 1) * 512], start=(kc == 0), stop=(kc == NCH - 1))
                        return ins

                    def fnu(e, up=up, wut=wut, fc=fc, tt=tt):
                        for kc in range(NCH):
                            ins = e.matmul(up, lhsT=wut[:, kc, fc * 128:(fc + 1) * 128],
                                           rhs=u[:, kc, tt * 512:(tt + 1) * 512], start=(kc == 0), stop=(kc == NCH - 1))
                        return ins
                    P.op("pe", fng, r=[wgres, ("u", tt)], w=[rg])
                    P.op("pe", fnu, r=[wures, ("u", tt)], w=[ru])
                    sb_ = sbuf[k % 2]
                    rs = ("s", k % 2)
                    P.op("act", lambda e, gp=gp, sb_=sb_: e.activation(out=sb_, in_=gp, func=AF.Silu), r=[rg], w=[rs])
                    P.op("dve", lambda e, up=up, sb_=sb_, hb=hb, fc=fc, tt=tt: e.tensor_tensor(
                        out=hb[:, fc, tt * 512:(tt + 1) * 512], in0=up, in1=sb_, op=ALU.mult),
                        r=[ru, rs], w=[hres])
                    k += 1
                    for _ in range(8):
                        if down_q:
                            do_down(down_q.pop(0))
            while down_q:
                do_down(down_q.pop(0))
            down_q = emit_down(g, wdt, wdres, hb, hres)
            if g + 2 < NG:
                pgu[g + 2] = loads_gu(g + 2)
            if g + 1 < NG:
                pd[g + 1] = loads_d(g + 1)
        while down_q:
            do_down(down_q.pop(0))

    def ffn_phase(x_src, xname, vi, x_dst, dname, wg, wu, wd):
        ar.reset()
        u = ar.alloc([128, NCH, 1024], BF16)
        acc = ar.alloc([128, NCH, 1024], F32)
        hbuf = [ar.alloc([128, 2, 1024], BF16) for _ in range(2)]
        sbuf = [ar.alloc([128, 512], F32) for _ in range(2)]
        xt2 = [ar.alloc([128, D], F32) for _ in range(2)]
        yt = ar.alloc([128, D], F32)
        utok = ar.alloc([128, D], BF16)
        vecA = ar.alloc([128, D], F32)
        vecB = ar.alloc([128, D], F32)
        junk = None
        for half in range(2):
            t0 = half * 1024
            prologue(x_src, xname, vi, u, t0, 1024, (xt2, utok, vecA, vecB, junk, yt, [("yt", b) for b in range(4)]))
            ffn_half(u, acc, hbuf, sbuf, wg, wu, wd)
            epilogue(acc, x_src, xname, vi, x_dst, dname, t0, 1024, (xt2, yt, vecA, "vecA"))
        P.barrier()

    def phase_win():
        ar.reset()
        u2 = ar.alloc([128, NCH, S], BF16)
        off_u2 = ar.off
        xt2 = [ar.alloc([128, D], F32) for _ in range(2)]
        utok = ar.alloc([128, D], BF16)
        vecA = ar.alloc([128, D], F32)
        vecB = ar.alloc([128, D], F32)
        junk = None
        repw = ar.alloc([128, D], F32)
        prologue(x1, "x1", 1, u2, 0, S, (xt2, utok, vecA, vecB, junk, repw, ["repw"]))
        P.barrier()
        ar.off = off_u2
        stg = [ar.alloc([128, S], F32) for _ in range(3)]
        vst = [ar.alloc([128, NCH, 256], BF16) for _ in range(2)]
        ctr = {"bank": 0, "stg": 0, "ev": 0, "vst": 0}
        ring[0] = 2
        lru_q = lru_setup(ctr)

        def drain(n):
            for _ in range(n):
                if lru_q["pending"]:
                    lru_q["pending"].pop(0)()

        def fm_tile(c0, dsts, sig=False, ncols=256, names=None):
            wt, res = wload(wk(w_in, c0, ncols))
            nchunk = (ncols + 127) // 128
            for cc in range(nchunk):
                m = min(128, ncols - cc * 128)
                k = ctr["stg"] % 3
                ctr["stg"] += 1
                st = stg[k]
                for tt in range(4):
                    bank = ctr["bank"] % 8
                    ctr["bank"] += 1
                    ps = pb[bank]

                    def fn(e, ps=ps, wt=wt, cc=cc, m=m, tt=tt):
                        for kc in range(NCH):
                            ins = e.matmul(ps[0:m, :], lhsT=wt[:, kc, cc * 128:cc * 128 + m],
                                           rhs=u2[:, kc, tt * 512:(tt + 1) * 512], start=(kc == 0), stop=(kc == NCH - 1))
                        return ins
                    P.op("pe", fn, r=[res, ("u", tt)], w=[("pb", bank)])
                    o_ = st[0:m, tt * 512:(tt + 1) * 512]
                    if sig:
                        P.op("act", lambda e, o_=o_, ps=ps, m=m: e.activation(out=o_, in_=ps[0:m, :], func=AF.Sigmoid),
                             r=[("pb", bank)], w=[("stg", k)])
                    else:
                        ctr["ev"] += 1
                        if ctr["ev"] % 2:
                            P.op("act", lambda e, o_=o_, ps=ps, m=m: e.activation(out=o_, in_=ps[0:m, :], func=AF.Copy),
                                 r=[("pb", bank)], w=[("stg", k)])
                        else:
                            P.op("dve", lambda e, o_=o_, ps=ps, m=m: e.tensor_copy(out=o_, in_=ps[0:m, :]),
                                 r=[("pb", bank)], w=[("stg", k)])
                    drain(2)
                P.dma("sp", dsts[cc], st[0:m, :], r=[("stg", k)], w=[names[cc] if names else ("win_out", c0, cc)],
                      chan=("stg", k))

        def tm_tile(c0, which, hcol):
            wt, res = wload(wk(w_in, c0, 256))
            k = ctr["vst"] % 2
            ctr["vst"] += 1
            vs = vst[k]
            for ti in range(NCH):
                bank = ctr["bank"] % 8
                ctr["bank"] += 1
                ps = pb[bank]

                def fn(e, ps=ps, wt=wt, ti=ti):
                    for kc in range(NCH):
                        ins = e.matmul(ps[:, 0:256], lhsT=u2[:, kc, ti * 128:(ti + 1) * 128], rhs=wt[:, kc, :],
                                       start=(kc == 0), stop=(kc == NCH - 1))
                    return ins
                P.op("pe", fn, r=[res, ("u", ti // 4)], w=[("pb", bank)])
                ctr["ev"] += 1
                if ctr["ev"] % 2:
                    P.op("act", lambda e, ps=ps, vs=vs, ti=ti: e.activation(out=vs[:, ti, :], in_=ps[:, 0:256], func=AF.Copy),
                         r=[("pb", bank)], w=[("vst", k)])
                else:
                    P.op("dve", lambda e, ps=ps, vs=vs, ti=ti: e.tensor_copy(out=vs[:, ti, :], in_=ps[:, 0:256]),
                         r=[("pb", bank)], w=[("vst", k)])
                if ti % 2:
                    drain(2)
            dst = vt_s[which].rearrange("(ti p) c -> p ti c", p=128)[:, :, hcol * 256:(hcol + 1) * 256]
            P.dma("sp", dst, vs, r=[("vst", k)], w=[("vt_out", which, hcol)], chan=("vst", k))

        for j in range(8):
            fm_tile(j * 256, [xa_s[2 * j], xa_s[2 * j + 1]], names=[("xa_s", 2 * j), ("xa_s", 2 * j + 1)])
            fm_tile(2048 + j * 256, [ya_s[2 * j], ya_s[2 * j + 1]], names=[("ya_s", 2 * j), ("ya_s", 2 * j + 1)])
            lru_q["pending"] += lru_chunk_ops(lru_q, 2 * j)
            lru_q["pending"] += lru_chunk_ops(lru_q, 2 * j + 1)
        for j in range(8):
            fm_tile(4096 + j * 256, [q_s[2 * j], q_s[2 * j + 1]])
        for i in range(6):
            for hcol in range(2):
                c0 = 6144 + i * 512 + hcol * 256
                if i == 3:
                    tm_tile(c0, 0, hcol)
                elif i == 5:
                    tm_tile(c0, 1, hcol)
                else:
                    fm_tile(c0, [kv_s[i, 2 * hcol], kv_s[i, 2 * hcol + 1]])
        fm_tile(9216, [ng_s], ncols=48)
        for j in range(16):
            fm_tile(9264 + j * 256, [mg_s[2 * j], mg_s[2 * j + 1]], sig=True)
        drain(10 ** 6)
        ring[0] = 0
        P.barrier()

    def lru_setup(ctr):
        lv = ar.alloc([128, 8, NCH], F32)
        cvec = ar.alloc([128, NCH], F32)
        c2vec = ar.alloc([128, NCH], F32)
        onef = ar.alloc([128, 1], F32)
        Lq = dict(lv=lv, cvec=cvec, c2vec=c2vec, onef=onef, ctr=ctr, pending=[])
        for nm in ("xa", "ya", "xc", "rr", "ii"):
            Lq[nm] = ar.alloc([128, S], F32)
        Lq["xcb"] = ar.alloc([128, S], BF16)
        P.dma("sp", lv, lruvT, w=["lv"], chan="lv")
        P.op("dve", lambda e: e.memset(onef, 1.0), w=["onef"])
        Lq["wr"] = wload(lru_w[:, 0], slot=0)
        Lq["wi"] = wload(lru_w[:, 1], slot=1)
        P.op("act", lambda e: e.activation(out=cvec, in_=lv[:, 7, :], func=AF.Exp, scale=-1.0), r=["lv"], w=["cvec"])
        P.op("dve", lambda e: e.tensor_scalar(out=cvec, in0=cvec, scalar1=1.0, scalar2=None, op0=ALU.add),
             r=["cvec"], w=["cvec"])
        P.op("act", lambda e: e.activation(out=cvec, in_=cvec, func=AF.Ln), r=["cvec"], w=["cvec"])
        P.op("dve", lambda e: e.tensor_scalar(out=c2vec, in0=cvec, scalar1=-16.0, scalar2=None, op0=ALU.mult),
             r=["cvec"], w=["c2vec"])
        P.op("dve", lambda e: e.tensor_scalar(out=cvec, in0=cvec, scalar1=-8.0, scalar2=None, op0=ALU.mult),
             r=["cvec"], w=["cvec"])
        return Lq

    def lru_chunk_ops(Lq, c):
        lv, cvec, c2vec, onef, ctr = Lq["lv"], Lq["cvec"], Lq["c2vec"], Lq["onef"], Lq["ctr"]
        xa, ya, xc, rr, ii, xcb = Lq["xa"], Lq["ya"], Lq["xc"], Lq["rr"], Lq["ii"], Lq["xcb"]
        (wr_t, wr_res), (wi_t, wi_res) = Lq["wr"], Lq["wi"]
        L = []
        L.append(lambda: P.dma("sp", xa, xa_s[c], r=[("xa_s", c)], w=["l_xa"], chan="l_xa"))
        L.append(lambda: P.dma("sp", ya, ya_s[c], r=[("ya_s", c)], w=["l_ya"], chan="l_ya"))
        L.append(lambda: P.op("dve", lambda e: e.tensor_scalar(out=xc, in0=xa, scalar1=lv[:, 3, c:c + 1],
                                                              scalar2=lv[:, 4, c:c + 1], op0=ALU.mult, op1=ALU.add),
                              r=["l_xa", "lv"], w=["l_xc"]))
        for sft in (1, 2, 3):
            L.append(lambda sft=sft: P.op("dve", lambda e: e.scalar_tensor_tensor(
                out=xc[:, sft:S], in0=xa[:, 0:S - sft], scalar=lv[:, 3 - sft, c:c + 1], in1=xc[:, sft:S],
                op0=ALU.mult, op1=ALU.add), r=["l_xa", "lv", "l_xc"], w=["l_xc"]))
        L.append(lambda: P.op("act", lambda e: e.activation(out=xcb, in_=xc, func=AF.Copy), r=["l_xc"], w=["l_xcb"]))
        for gi, (wt, wres, dst, bcol, rdst) in enumerate(((wr_t, wr_res, rr, 5, "l_rr"), (wi_t, wi_res, ii, 6, "l_ii"))):
            for tt in range(4):
                def mm(wt=wt, wres=wres, dst=dst, bcol=bcol, rdst=rdst, tt=tt):
                    bank = ctr["bank"] % 8
                    ctr["bank"] += 1
                    ps = pb[bank]
                    P.op("pe", lambda e: e.matmul(ps, lhsT=wt[:, c, :], rhs=xcb[:, tt * 512:(tt + 1) * 512],
                                                  start=True, stop=True), r=[wres, "l_xcb"], w=[("pb", bank)])
                    P.op("act", lambda e: e.activation(out=dst[:, tt * 512:(tt + 1) * 512], in_=ps, func=AF.Sigmoid,
                                                       bias=lv[:, bcol, c:c + 1]), r=[("pb", bank), "lv"], w=[rdst])
                L.append(mm)
        L.append(lambda: P.op("act", lambda e: e.activation(out=xa, in_=rr, func=AF.Exp, scale=cvec[:, c:c + 1]),
                              r=["l_rr", "cvec", "l_xc"], w=["l_xa"]))
        L.append(lambda: P.op("act", lambda e: e.activation(out=rr, in_=rr, func=AF.Exp, scale=c2vec[:, c:c + 1]),
                              r=["l_rr", "c2vec"], w=["l_rr"]))
        L.append(lambda: P.op("act", lambda e: e.activation(out=rr, in_=rr, func=AF.Sqrt, scale=-1.0, bias=onef[:, 0:1]),
                              r=["l_rr", "onef"], w=["l_rr"]))
        L.append(lambda: P.op("act", lambda e: e.activation(out=ya, in_=ya, func=AF.Gelu_apprx_tanh), r=["l_ya"], w=["l_ya"]))
        L.append(lambda: P.op("dve", lambda e: e.tensor_tensor(out=ii, in0=ii, in1=xc, op=ALU.mult),
                              r=["l_ii", "l_xc"], w=["l_ii"]))
        L.append(lambda: P.op("dve", lambda e: e.tensor_tensor(out=rr, in0=rr, in1=ii, op=ALU.mult),
                              r=["l_rr", "l_ii"], w=["l_rr"]))
        L.append(lambda: P.op("dve", lambda e: e.tensor_tensor_scan(out=xc, data0=xa, data1=rr, initial=0.0,
                                                                   op0=ALU.mult, op1=ALU.add),
                              r=["l_xa", "l_rr", "l_ii"], w=["l_xc"]))
        L.append(lambda: P.op("dve", lambda e: e.tensor_tensor(out=xcb, in0=xc, in1=ya, op=ALU.mult),
                              r=["l_xc", "l_ya"], w=["l_xcb"]))
        L.append(lambda: P.dma("sp", lru_s[c], xcb, r=["l_xcb"], w=[("lru_out",)], chan="l_xcb"))
        return L

    def phase_nsa():
        ar.reset()
        rope = ar.alloc([32, 2, S], F32)
        si2 = ar.alloc([8, 256], BF16)
        pat_f = ar.alloc([8, 1, 128], F32)
        pat4 = ar.alloc([8, 1, 4, 128], BF16)
        tri_f = ar.alloc([128, 2, 128], F32)
        tri4 = ar.alloc([128, 2, 4, 128], BF16)
        Eb = ar.alloc([32, S], BF16)
        ovl = ar.alloc([128, 32], F32)
        selc = ar.alloc([128, 2, NCH, 32], F32)
        w2b = ar.alloc([128, 2, 128], BF16)
        peb = ar.alloc([128, 2, 32], BF16)
        bias_kv = ar.alloc([128, 2], F32)
        P.dma("sp", rope, c_rope, w=["rope"], chan="rope")
        P.dma("pool", si2, c_si2, w=["si2"], chan="si2")
        P.dma("sp", pat_f, c_pat.rearrange("k (o q) -> k o q", o=1), w=["pat_f"], chan="pat_f")
        P.dma("sp", tri_f, c_tri, w=["tri_f"], chan="tri_f")
        P.dma("pool", Eb, c_E, w=["Eb"], chan="Eb")
        P.dma("sp", ovl, c_ovl, w=["ovl"], chan="ovl")
        P.dma("sp", selc, c_sel, w=["selc"], chan="selc")
        P.dma("pool", w2b, cmp_w2, w=["w2b"], chan="w2b")
        P.dma("pool", peb, cmp_pe, w=["peb"], chan="peb")
        w1k, w1k_res = wload(cmp_w1[:, 0], slot=0)
        w1v, w1v_res = wload(cmp_w1[:, 1], slot=1)
        w1 = (w1k, w1v)
        w1res = (w1k_res, w1v_res)

        def bcast4(e, dst, src, n_outer):
            s_ = src
            v = bass.AP(s_.tensor, s_.offset, [list(s_.ap[0]), list(s_.ap[1]), [0, 4], list(s_.ap[-1])])
            return e.tensor_copy(out=dst, in_=v)
        P.op("dve", lambda e: bcast4(e, pat4, pat_f, 1), r=["pat_f"], w=["pat4"])
        P.op("dve", lambda e: bcast4(e, tri4, tri_f, 2), r=["tri_f"], w=["tri4"])
        for kv in range(2):
            def fn(e, kv=kv):
                for l in range(32):
                    ins = e.matmul(pb[6][:, kv:kv + 1], lhsT=w1[kv][:, l, :], rhs=peb[:, kv, l:l + 1],
                                   start=(l == 0), stop=(l == 31))
                return ins
            P.op("pe", fn, r=[w1res[kv], "peb"], w=[("pb", 6)])
        P.op("dve", lambda e: e.tensor_copy(out=bias_kv, in_=pb[6][:, 0:2]), r=[("pb", 6)], w=["bias_kv"])

        tf2 = [ar.alloc([128, S], F32) for _ in range(2)]
        kb = [ar.alloc([128, S], BF16) for _ in range(4)]
        tp = ar.alloc([32, S], F32)
        qb = ar.alloc([128, 4, S], BF16)
        vtm = [ar.alloc([128, NCH, 128], BF16) for _ in range(2)]
        glq2 = [ar.alloc([128, 12, 256], F32) for _ in range(2)]
        nsa_g = ar.alloc([128, 4, S], BF16)
        NPT = 6
        ptr = [ar.alloc([128, 4, 128], BF16) for _ in range(NPT)]
        hcmp = ar.alloc([128, 2, 128], BF16)
        kcT = ar.alloc([128, 128], BF16)
        vc = ar.alloc([128, 128], BF16)
        pn = ar.alloc([128, 4, 128], F32)
        rd = ar.alloc([128, 4, 128], F32)
        wg_ = ar.alloc([128, 4, 128], F32)
        oacc2 = [ar.alloc([128, 4, 128], F32) for _ in range(2)]
        tmpb = ar.alloc([128, 4, 128], F32)
        val = ar.alloc([128, 32], F32)
        val2 = ar.alloc([128, 32], F32)
        m8 = ar.alloc([128, 16], F32)
        negsel = ar.alloc([128, 32], BF16)
        nselT4 = ar.alloc([32, 4, 128], BF16)
        print("nsa arena used", ar.off, "of", ar.nbytes)

        def v4(ap512):
            return ap512.rearrange("p (h q) -> p h q", h=4)

        def rope_inplace(tf, rtf, src):
            P.dma("sp", [tp[0:16], tp[16:32]], [src[16:32, :], src[0:16, :]], r=[("win_out",)], w=["tp"], chan="tp")

            P.op("pool", lambda e: e.tensor_tensor(out=tp, in0=tp, in1=rope[:, 1, :], op=ALU.mult),
                 r=["tp", "rope"], w=["tp"])
            P.op("pool", lambda e, tf=tf: e.tensor_tensor(out=tf[0:32, :], in0=tf[0:32, :], in1=rope[:, 0, :], op=ALU.mult),
                 r=[rtf, "rope"], w=[rtf])
            P.op("pool", lambda e, tf=tf: e.tensor_tensor(out=tf[0:32, :], in0=tf[0:32, :], in1=tp, op=ALU.add),
                 r=[rtf, "tp"], w=[rtf])

        brc = [0]
        tfc = [0]
        sctr = [0]
        pctr = [0]

        for g in range(4):
            srcs = [kv_s[0, g], kv_s[1, g], kv_s[2, g], kv_s[4, g]]
            for i4 in range(4):
                tfb = tf2[tfc[0] % 2]
                rtf = ("tf", tfc[0] % 2)
                tfc[0] += 1
                P.dma("sp", tfb, srcs[i4], r=[("win_out",)], w=[rtf], chan=rtf)
                if i4 != 1:
                    rope_inplace(tfb, rtf, srcs[i4])
                P.op("act", lambda e, i4=i4, tfb=tfb: e.activation(out=kb[i4], in_=tfb, func=AF.Copy),
                     r=[rtf], w=[("kb", i4)])
            for w_ in range(2):
                src = vt_s[w_].rearrange("(ti p) c -> p ti c", p=128)[:, :, g * 128:(g + 1) * 128]
                P.dma("sp", vtm[w_], src, r=[("vt_out",)], w=[("vtm", w_)], chan=("vtm", w_))
            for kv in range(2):
                src_b = kb[kv]
                hps = pb[6]

                def fn(e, kv=kv, src_b=src_b, hps=hps):
                    for l in range(32):
                        rhs = bass.AP(src_b.tensor, src_b.offset + l, [list(src_b.ap[0]), [16, 127]])
                        ins = e.matmul(hps[:, 0:127], lhsT=w1[kv][:, l, :], rhs=rhs,
                                       start=(l == 0), stop=(l == 31))
                    return ins
                P.op("pe", fn, r=[w1res[kv], ("kb", kv)], w=[("pb", 6)])
                P.op("act", lambda e, kv=kv, hps=hps: e.activation(out=hcmp[:, kv, 0:127], in_=hps[:, 0:127],
                                                                  func=AF.Gelu_apprx_tanh, bias=bias_kv[:, kv:kv + 1]),
                     r=[("pb", 6), "bias_kv"], w=[("hcmp", kv)])
                ops_ = pb[7]
                if kv == 0:
                    P.op("pe", lambda e, ops_=ops_: e.matmul(ops_[:, 0:127], lhsT=w2b[:, 0, :], rhs=hcmp[:, 0, 0:127],
                                                             start=True, stop=True),
                         r=["w2b", ("hcmp", 0)], w=[("pb", 7)])
                    P.op("dve", lambda e, ops_=ops_: e.tensor_copy(out=kcT[:, 0:127], in_=ops_[:, 0:127]),
                         r=[("pb", 7)], w=["kcT"])
                else:
                    P.op("pe", lambda e, ops_=ops_: e.matmul(ops_[0:127, 0:128], lhsT=hcmp[:, 1, 0:127], rhs=w2b[:, 1, :],
                                                             start=True, stop=True),
                         r=["w2b", ("hcmp", 1)], w=[("pb", 7)])
                    P.op("dve", lambda e, ops_=ops_: e.tensor_copy(out=vc[0:127, :], in_=ops_[0:127, 0:128]),
                         r=[("pb", 7)], w=["vc"])
            for h in range(4):
                qf = tf2[tfc[0] % 2]
                rq_ = ("tf", tfc[0] % 2)
                tfc[0] += 1
                P.dma("sp", qf, q_s[g * 4 + h], r=[("win_out",)], w=[rq_], chan=rq_)
                rope_inplace(qf, rq_, q_s[g * 4 + h])
                P.op("act", lambda e, qf=qf, h=h: e.activation(out=qb[:, h, :], in_=qf, func=AF.Copy),
                     r=[rq_], w=[("qb", h)])
            kslc_b, kwin_b = kb[2], kb[3]

            SB = [0, 1, 6, 7]
            DEPTH = 3
            fifo = []

            def push(item):
                fifo.append(item)
                while len(fifo) > DEPTH:
                    fifo.pop(0)()

            def drain_all():
                while fifo:
                    fifo.pop(0)()

            def attend(tiles, rq, post):
                pair = brc[0] % 2
                brc[0] += 1
                Db, Ob = 2 + 2 * pair, 3 + 2 * pair
                Dp, Op_ = pb[Db], pb[Ob]
                n = len(tiles)
                for ti in range(n):
                    nk, lk, masks, v_ap, rl = tiles[ti]
                    sb_ = SB[sctr[0] % 4]
                    sctr[0] += 1
                    Sp = pb[sb_]

                    def fn(e, Sp=Sp, nk=nk, lk=lk, masks=masks):
                        ins = e.matmul(v4(Sp)[0:nk], lhsT=lk, rhs=rq, start=True, stop=(len(masks) == 0))
                        for mi, (ml, mr, _) in enumerate(masks):
                            ins = e.matmul(v4(Sp)[0:nk], lhsT=ml, rhs=mr, start=False, stop=(mi == len(masks) - 1))
                        return ins
                    rl2 = list(rl)
                    for (_, _, mrl) in masks:
                        rl2 += mrl
                    P.op("pe", fn, r=rl2 + [("qb", hh_) for hh_ in range(4)] + ["ident_b"], w=[("pb", sb_)])
                    pk = pctr[0] % NPT
                    pctr[0] += 1
                    pt = ptr[pk]
                    P.op("act", lambda e, Sp=Sp, pt=pt, nk=nk: e.activation(out=pt[0:nk], in_=v4(Sp)[0:nk], func=AF.Exp,
                                                                            scale=SCALE),
                         r=[("pb", sb_)], w=[("pt", pk)])

                    def pv(ti=ti, pt=pt, pk=pk, nk=nk, v_ap=v_ap, rl=rl):
                        def fnp(e):
                            e.matmul(v4(Dp), lhsT=ones_b[0:nk, :], rhs=pt[0:nk], start=(ti == 0), stop=(ti == n - 1))
                            return e.matmul(v4(Op_), lhsT=v_ap, rhs=pt[0:nk], start=(ti == 0), stop=(ti == n - 1))
                        P.op("pe", fnp, r=[("pt", pk), "ones_b"] + list(rl), w=[("pb", Db), ("pb", Ob)])
                        if ti == n - 1:
                            post(Db, Ob, (pt, ("pt", pk)))
                    push(pv)

            def combine(Db, Ob, br, qt, first):
                glq = glq2[(qt // 2) % 2]
                rglq = ("glq", (qt // 2) % 2)
                oacc = oacc2[qt % 2]
                roa = ("oacc", qt % 2)
                gsl = glq[:, br:12:3, (qt % 2) * 128:(qt % 2) * 128 + 128]
                if not first:
                    P.op("dve", lambda e: e.reciprocal(out=rd, in_=v4(pb[Db])), r=[("pb", Db)], w=["rd"])
                    P.op("dve", lambda e: e.tensor_tensor(out=wg_, in0=rd, in1=gsl, op=ALU.mult), r=["rd", rglq], w=["wg"])
                else:
                    P.op("dve", lambda e: e.tensor_tensor(out=wg_, in0=rd, in1=gsl, op=ALU.mult), r=["rd", rglq], w=["wg"])
                if first:
                    P.op("dve", lambda e: e.tensor_tensor(out=oacc, in0=v4(pb[Ob]), in1=wg_, op=ALU.mult),
                         r=[("pb", Ob), "wg"], w=[roa])
                else:
                    P.op("dve", lambda e: e.tensor_tensor(out=tmpb, in0=v4(pb[Ob]), in1=wg_, op=ALU.mult),
                         r=[("pb", Ob), "wg"], w=["tmpb"])
                    P.op("pool", lambda e: e.tensor_tensor(out=oacc, in0=oacc, in1=tmpb, op=ALU.add),
                         r=[roa, "tmpb"], w=[roa])

            def emit_imp(qt, ncnt):
                def fni(e):
                    for h in range(4):
                        ins = e.matmul(pb[6][:, 0:32], lhsT=pn[0:ncnt, h, :], rhs=ovl[0:ncnt, :],
                                       start=(h == 0), stop=(h == 3))
                    return ins
                P.op("pe", fni, r=["pn", "ovl"], w=[("pb", 6)])
                P.seq("dve", [
                    (lambda e: e.tensor_tensor(out=val, in0=pb[6][:, 0:32], in1=selc[:, 0, qt, :], op=ALU.mult),
                     [("pb", 6), "selc"], ["val"]),
                    (lambda e: e.tensor_tensor(out=val, in0=val, in1=selc[:, 1, qt, :], op=ALU.add),
                     ["val", "selc"], ["val"]),
                    (lambda e: e.max(out=m8[:, 0:8], in_=val), ["val"], ["m8a"]),
                    (lambda e: e.tensor_scalar(out=val2, in0=val, scalar1=m8[:, 7:8], scalar2=-1e9,
                                               op0=ALU.is_ge, op1=ALU.mult), ["val", "m8a"], ["val2"]),
                    (lambda e: e.tensor_tensor(out=val2, in0=val2, in1=val, op=ALU.add), ["val2", "val"], ["val2"]),
                    (lambda e: e.max(out=m8[:, 8:16], in_=val2), ["val2"], ["m8b"]),
                    (lambda e: e.tensor_scalar(out=negsel, in0=val, scalar1=m8[:, 15:16], scalar2=NEGB,
                                               op0=ALU.is_lt, op1=ALU.mult), ["val", "m8b"], ["negsel"]),
                ])

            def post_cmp(qt, ncnt):
                def post(Db, Ob, lastpt):
                    pt_c, rpt = lastpt
                    P.op("dve", lambda e: e.tensor_scalar(out=rd, in0=v4(pb[Db]), scalar1=1e-30, scalar2=None, op0=ALU.max),
                         r=[("pb", Db)], w=["rd"])
                    P.op("dve", lambda e: e.reciprocal(out=rd, in_=rd), r=["rd"], w=["rd"])
                    if qt >= 8:
                        P.op("dve", lambda e: e.tensor_tensor(out=pn[0:ncnt], in0=pt_c[0:ncnt], in1=rd[0:ncnt], op=ALU.mult),
                             r=[rpt, "rd"], w=["pn"])
                    combine(Db, Ob, 0, qt, True)
                    if qt >= 8:
                        push(lambda: emit_imp(qt, ncnt))
                return post

            def post_mid(br, qt):
                def post(Db, Ob, lastpt):
                    combine(Db, Ob, br, qt, False)
                return post

            def post_last(br, qt):
                def post(Db, Ob, lastpt):
                    combine(Db, Ob, br, qt, False)
                    P.op("act", lambda e: e.activation(out=nsa_g[:, :, qt * 128:(qt + 1) * 128], in_=oacc2[qt % 2], func=AF.Copy),
                         r=[("oacc", qt % 2)], w=["nsa_g"])
                return post

            for qt in range(NCH):
                if qt % 2 == 0:
                    qq = qt // 2
                    glq = glq2[qq % 2]
                    rglq = ("glq", qq % 2)
                    src = bass.AP(ng_s.ap().tensor, ng_s.ap().offset + g * 12 * S + qq * 256, [[0, 128], [S, 12], [1, 256]])
                    P.dma("sp", glq, src, r=[("win_out",)], w=[rglq], chan=rglq)
                    P.op("act", lambda e, glq=glq: e.activation(out=glq, in_=glq, func=AF.Sigmoid), r=[rglq], w=[rglq])
                rq = qb[:, :, qt * 128:(qt + 1) * 128]
                ncnt = min(127, 8 * qt + 7)
                c_lo = 129 - 8 * qt
                tiles = [(ncnt, kcT[:, 0:ncnt], [(si2[0:8, c_lo:c_lo + ncnt], pat4[:, 0], ["si2", "pat4"])],
                          vc[0:ncnt, :], ["kcT", "vc"])]
                attend(tiles, rq, post_cmp(qt, ncnt))
                tiles = []
                for kt in range(max(0, qt - 4), qt + 1):
                    masks = []
                    if kt == qt:
                        masks.append((ident_b, tri4[:, 0], ["tri4"]))
                    if kt == qt - 4:
                        masks.append((ident_b, tri4[:, 1], ["tri4"]))
                    tiles.append((128, kwin_b[:, kt * 128:(kt + 1) * 128], masks, vtm[1][:, kt, :], [("kb", 3), ("vtm", 1)]))
                attend(tiles, rq, post_mid(2, qt))
                if qt >= 8:
                    drain_all()
                    tps = pb[7][:].bitcast(BF16)
                    P.op("pe", lambda e, tps=tps: e.transpose(tps[0:32, 0:128], negsel, ident_b),
                         r=["negsel", "ident_b"], w=[("pb", 7)])
                    P.op("dve", lambda e, tps=tps: bcast4(e, nselT4.rearrange("p (o h) q -> p o h q", o=1),
                                                          tps[0:32, 0:128].rearrange("p (o q) -> p o q", o=1), 1),
                         r=[("pb", 7)], w=["nselT4"])
                tiles = []
                for kt in range(qt + 1):
                    masks = []
                    if qt >= 8:
                        masks.append((Eb[0:32, kt * 128:(kt + 1) * 128], nselT4, ["Eb", "nselT4"]))
                    if kt == qt:
                        masks.append((ident_b, tri4[:, 0], ["tri4"]))
                    tiles.append((128, kslc_b[:, kt * 128:(kt + 1) * 128], masks, vtm[0][:, kt, :], [("kb", 2), ("vtm", 0)]))
                attend(tiles, rq, post_last(1, qt))
            drain_all()
            for h in range(4):
                P.dma("sp", nsa_s[g * 4 + h], nsa_g[:, h, :], r=["nsa_g"], w=[("nsa_out",)], chan="nsa_g")
        P.barrier()

    def phase_merge():
        for half in range(2):
            t0 = half * 1024
            ar.reset()
            lru_h = ar.alloc([128, NCH, 1024], BF16)
            nsa_h = ar.alloc([128, NCH, 1024], BF16)
            gab = [[ar.alloc([128, 1024], F32) for _ in range(2)] for _ in range(2)]
            mst = [ar.alloc([128, 1024], BF16) for _ in range(2)]
            t1 = ar.alloc([128, 512], F32)
            t2 = ar.alloc([128, 512], F32)
            P.dma("sp", lru_h, lru_s.rearrange("c p s -> p c s")[:, :, t0:t0 + 1024], r=[("lru_out",)], w=["lru_h"],
                  chan="lru_h")
            P.dma("sp", nsa_h, nsa_s.rearrange("c p s -> p c s")[:, :, t0:t0 + 1024], r=[("nsa_out",)], w=["nsa_h"],
                  chan="nsa_h")
            bctr = 0
            for j in range(8):
                wa_t, wa_res = wload(wk(w_a, j * 256, 256))
                wb_t, wb_res = wload(wk(w_b, j * 256, 256))
                for cc in range(2):
                    dc = j * 2 + cc
                    k = dc % 2
                    P.dma("sp", gab[0][k], mg_s[dc][:, t0:t0 + 1024], r=[("win_out",)], w=[("ga", k)], chan=("ga", k))
                    P.dma("sp", gab[1][k], mg_s[16 + dc][:, t0:t0 + 1024], r=[("win_out",)], w=[("gb", k)], chan=("gb", k))
                    for tt in range(2):
                        ba, bb = (bctr % 4) * 2, (bctr % 4) * 2 + 1
                        bctr += 1
                        for (bank, wt, wres, act_, ares) in ((ba, wa_t, wa_res, lru_h, "lru_h"), (bb, wb_t, wb_res, nsa_h, "nsa_h")):
                            def fn(e, bank=bank, wt=wt, act_=act_, cc=cc, tt=tt):
                                for kc in range(NCH):
                                    ins = e.matmul(pb[bank], lhsT=wt[:, kc, cc * 128:(cc + 1) * 128],
                                                   rhs=act_[:, kc, tt * 512:(tt + 1) * 512], start=(kc == 0), stop=(kc == NCH - 1))
                                return ins
                            P.op("pe", fn, r=[wres, ares], w=[("pb", bank)])

                        P.op("dve", lambda e, ba=ba, k=k, tt=tt: e.tensor_tensor(
                            out=t1, in0=pb[ba], in1=gab[0][k][:, tt * 512:(tt + 1) * 512], op=ALU.mult),
                            r=[("pb", ba), ("ga", k)], w=["t1"])
                        P.op("dve", lambda e, bb=bb, k=k, tt=tt: e.tensor_tensor(
                            out=t2, in0=pb[bb], in1=gab[1][k][:, tt * 512:(tt + 1) * 512], op=ALU.mult),
                            r=[("pb", bb), ("gb", k)], w=["t2"])
                        P.op("dve", lambda e, k=k, tt=tt: e.tensor_tensor(
                            out=mst[k][:, tt * 512:(tt + 1) * 512], in0=t1, in1=t2, op=ALU.add),
                            r=["t1", "t2"], w=[("mst", k)])
                    P.dma("sp", m_s[dc][:, t0:t0 + 1024], mst[k], r=[("mst", k)], w=[("m_out", half)], chan=("mst", k))
            P.barrier()
            ar.reset()
            u = ar.alloc([128, NCH, 1024], BF16)
            acc = ar.alloc([128, NCH, 1024], F32)
            xt2 = [ar.alloc([128, D], F32) for _ in range(2)]
            yt = ar.alloc([128, D], F32)
            vecG = ar.alloc([128, D], F32)
            P.dma("sp", u, m_s.rearrange("c p s -> p c s")[:, :, t0:t0 + 1024], r=[("m_out", half)],
                  w=[("u", 0), ("u", 1)], chan="u_m")
            ev = 0
            for j in range(8):
                wo_t, wo_res = wload(wk(w_o, j * 256, 256))
                for cc in range(2):
                    dc = j * 2 + cc
                    for tt in range(2):
                        bank = bctr % 8
                        bctr += 1

                        def fn(e, bank=bank, wo_t=wo_t, cc=cc, tt=tt):
                            for kc in range(NCH):
                                ins = e.matmul(pb[bank], lhsT=wo_t[:, kc, cc * 128:(cc + 1) * 128],
                                               rhs=u[:, kc, tt * 512:(tt + 1) * 512], start=(kc == 0), stop=(kc == NCH - 1))
                            return ins
                        P.op("pe", fn, r=[wo_res, ("u", tt)], w=[("pb", bank)])
                        a_ = acc[:, dc, tt * 512:(tt + 1) * 512]
                        ev += 1
                        if ev % 2:
                            P.op("act", lambda e, a_=a_, bank=bank: e.activation(out=a_, in_=pb[bank], func=AF.Copy),
                                 r=[("pb", bank)], w=[("acc", dc, tt)])
                        else:
                            P.op("dve", lambda e, a_=a_, bank=bank: e.tensor_copy(out=a_, in_=pb[bank]),
                                 r=[("pb", bank)], w=[("acc", dc, tt)])
            epilogue(acc, x1, "x1", 1, x2, "x2", t0, 1024, (xt2, yt, vecG, "vecG"))
            P.barrier()

    if "ada" in phases:
        phase_ada()
    if "ffn1" in phases:
        ffn_phase(x_in, "x", 0, x1, "x1", w1g, w1u, w1d)
    if "win" in phases:
        phase_win()
    if "nsa" in phases:
        phase_nsa()
    if "merge" in phases:
        phase_merge()
    if "ffn2" in phases:
        ffn_phase(x2, "x2", 2, out, "out", w2g, w2u, w2d)
    P.barrier()
    P.emit(es)
    es.close()
    P.dr = dr
    return nc, P


def _consts():
    c = {}
    pos = np.arange(S, dtype=np.float32)
    inv_freq = (np.float32(500000.0) ** (-np.arange(0, 32, 2, dtype=np.float32) / np.float32(32))).astype(np.float32)
    ang = (pos[:, None] * inv_freq[None, :]).astype(np.float32)
    cos, sin = np.cos(ang).astype(np.float32), np.sin(ang).astype(np.float32)
    rope = np.zeros((32, 2, S), np.float32)
    rope[0:16, 0] = cos.T
    rope[16:32, 0] = cos.T
    rope[0:16, 1] = -sin.T
    rope[16:32, 1] = sin.T
    c["c_rope"] = rope
    c["c_ident"] = np.eye(128, dtype=np.float32)
    k8 = np.arange(8)[:, None]
    c["c_si2"] = (np.arange(256)[None, :] == k8 + 128).astype(np.float32)
    c["c_pat"] = np.where(16 * k8 + 15 <= np.arange(128)[None, :], 0.0, NEGB).astype(np.float32)
    k = np.arange(128)[:, None]
    ql = np.arange(128)[None, :]
    tri = np.zeros((128, 2, 128), np.float32)
    tri[:, 0, :] = np.where(k <= ql, 0.0, NEGB)
    tri[:, 1, :] = np.where(k > ql, 0.0, NEGB)
    c["c_tri"] = tri
    j = np.arange(32)[:, None]
    kk = np.arange(S)[None, :]
    c["c_E"] = (kk // 64 == j).astype(np.float32)
    nn = np.arange(128)[:, None]
    jj = np.arange(32)[None, :]
    ovl = ((16 * nn < 64 * jj + 64) & (16 * nn + 32 > 64 * jj) & (nn < 127)).astype(np.float32)
    c["c_ovl"] = ovl
    tok = np.arange(S)
    cur = tok // 64
    blk = np.arange(32)
    forced = (blk[None, :] == 0) | (blk[None, :] == cur[:, None]) | (blk[None, :] == cur[:, None] - 1)
    causal = (blk[None, :] * 64) <= tok[:, None]
    c01 = (causal & ~forced).astype(np.float32)
    adj = np.where(forced, 10.0 + blk[None, :], np.where(causal, 0.0, -1.0 - 0.01 * blk[None, :])).astype(np.float32)
    sel = np.zeros((128, 2, NCH, 32), np.float32)
    sel[:, 0] = c01.reshape(NCH, 128, 32).transpose(1, 0, 2)
    sel[:, 1] = adj.reshape(NCH, 128, 32).transpose(1, 0, 2)
    c["c_sel"] = sel
    return c


def _fm(v):
    return np.ascontiguousarray(np.asarray(v, np.float32).reshape(NCH, 128).T)


def prep_shared(inp):
    sh = {}
    sh["w_ada"] = np.ascontiguousarray(inp["w_ada"][0])
    sh["b_adaT"] = np.ascontiguousarray(inp["b_ada"][0].reshape(144, 128).T)
    sh["gainsT"] = np.ascontiguousarray(np.stack([_fm(inp[k][0]) for k in (
        "ffn1_pre_g", "ffn1_post_g", "mix_pre_g", "mix_post_g", "ffn2_pre_g", "ffn2_post_g")], axis=1))
    for k in ("ffn1_w_gate", "ffn1_w_up", "ffn1_w_down", "ffn2_w_gate", "ffn2_w_up", "ffn2_w_down",
              "w_in", "w_a_out", "w_b_out", "w_out"):
        sh[k] = np.ascontiguousarray(inp[k][0])
    cw = inp["conv_w"][0]
    sh["lruvT"] = np.ascontiguousarray(np.stack(
        [_fm(cw[0]), _fm(cw[1]), _fm(cw[2]), _fm(cw[3]), _fm(inp["conv_b"][0]), _fm(inp["lru_br"][0]),
         _fm(inp["lru_bi"][0]), _fm(inp["lru_lambda"][0])], axis=1))
    sh["lru_w"] = np.ascontiguousarray(np.stack([inp["lru_wr"][0], inp["lru_wi"][0]], axis=0).transpose(2, 0, 1, 3))
    sh["cmp_pe"] = np.ascontiguousarray(np.stack([inp["cmp_pe_k"][0].T, inp["cmp_pe_v"][0].T], axis=1))
    sh["cmp_w1"] = np.ascontiguousarray(np.stack(
        [inp["cmp_w1_k"][0].reshape(32, 128, 128).transpose(1, 0, 2),
         inp["cmp_w1_v"][0].reshape(32, 128, 128).transpose(1, 0, 2)], axis=1))
    sh["cmp_w2"] = np.ascontiguousarray(np.stack([inp["cmp_w2_k"][0], inp["cmp_w2_v"][0]], axis=1))
    sh.update(_consts())
    return sh


ALL_PHASES = ["ada", "ffn1", "win", "nsa", "merge", "ffn2"]


def kernel(**inp):
    nc, P = build({"phases": ALL_PHASES, "roles": {"x1": "out", "x2": "out"}})
    sh = prep_shared(inp)
    sh = {k: v for k, v in sh.items() if k in P.dr}
    in_maps = []
    for b in range(8):
        m = dict(sh)
        m["x"] = np.ascontiguousarray(inp["x"][b], dtype=np.float32)
        m["cT"] = _fm(inp["c"][b])
        in_maps.append(m)
    res = run_bass_kernel_spmd(nc, in_maps, core_ids=list(range(8)))
    return np.stack([np.asarray(r["out"], np.float32) for r in res.results], axis=0)
```
